# Optimizing a Trainium2 kernel written in Bass

```python
import math
import jax, jax.numpy as jnp
from jax import lax
import numpy as np

D_MODEL = 2048
BATCH = 2
SEQ = 8192
DEPTH = 1

HEAD_DIM = 128
N_HEADS_TOTAL = D_MODEL // HEAD_DIM
N_HEADS_A = N_HEADS_TOTAL // 2
N_HEADS_B = N_HEADS_TOTAL - N_HEADS_A
DIFF_QK_DIM = HEAD_DIM // 2
WIDTH_A = N_HEADS_A * HEAD_DIM
WIDTH_B = N_HEADS_B * HEAD_DIM
MIX_WIDTH = WIDTH_A + WIDTH_B
IN_PROJ_WIDTH = 3 * WIDTH_A + 3 * WIDTH_B
DILATED_PATTERNS = ((128, 1), (512, 4), (2048, 16))
D_FF = -(-8 * D_MODEL // (3 * 256)) * 256
ROPE_THETA = 10000.0
RMS_EPS = 1e-6
Q_BLOCK = 128

kernel_name = "hymba_dilated_diffattn_swiglu_block"


def rmsnorm(x, g):
    xf = x.astype(jnp.float32)
    y = xf * lax.rsqrt(jnp.mean(xf * xf, axis=-1, keepdims=True) + RMS_EPS)
    return (y * g.astype(jnp.float32)).astype(x.dtype)


def rope(x, positions):
    e = x.shape[-1]
    half = e // 2
    inv = ROPE_THETA ** (-jnp.arange(half, dtype=jnp.float32) / half)
    ang = positions.astype(jnp.float32)[..., None] * inv
    ang = ang.reshape(ang.shape[:2] + (1,) * (x.ndim - 3) + (half,))
    cos, sin = jnp.cos(ang), jnp.sin(ang)
    xf = x.astype(jnp.float32)
    x1, x2 = xf[..., :half], xf[..., half:]
    return jnp.concatenate([x1 * cos - x2 * sin, x2 * cos + x1 * sin], axis=-1).astype(x.dtype)


def dilated_window_attention(q, k, v, window, dilation):
    B, S, H, E = q.shape
    n = window // dilation
    L = S // dilation
    nblk = -(-L // n)
    Lp = nblk * n

    def to_blocks(a):
        a = a.reshape(B, L, dilation, H, E)
        a = jnp.pad(a, ((0, 0), (0, Lp - L), (0, 0), (0, 0), (0, 0)))
        return a.reshape(B, nblk, n, dilation, H, E)

    def with_prev(a):
        prev = jnp.pad(a, ((0, 0), (1, 0), (0, 0), (0, 0), (0, 0), (0, 0)))[:, :-1]
        return jnp.concatenate([prev, a], axis=2)

    qb = to_blocks(q)
    kc = with_prev(to_blocks(k))
    vc = with_prev(to_blocks(v))

    s = jnp.einsum('bnqrhe,bnkrhe->bnrhqk', qb, kc).astype(jnp.float32)
    qi = jnp.arange(n)[:, None]
    kj = jnp.arange(2 * n)[None, :]
    dist = qi + n - kj
    band = (dist >= 0) & (dist <= n)
    blk = jnp.arange(nblk)[:, None, None]
    mask = band[None] & ((blk > 0) | (kj[None] >= n))
    s = jnp.where(mask[None, :, None, None], s, -jnp.inf)
    m = jnp.max(s, axis=-1, keepdims=True)
    p = jnp.exp(s - m)
    den = jnp.sum(p, axis=-1)
    o = jnp.einsum('bnrhqk,bnkrhe->bnqrhe', p, vc.astype(jnp.float32))
    o = o / den.transpose(0, 1, 4, 2, 3)[..., None]
    lse = (m[..., 0] + jnp.log(den)).transpose(0, 1, 4, 2, 3)
    o = o.reshape(B, Lp, dilation, H, E)[:, :L].reshape(B, S, H, E)
    lse = lse.reshape(B, Lp, dilation, H)[:, :L].reshape(B, S, H)
    return o, lse


def dilated_mixture_attention(q, k, v):
    outs, lses = [], []
    for window, dilation in DILATED_PATTERNS:
        o, lse = dilated_window_attention(q, k, v, window, dilation)
        outs.append(o)
        lses.append(lse)
    w = jax.nn.softmax(jnp.stack(lses, axis=0), axis=0)
    return jnp.sum(w[..., None] * jnp.stack(outs, axis=0), axis=0)


def differential_attention(q, k, v, lam):
    B, S, _, H, e = q.shape
    n_q = S // Q_BLOCK
    q_blocks = q.reshape(B, n_q, Q_BLOCK, 2, H, e).transpose(1, 0, 2, 3, 4, 5)
    kpos = jnp.arange(S)
    vf = v.astype(jnp.float32)

    def one_block(args):
        bi, qblk = args
        s = jnp.einsum('bqmhe,bkmhe->bmhqk', qblk, k).astype(jnp.float32)
        qpos = bi * Q_BLOCK + jnp.arange(Q_BLOCK)
        causal = kpos[None, :] <= qpos[:, None]
        s = jnp.where(causal, s, -jnp.inf)
        a = jax.nn.softmax(s, axis=-1)
        attn = a[:, 0] - lam * a[:, 1]
        return jnp.einsum('bhqk,bkhe->bqhe', attn, vf)

    out = lax.map(one_block, (jnp.arange(n_q), q_blocks))
    return out.transpose(1, 0, 2, 3, 4).reshape(B, S, H, v.shape[-1])


def setup_inputs(seed: int = 0) -> dict:
    key = jax.random.key(seed)
    ks = jax.random.split(key, 20)
    f32 = jnp.float32

    def nrm(k, shape, scale):
        return jax.random.normal(k, shape, f32) * scale

    def gain(k):
        return 1.0 + nrm(k, (DEPTH, D_MODEL), 0.05)

    x = jax.random.normal(ks[0], (BATCH, SEQ, D_MODEL), f32)
    c = jax.random.normal(ks[1], (BATCH, D_MODEL), f32)
    offset = jax.random.randint(ks[2], (BATCH, 1), 0, 4096, dtype=jnp.int32)
    positions = offset + jnp.arange(SEQ, dtype=jnp.int32)[None, :]
    return {
        "x": x,
        "c": c,
        "positions": positions,
        "w_ada": nrm(ks[3], (DEPTH, D_MODEL, 6 * D_MODEL), 0.5 * D_MODEL ** -0.5),
        "b_ada": nrm(ks[4], (DEPTH, 6 * D_MODEL), 0.02),
        "g_pre_attn": gain(ks[5]),
        "w_in": nrm(ks[6], (DEPTH, D_MODEL, IN_PROJ_WIDTH), D_MODEL ** -0.5),
        "g_out_a": 1.0 + nrm(ks[7], (DEPTH, HEAD_DIM), 0.05),
        "lambda_q1": nrm(ks[8], (DEPTH, DIFF_QK_DIM), 0.1),
        "lambda_k1": nrm(ks[9], (DEPTH, DIFF_QK_DIM), 0.1),
        "lambda_q2": nrm(ks[10], (DEPTH, DIFF_QK_DIM), 0.1),
        "lambda_k2": nrm(ks[11], (DEPTH, DIFF_QK_DIM), 0.1),
        "g_subln_b": 1.0 + nrm(ks[12], (DEPTH, HEAD_DIM), 0.05),
        "w_out": nrm(ks[13], (DEPTH, MIX_WIDTH, D_MODEL), MIX_WIDTH ** -0.5),
        "g_post_attn": gain(ks[14]),
        "g_pre_ffn": gain(ks[15]),
        "w_gate": nrm(ks[16], (DEPTH, D_MODEL, D_FF), D_MODEL ** -0.5),
        "w_up": nrm(ks[17], (DEPTH, D_MODEL, D_FF), D_MODEL ** -0.5),
        "w_down": nrm(ks[18], (DEPTH, D_FF, D_MODEL), D_FF ** -0.5),
        "g_post_ffn": gain(ks[19]),
    }


def reference(x, c, positions, w_ada, b_ada, g_pre_attn, w_in, g_out_a,
              lambda_q1, lambda_k1, lambda_q2, lambda_k2, g_subln_b, w_out,
              g_post_attn, g_pre_ffn, w_gate, w_up, w_down, g_post_ffn):
    B, S, _ = x.shape
    scale_a = HEAD_DIM ** -0.5
    scale_b = DIFF_QK_DIM ** -0.5
    for l in range(DEPTH):
        lambda_init = 0.8 - 0.6 * math.exp(-0.3 * l)
        mod = jax.nn.silu(c) @ w_ada[l] + b_ada[l]
        sh_a, sc_a, gt_a, sh_f, sc_f, gt_f = [m[:, None, :] for m in jnp.split(mod, 6, axis=-1)]

        h = rmsnorm(x, g_pre_attn[l]) * (1.0 + sc_a) + sh_a
        proj = h @ w_in[l]
        qa, ka, va, qb, kb, vb = jnp.split(proj, 6, axis=-1)

        qa = rope(qa.reshape(B, S, N_HEADS_A, HEAD_DIM), positions) * scale_a
        ka = rope(ka.reshape(B, S, N_HEADS_A, HEAD_DIM), positions)
        va = va.reshape(B, S, N_HEADS_A, HEAD_DIM)
        oa = dilated_mixture_attention(qa, ka, va).astype(x.dtype)
        oa = rmsnorm(oa, g_out_a[l])

        qb = rope(qb.reshape(B, S, 2, N_HEADS_B, DIFF_QK_DIM), positions) * scale_b
        kb = rope(kb.reshape(B, S, 2, N_HEADS_B, DIFF_QK_DIM), positions)
        vb = vb.reshape(B, S, N_HEADS_B, HEAD_DIM)
        lam = (jnp.exp(jnp.sum(lambda_q1[l].astype(jnp.float32) * lambda_k1[l].astype(jnp.float32)))
               - jnp.exp(jnp.sum(lambda_q2[l].astype(jnp.float32) * lambda_k2[l].astype(jnp.float32)))
               + lambda_init)
        ob = differential_attention(qb, kb, vb, lam).astype(x.dtype)
        ob = rmsnorm(ob, g_subln_b[l]) * (1.0 - lambda_init)

        mix = jnp.concatenate([oa.reshape(B, S, WIDTH_A), ob.reshape(B, S, WIDTH_B)], axis=-1)
        x = x + gt_a * rmsnorm(mix @ w_out[l], g_post_attn[l])

        h = rmsnorm(x, g_pre_ffn[l]) * (1.0 + sc_f) + sh_f
        f = (jax.nn.silu(h @ w_gate[l]) * (h @ w_up[l])) @ w_down[l]
        x = x + gt_f * rmsnorm(f, g_post_ffn[l])
    return x
```

```python
import math
import os
from contextlib import ExitStack

import numpy as np
import ml_dtypes

import concourse.bass as bass
import concourse.mybir as mybir
from concourse.bass_utils import run_bass_kernel_spmd

F32 = mybir.dt.float32
BF16 = mybir.dt.bfloat16
I32 = mybir.dt.int32
AF = mybir.ActivationFunctionType
ALU = mybir.AluOpType
AX = mybir.AxisListType

SEM_LIMIT = 32000
SAME_ENGINE_SYNC = True


class Buf:
    __slots__ = ("name", "writers", "readers")

    def __init__(self, name=""):
        self.name = name
        self.writers = {}
        self.readers = {}


class SemCtr:
    def __init__(self, S):
        self.S = S
        self.vid = S.new_vsem()
        self.count = 0
        self.hist = {}

    def bump(self, inc):
        if self.count + inc > SEM_LIMIT:
            self.hist[self.vid] = self.count
            self.vid = self.S.new_vsem()
            self.count = 0
        self.count += inc
        return self.vid, self.count

    def current_for(self, vid):
        return self.count if vid == self.vid else self.hist[vid]


class Ev:
    __slots__ = ("eng", "fn", "deps", "sem", "val", "flag", "is_dma", "ctr", "phase", "noinst")


ENGS = ("pe", "act", "dve", "pool", "sp")


class Sched:
    def __init__(self, nc, outer):
        self.nc = nc
        self.outer = outer
        self.prog = {e: [] for e in ENGS}
        self.nvsem = 0
        self.phase = 0
        self.sems = []
        self.eng_ctr = {e: SemCtr(self) for e in ENGS}
        self.waited = {e: {} for e in ENGS}
        self.barrier = []
        self.phase_dmas = {}
        self.n_instr = 0

    def new_vsem(self):
        self.nvsem += 1
        return self.nvsem - 1

    def dma_ctr(self):
        return SemCtr(self)

    def op(self, eng, fn, reads=(), writes=(), dma_ctr=None, noinst=False, carry=False):
        ev = Ev()
        ev.noinst = noinst
        ev.eng = eng
        ev.fn = fn
        ev.is_dma = dma_ctr is not None
        ev.ctr = dma_ctr
        ev.flag = ev.is_dma
        ev.sem = None
        ev.val = None
        ev.phase = self.phase
        deps = {}
        for b in reads:
            for w in b.writers.values():
                deps[id(w)] = w
            if b.name.startswith("ps"):
                for k_, r in b.readers.items():
                    if k_ != eng:
                        deps[id(r)] = r
        for b in writes:
            for w in b.writers.values():
                deps[id(w)] = w
            for r in b.readers.values():
                deps[id(r)] = r
        dl = []
        for d in deps.values():
            if d is ev:
                continue
            if (not d.is_dma) and d.phase < self.phase:
                continue
            if (not d.is_dma) and d.eng == eng:
                if eng == "pe" or not SAME_ENGINE_SYNC:
                    continue
            if d.is_dma:
                dl.append((d, d.ctr.current_for(d.sem)))
            else:
                d.flag = True
                dl.append((d, None))
        ev.deps = dl
        if ev.is_dma:
            ev.sem, ev.val = dma_ctr.bump(16)
            if not carry:
                self.phase_dmas[ev.sem] = ev.val
        key = ("d", id(dma_ctr)) if ev.is_dma else eng
        for b in reads:
            b.readers[key] = ev
        for b in writes:
            b.writers[key] = ev
        self.prog[eng].append(ev)
        return ev

    def flush(self):
        nc = self.nc
        prog = self.prog
        new_barrier = []
        for e in ENGS:
            last = None
            for ev in prog[e]:
                if not ev.is_dma and not ev.noinst:
                    last = ev
            if last is not None:
                last.flag = True
        for e in ENGS:
            ctr = self.eng_ctr[e]
            lastev = None
            for ev in prog[e]:
                if ev.is_dma:
                    continue
                if ev.flag and not ev.noinst:
                    ev.sem, ev.val = ctr.bump(1)
                    lastev = ev
            if lastev is not None:
                new_barrier.append((lastev.sem, lastev.val))
        while len(self.sems) < self.nvsem:
            self.sems.append(self.outer.enter_context(nc.semaphore(f"s{len(self.sems)}")))
        sems = self.sems
        old_barrier = self.barrier

        def run(engname):
            def body(eng):
                waited = self.waited[engname]
                for vid, val in old_barrier:
                    if waited.get(vid, 0) < val:
                        eng.wait_ge(sems[vid], val)
                        waited[vid] = val
                for ev in prog[engname]:
                    for d, snap in ev.deps:
                        vid = d.sem
                        val = snap if d.is_dma else d.val
                        if waited.get(vid, 0) < val:
                            eng.wait_ge(sems[vid], val)
                            waited[vid] = val
                    ins = ev.fn(eng)
                    self.n_instr += 1
                    if ev.flag and not ev.noinst:
                        ins.then_inc(sems[ev.sem], 16 if ev.is_dma else 1)
            return body

        with nc.Block() as block:
            block.sync(run("sp"))
            block.scalar(run("act"))
            block.vector(run("dve"))
            block.gpsimd(run("pool"))
            block.tensor(run("pe"))
        new_barrier.extend(self.phase_dmas.items())
        self.phase_dmas = {}
        self.barrier = new_barrier
        self.prog = {e: [] for e in ENGS}
        self.phase += 1


def sched_finish(S):
    nc = S.nc
    sems = S.sems
    items = list(S.barrier)

    def body(eng):
        waited = S.waited["sp"]
        for vid, val in items:
            if waited.get(vid, 0) < val:
                eng.wait_ge(sems[vid], val)
                waited[vid] = val

    with nc.Block() as block:
        block.sync(body)


class Ring:
    def __init__(self, items):
        self.items = items
        self.i = 0

    def next(self):
        it = self.items[self.i % len(self.items)]
        self.i += 1
        return it


D = 2048
KC = 16
NT = 2048
NSLOT = 4
DFF = 5632
NFC = DFF // 128
HD = 128
SCALE_A = HD ** -0.5
SCALE_B = 64 ** -0.5
EPS = 1e-6
LAMBDA_INIT = 0.8 - 0.6 * math.exp(-0.3 * 0)
NEG = -30000.0
INV2PI = float(np.float32(1.0 / (2 * np.pi)))
MAGIC = 12582912.0
C1 = 6.28125
C2 = float(np.float32(2 * np.pi - 6.28125))
HALFPI = float(np.pi / 2)
PI_SAFE = float(np.nextafter(np.float32(np.pi), np.float32(0)))


class StopBuild(Exception):
    pass


def build(debug=False, upto=9):
    nc = bass.Bass("TRN2", target_bir_lowering=False)
    dk = "ExternalOutput" if debug else "Internal"

    def din(name, shape, dt):
        return nc.dram_tensor(name, shape, dt, kind="ExternalInput").ap()

    def dscr(name, shape, dt):
        return nc.dram_tensor(name, shape, dt, kind=dk).ap()

    xs = din("xs", [NSLOT, NT, D], F32)
    posi_d = din("posi", [NSLOT, 1, NT], I32)
    ebias_d = din("ebias", [128, 4], F32)
    cT_d = din("cT", [128, KC], F32)
    w_ada = din("w_ada", [D, 6 * D], F32)
    b_ada = din("b_ada", [1, 6 * D], F32)
    gpa_d = din("gpa", [128, KC], F32)
    gpf_d = din("gpf", [128, KC], F32)
    gposta_d = din("gposta", [1, D], F32)
    gpostf_d = din("gpostf", [1, D], F32)
    w_in = din("w_in", [D, 6144], F32)
    gcol_d = din("gcol", [128, 2], F32)
    lamv_d = din("lamv", [1, 256], F32)
    w_out = din("w_out", [D, D], F32)
    w_gate = din("w_gate", [D, DFF], F32)
    w_up = din("w_up", [D, DFF], F32)
    w_down = din("w_down", [DFF, D], F32)
    rc_d = din("rc", [128, 4], F32)
    cb16_d = din("cb16", [128, 3 * 128 + 4 * 512 + 2 * 128], BF16)
    y = nc.dram_tensor("y", [NT, D], F32, kind="ExternalOutput").ap()

    qaT = dscr("qaT", [8, 128, NT], BF16)
    kaT = dscr("kaT", [8, 128, 2 * NT], BF16)
    va = dscr("va", [2 * NT, 1024], BF16)
    qbT = dscr("qbT", [8, 128, NT], BF16)
    kbT = dscr("kbT", [8, 128, 4 * NT], BF16)
    vb = dscr("vb", [4 * NT, 1024], BF16)
    mixs = dscr("mixs", [16, 128, NT], BF16)
    x1s = dscr("x1s", [NT, D], F32)
    h2s = dscr("h2s", [4, 128, KC, 512], BF16)
    ggs = dscr("ggs", [2, 128, D], F32)

    try:
      with ExitStack() as outer:
        S = Sched(nc, outer)
        build.S = S

        def sbo(name, shape, dt):
            return outer.enter_context(nc.sbuf_tensor("s_" + name, shape, dt))

        ident_t = sbo("ident", [128, 128], BF16)
        pswA_t = sbo("pswA", [128, 128], BF16)
        pswB_t = sbo("pswB", [128, 128], BF16)
        cmask_t = sbo("cmask", [128, 4, 512], BF16)
        band_t = sbo("band", [128, 2, 128], BF16)
        ident = ident_t[:]
        pswA = pswA_t[:]
        pswB = pswB_t[:]

        def cmask(m):
            return cmask_t[:, m, :]

        def bandm(hf):
            return band_t[:, hf, :]
        ones_bf = sbo("ones_bf", [128, 128], BF16)
        ones_f = sbo("ones_f", [128, 128], F32)
        ebias = sbo("ebias", [128, 4], F32)
        rc = sbo("rc", [128, 4], F32)
        halfpi = sbo("halfpi", [128, 1], F32)
        modv = sbo("modv", [128, 4, KC], F32)
        gcol = sbo("gcol", [128, 2], F32)
        gsub08 = sbo("gsub08", [128, 1], F32)
        neglam = sbo("neglam", [128, 1], F32)
        B_const = Buf("const")
        B_scr = {k: Buf(k) for k in ["qaT", "kaT", "va", "qbT", "kbT", "vb", "mixs", "x1s", "h2s", "y", "ggs"]}

        with ExitStack() as ph:
            def sb(name, shape, dt):
                return ph.enter_context(nc.sbuf_tensor("s_" + name, shape, dt))

            def ps(name, shape, dt):
                return ph.enter_context(nc.psum_tensor("p_" + name, shape, dt))
            cT = sb("cT", [128, KC], F32)
            GGa = sb("GGa0", [128, D], F32)
            GGf = sb("GGf0", [128, D], F32)
            scT = sb("scT", [128, KC], BF16)
            brow = sb("brow", [1, 6 * D], F32)
            modrow = sb("modrow", [1, 6 * D], F32)
            gpa = sb("gpa", [128, KC], F32)
            gpf = sb("gpf", [128, KC], F32)
            lamv = sb("lamv", [128, 256], F32)
            lprod = sb("lprod", [128, 128], F32)
            lsum = sb("lsum", [128, 2], F32)
            wbl = [sb(f"wbl{i}", [128, KC, 512], BF16) for i in range(2)]
            psM = ps("psM", [1, 512], F32)
            psC = ps("psC", [128, 4, KC], F32)
            psR = [ps(f"psR{i}", [128, 512], F32) for i in range(2)]
            B_cT, B_scT, B_brow, B_modrow, B_g, B_lam, B_lp, B_ls = (Buf(n) for n in
                                                                       ["cT", "scT", "brow", "modrow", "g", "lam", "lp", "ls"])
            B_wbl = [Buf("wbl0"), Buf("wbl1")]
            B_psM, B_psC = Buf("psM"), Buf("psC")
            B_psR = [Buf("psR0"), Buf("psR1")]
            B_GGa, B_GGf = Buf("GGa"), Buf("GGf")
            c0 = S.dma_ctr()
            for (dst, src) in [(ident, cb16_d[:, 0:128]), (pswA, cb16_d[:, 128:256]), (pswB, cb16_d[:, 256:384]),
                               (cmask_t[:], cb16_d[:, 384:384 + 2048].rearrange("p (m q) -> p m q", m=4)),
                               (band_t[:], cb16_d[:, 384 + 2048:384 + 2048 + 256].rearrange("p (m q) -> p m q", m=2)),
                               (ebias[:], ebias_d), (rc[:], rc_d), (gcol[:], gcol_d)]:
                S.op("sp", lambda e, dst=dst, src=src: e.dma_start(out=dst, in_=src), writes=[B_const], dma_ctr=c0)
            c1 = S.dma_ctr()
            S.op("sp", lambda e: e.dma_start(out=cT[:], in_=cT_d), writes=[B_cT], dma_ctr=c1)
            S.op("sp", lambda e: e.dma_start(out=brow[:], in_=b_ada), writes=[B_brow], dma_ctr=c1)
            S.op("sp", lambda e: e.dma_start(out=gpa[:], in_=gpa_d), writes=[B_g], dma_ctr=c1)
            S.op("sp", lambda e: e.dma_start(out=gpf[:], in_=gpf_d), writes=[B_g], dma_ctr=c1)
            S.op("sp", lambda e: e.dma_start(out=lamv[:], in_=lamv_d.broadcast_to([128, 256])), writes=[B_lam], dma_ctr=c1)
            c2 = S.dma_ctr()
            S.op("sp", lambda e: e.dma_start(out=GGa[:], in_=gposta_d.broadcast_to([128, D])), writes=[B_GGa], dma_ctr=c2)
            S.op("sp", lambda e: e.dma_start(out=GGf[:], in_=gpostf_d.broadcast_to([128, D])), writes=[B_GGf], dma_ctr=c2)
            S.op("pool", lambda e: e.memset(ones_bf[:], 1.0), writes=[B_const])
            S.op("pool", lambda e: e.memset(ones_f[:], 1.0), writes=[B_const])
            S.op("pool", lambda e: e.memset(halfpi[:], HALFPI), writes=[B_const])
            S.op("act", lambda e: e.activation(out=scT[:], in_=cT[:], func=AF.Silu), reads=[B_cT], writes=[B_scT])
            cw = [S.dma_ctr(), S.dma_ctr()]
            def load_ada(blk):
                i = blk % 2
                src = w_ada[:, blk * 512:(blk + 1) * 512].rearrange("(kc p) n -> p kc n", p=128)
                S.op("pool", lambda e, i=i, src=src: e.dma_start(out=wbl[i][:], in_=src), writes=[B_wbl[i]], dma_ctr=cw[i])
            load_ada(0)
            for blk in range(24):
                i = blk % 2
                if blk + 1 < 24:
                    load_ada(blk + 1)
                for kc in range(KC):
                    S.op("pe", lambda e, i=i, kc=kc: e.matmul(psM[:], lhsT=scT[:, kc:kc + 1], rhs=wbl[i][:, kc, :],
                                                             start=(kc == 0), stop=(kc == KC - 1)),
                         reads=[B_scT, B_wbl[i]], writes=[B_psM])
                S.op("dve", lambda e, blk=blk: e.tensor_tensor(out=modrow[0:1, blk * 512:(blk + 1) * 512], in0=psM[:],
                                                               in1=brow[0:1, blk * 512:(blk + 1) * 512], op=ALU.add),
                     reads=[B_psM, B_brow], writes=[B_modrow])
            for vi, sec in enumerate([1, 0, 4, 3]):
                for j in range(KC):
                    o = sec * D + j * 128
                    S.op("pe", lambda e, vi=vi, j=j, o=o: e.matmul(psC[:, vi, j:j + 1], lhsT=modrow[0:1, o:o + 128],
                                                                    rhs=ones_f[0:1, 0:1], start=True, stop=True),
                         reads=[B_modrow, B_const], writes=[B_psC])
            B_modv = B_const
            S.op("dve", lambda e: e.tensor_scalar(out=modv[:, 0, :], in0=psC[:, 0, :], scalar1=1.0, scalar2=None, op0=ALU.add),
                 reads=[B_psC], writes=[B_modv])
            S.op("dve", lambda e: e.tensor_tensor(out=modv[:, 0, :], in0=modv[:, 0, :], in1=gpa[:], op=ALU.mult),
                 reads=[B_modv, B_g], writes=[B_modv])
            S.op("dve", lambda e: e.tensor_copy(out=modv[:, 1, :], in_=psC[:, 1, :]), reads=[B_psC], writes=[B_modv])
            S.op("dve", lambda e: e.tensor_scalar(out=modv[:, 2, :], in0=psC[:, 2, :], scalar1=1.0, scalar2=None, op0=ALU.add),
                 reads=[B_psC], writes=[B_modv])
            S.op("dve", lambda e: e.tensor_tensor(out=modv[:, 2, :], in0=modv[:, 2, :], in1=gpf[:], op=ALU.mult),
                 reads=[B_modv, B_g], writes=[B_modv])
            S.op("dve", lambda e: e.tensor_copy(out=modv[:, 3, :], in_=psC[:, 3, :]), reads=[B_psC], writes=[B_modv])
            k = 0
            for (GG, Bg, sec) in [(GGa, B_GGa, 2), (GGf, B_GGf, 5)]:
                for cbk in range(4):
                    o = sec * D + cbk * 512
                    pr, Bpr = psR[k % 2], B_psR[k % 2]
                    k += 1
                    S.op("pe", lambda e, pr=pr, o=o: e.matmul(pr[:], lhsT=ones_f[0:1, :], rhs=modrow[0:1, o:o + 512],
                                                              start=True, stop=True),
                         reads=[B_modrow, B_const], writes=[Bpr])
                    S.op("dve", lambda e, GG=GG, pr=pr, cbk=cbk: e.tensor_tensor(out=GG[:, cbk * 512:(cbk + 1) * 512], in0=pr[:],
                                                                                 in1=GG[:, cbk * 512:(cbk + 1) * 512], op=ALU.mult),
                         reads=[Bpr, Bg], writes=[Bg])
            c_ggw = S.dma_ctr()
            S.op("sp", lambda e: e.dma_start(out=ggs[0], in_=GGa[:]), reads=[B_GGa], writes=[B_scr["ggs"]], dma_ctr=c_ggw)
            S.op("sp", lambda e: e.dma_start(out=ggs[1], in_=GGf[:]), reads=[B_GGf], writes=[B_scr["ggs"]], dma_ctr=c_ggw)
            S.op("dve", lambda e: e.tensor_tensor(out=lprod[:, 0:64], in0=lamv[:, 0:64], in1=lamv[:, 64:128], op=ALU.mult),
                 reads=[B_lam], writes=[B_lp])
            S.op("dve", lambda e: e.tensor_tensor(out=lprod[:, 64:128], in0=lamv[:, 128:192], in1=lamv[:, 192:256], op=ALU.mult),
                 reads=[B_lam], writes=[B_lp])
            S.op("dve", lambda e: e.reduce_sum(out=lsum[:, 0:1], in_=lprod[:, 0:64], axis=AX.X), reads=[B_lp], writes=[B_ls])
            S.op("dve", lambda e: e.reduce_sum(out=lsum[:, 1:2], in_=lprod[:, 64:128], axis=AX.X), reads=[B_lp], writes=[B_ls])
            S.op("act", lambda e: e.activation(out=lsum[:], in_=lsum[:], func=AF.Exp), reads=[B_ls], writes=[B_ls])
            S.op("dve", lambda e: e.tensor_tensor(out=neglam[:], in0=lsum[:, 1:2], in1=lsum[:, 0:1], op=ALU.subtract),
                 reads=[B_ls], writes=[B_const])
            S.op("dve", lambda e: e.tensor_scalar(out=neglam[:], in0=neglam[:], scalar1=-LAMBDA_INIT, scalar2=None, op0=ALU.add),
                 reads=[B_const], writes=[B_const])
            S.op("dve", lambda e: e.tensor_scalar(out=gsub08[:], in0=gcol[:, 1:2], scalar1=1.0 - LAMBDA_INIT, scalar2=None,
                                                   op0=ALU.mult), reads=[B_const], writes=[B_const])
            S.flush()

        def norm_tiles_to_hT(get_tile, ssq, B_ssq, xn, B_xn, psT, B_psT, hT_ap_fn, B_hT, mi, evk):
            for t in range(4):
                xt, Bx = get_tile(t)
                S.op("act", lambda e, xt=xt, t=t: e.activation(out=xn[:, t, :], in_=xt, func=AF.Square, accum_out=ssq[:, t:t + 1]),
                     reads=[Bx], writes=[B_xn, B_ssq[t]])
                S.op("act", lambda e, t=t: e.activation(out=ssq[:, t:t + 1], in_=ssq[:, t:t + 1], func=AF.Sqrt, bias=EPS,
                                                        scale=1.0 / D), reads=[B_ssq[t]], writes=[B_ssq[t]])
                S.op("dve", lambda e, t=t: e.reciprocal(out=ssq[:, t:t + 1], in_=ssq[:, t:t + 1]), reads=[B_ssq[t]],
                     writes=[B_ssq[t]])
                S.op("dve", lambda e, xt=xt, t=t: e.tensor_scalar(out=xn[:, t, :], in0=xt, scalar1=ssq[:, t:t + 1], scalar2=None,
                                                                   op0=ALU.mult), reads=[Bx, B_ssq[t]], writes=[B_xn])
            for kc in range(KC):
                pt, Bpt = psT[kc % 2], B_psT[kc % 2]
                for t in range(4):
                    S.op("pe", lambda e, pt=pt, t=t, kc=kc: e.transpose(out=pt[:, t * 128:(t + 1) * 128],
                                                                        in_=xn[:, t, kc * 128:(kc + 1) * 128], identity=ident),
                         reads=[B_xn, B_const], writes=[Bpt])
                dst = hT_ap_fn(kc)
                if (kc + evk) % 2 == 0:
                    S.op("act", lambda e, dst=dst, pt=pt, kc=kc: e.activation(out=dst, in_=pt[:], func=AF.Identity,
                                                                              bias=modv[:, mi + 1, kc:kc + 1],
                                                                              scale=modv[:, mi, kc:kc + 1]),
                         reads=[Bpt, B_const], writes=[B_hT[kc]])
                else:
                    S.op("dve", lambda e, dst=dst, pt=pt, kc=kc: e.tensor_scalar(out=dst, in0=pt[:], scalar1=modv[:, mi, kc:kc + 1],
                                                                                 scalar2=modv[:, mi + 1, kc:kc + 1],
                                                                                 op0=ALU.mult, op1=ALU.add),
                         reads=[Bpt, B_const], writes=[B_hT[kc]])

        if upto < 1:
            raise StopBuild()
        with ExitStack() as ph:
            def sb(name, shape, dt):
                return ph.enter_context(nc.sbuf_tensor("s_" + name, shape, dt))

            def ps(name, shape, dt):
                return ph.enter_context(nc.psum_tensor("p_" + name, shape, dt))
            hT = sb("hT", [128, KC, NT], BF16)
            B_hT = [Buf(f"hT{kc}") for kc in range(KC)]
            xtl = [sb(f"xt{i}", [128, D], F32) for i in range(2)]
            B_xtl = [Buf(f"xt{i}") for i in range(2)]
            c_xt = [S.dma_ctr() for _ in range(2)]
            xn = sb("xn", [128, 4, D], BF16)
            B_xn = Buf("xn")
            ssq = sb("ssq", [128, 4], F32)
            B_ssq = [Buf(f"ssq{i}") for i in range(4)]
            wb = [sb(f"wb{i}", [128, KC, 512], BF16) for i in range(2)]
            B_wb = [Buf("wb0"), Buf("wb1")]
            c_wb = [S.dma_ctr(), S.dma_ctr()]
            tabs = sb("tabs", [128, 4, NT], F32)
            B_tabs = Buf("tabs")
            pos_i = sb("pos_i", [128, 512], I32)
            posf = sb("posf", [128, 512], F32)
            ang = sb("ang", [128, 512], F32)
            ru = sb("ru", [128, 512], F32)
            B_posi, B_posf, B_ang, B_ru = Buf("posi"), Buf("posf"), Buf("ang"), Buf("ru")
            c_pos = S.dma_ctr()
            xq = Ring([(sb(f"xq{i}", [128, 512], BF16), Buf(f"xq{i}")) for i in range(2)])
            t1r = Ring([(sb(f"t1_{i}", [128, 512], F32), Buf(f"t1_{i}")) for i in range(2)])
            t2r = Ring([(sb(f"t2_{i}", [128, 512], F32), Buf(f"t2_{i}")) for i in range(2)])
            qst = Ring([(sb(f"qst{i}", [128, 512], BF16), Buf(f"qst{i}"), S.dma_ctr()) for i in range(3)])
            vst = Ring([(sb(f"vst{i}", [128, 4, 512], BF16), Buf(f"vst{i}"), S.dma_ctr()) for i in range(2)])
            psT = [ps(f"psT{i}", [128, 512], BF16) for i in range(2)]
            B_psT = [Buf("psT0"), Buf("psT1")]
            psQ = Ring([(ps(f"psQ{i}", [128, 512], F32), Buf(f"psQ{i}")) for i in range(2)])
            psW = Ring([(ps(f"psW{i}", [128, 512], F32), Buf(f"psW{i}")) for i in range(2)])
            psV = Ring([(ps(f"psV{i}", [128, 512], F32), Buf(f"psV{i}")) for i in range(2)])

            def blocks_for(slot):
                bl = []
                if slot == 0:
                    bl += [("q", 0, qaT, 0, 0), ("q", 512, qaT, 4, 0)]
                if slot <= 1:
                    bl += [("k", 1024, kaT, 0, 0), ("k", 1536, kaT, 4, 0)]
                if slot == 0:
                    bl += [("q", 3072, qbT, 0, 1), ("q", 3584, qbT, 4, 1)]
                bl += [("k", 4096, kbT, 0, 1), ("k", 4608, kbT, 4, 1)]
                if slot <= 1:
                    bl += [("v", 2048, va, 0, 0), ("v", 2560, va, 512, 0)]
                bl += [("v", 5120, vb, 0, 1), ("v", 5632, vb, 512, 1)]
                return bl

            DBG_SLOTS = int(os.environ.get('K_DBG_SLOTS', NSLOT))
            DBG_BLOCKS = int(os.environ.get('K_DBG_BLOCKS', 99))
            DBG_HT = int(os.environ.get('K_DBG_HT', 1))
            jobs = [(slot, bi) for slot in range(DBG_SLOTS) for bi in blocks_for(slot)[:DBG_BLOCKS]]

            def load_w(n):
                if n >= len(jobs):
                    return
                i = n % 2
                col = jobs[n][1][1]
                src = w_in[:, col:col + 512].rearrange("(kc p) n -> p kc n", p=128)
                S.op("pool", lambda e, i=i, src=src: e.dma_start(out=wb[i][:], in_=src), writes=[B_wb[i]], dma_ctr=c_wb[i])
            load_w(0)
            nwb = 0
            evk = 0
            for slot in range(DBG_SLOTS):
                for g in range(4):
                    S.op("sp", lambda e, slot=slot, g=g: e.dma_start(
                        out=pos_i[:], in_=posi_d[slot, :, g * 512:(g + 1) * 512].broadcast_to([128, 512])),
                        writes=[B_posi], dma_ctr=c_pos)
                    S.op("dve", lambda e: e.tensor_copy(out=posf[:], in_=pos_i[:]), reads=[B_posi], writes=[B_posf])
                    for ts in range(2):
                        if ts == 0 and slot > 1:
                            continue
                        S.op("dve", lambda e, ts=ts: e.tensor_scalar(out=ang[:], in0=posf[:], scalar1=rc[:, ts:ts + 1], scalar2=None,
                                                                      op0=ALU.mult), reads=[B_posf, B_const], writes=[B_ang])
                        S.op("dve", lambda e: e.tensor_scalar(out=ru[:], in0=ang[:], scalar1=INV2PI, scalar2=MAGIC, op0=ALU.mult,
                                                               op1=ALU.add), reads=[B_ang], writes=[B_ru])
                        S.op("dve", lambda e: e.tensor_scalar(out=ru[:], in0=ru[:], scalar1=MAGIC, scalar2=None, op0=ALU.subtract),
                             reads=[B_ru], writes=[B_ru])
                        S.op("dve", lambda e: e.scalar_tensor_tensor(out=ang[:], in0=ru[:], scalar=-C1, in1=ang[:], op0=ALU.mult,
                                                                      op1=ALU.add), reads=[B_ru, B_ang], writes=[B_ang])
                        S.op("dve", lambda e: e.scalar_tensor_tensor(out=ang[:], in0=ru[:], scalar=-C2, in1=ang[:], op0=ALU.mult,
                                                                      op1=ALU.add), reads=[B_ru, B_ang], writes=[B_ang])
                        S.op("dve", lambda e: e.tensor_scalar(out=ang[:], in0=ang[:], scalar1=-PI_SAFE, scalar2=PI_SAFE, op0=ALU.max,
                                                               op1=ALU.min), reads=[B_ang], writes=[B_ang])
                        S.op("act", lambda e, ts=ts, g=g: e.activation(out=tabs[:, 2 * ts + 1, g * 512:(g + 1) * 512], in_=ang[:],
                                                                       func=AF.Sin, scale=rc[:, 2 + ts:3 + ts]),
                             reads=[B_ang, B_const], writes=[B_tabs])
                        S.op("act", lambda e: e.activation(out=ru[:], in_=ang[:], func=AF.Abs), reads=[B_ang], writes=[B_ru])
                        S.op("act", lambda e, ts=ts, g=g: e.activation(out=tabs[:, 2 * ts, g * 512:(g + 1) * 512], in_=ru[:],
                                                                       func=AF.Sin, bias=halfpi[:, 0:1], scale=-1.0),
                             reads=[B_ru, B_const], writes=[B_tabs])
                for g in range(4 if DBG_HT else 0):
                    def get_tile(t, slot=slot, g=g):
                        r0 = g * 512 + t * 128
                        S.op("sp", lambda e, slot=slot, r0=r0, t=t: e.dma_start(out=xtl[t % 2][:], in_=xs[slot, r0:r0 + 128, :]),
                             writes=[B_xtl[t % 2]], dma_ctr=c_xt[t % 2])
                        return xtl[t % 2][:], B_xtl[t % 2]
                    norm_tiles_to_hT(get_tile, ssq, B_ssq, xn, B_xn, psT, B_psT,
                                     lambda kc, g=g: hT[:, kc, g * 512:(g + 1) * 512], B_hT, 0, evk)
                    evk += 1
                tokA = {0: NT, 1: 0}.get(slot, None)
                tokB = slot * NT
                for (kind, col, dst, h0, rs) in blocks_for(slot)[:DBG_BLOCKS]:
                    i = nwb % 2
                    assert jobs[nwb][0] == slot and jobs[nwb][1][1] == col
                    nwb += 1
                    load_w(nwb)
                    tok0 = (tokA if rs == 0 else tokB)
                    if kind == "q":
                        tok0 = 0
                    Bdst = B_scr[{id(qaT): "qaT", id(kaT): "kaT", id(va): "va", id(qbT): "qbT", id(kbT): "kbT", id(vb): "vb"}[id(dst)]]
                    if kind in ("q", "k"):
                        psw = pswA if rs == 0 else pswB
                        for hh in range(4):
                            for g in range(4):
                                pq, Bpq = psQ.next()
                                for kc in range(KC):
                                    S.op("pe", lambda e, pq=pq, i=i, kc=kc, hh=hh, g=g: e.matmul(
                                        pq[:], lhsT=wb[i][:, kc, hh * 128:(hh + 1) * 128], rhs=hT[:, kc, g * 512:(g + 1) * 512],
                                        start=(kc == 0), stop=(kc == KC - 1)), reads=[B_wb[i], B_hT[kc]], writes=[Bpq])
                                xqt, Bxq = xq.next()
                                S.op("act", lambda e, xqt=xqt, pq=pq: e.activation(out=xqt[:], in_=pq[:], func=AF.Copy),
                                     reads=[Bpq], writes=[Bxq])
                                DBGQ = os.environ.get("K_DBG_Q", "full")
                                if DBGQ == "mm":
                                    continue
                                pw, Bpw = psW.next()
                                DBGH = os.environ.get("K_DBG_H", "")
                                if DBGH == "H2":
                                    S.op("pe", lambda e, pw=pw, psw=psw, g=g: e.matmul(pw[:], lhsT=psw, rhs=hT[:, 0, g * 512:(g + 1) * 512],
                                                                                      start=True, stop=True),
                                         reads=[B_hT[0], B_const], writes=[Bpw])
                                elif DBGH == "H1":
                                    S.op("pe", lambda e, pw=pw, xqt=xqt: e.matmul(pw[:], lhsT=ident, rhs=xqt[:], start=True, stop=True),
                                         reads=[Bxq, B_const], writes=[Bpw])
                                else:
                                    S.op("pe", lambda e, pw=pw, psw=psw, xqt=xqt: e.matmul(pw[:], lhsT=psw, rhs=xqt[:], start=True,
                                                                                           stop=True),
                                         reads=[Bxq, B_const], writes=[Bpw])
                                if DBGQ == "swap":
                                    continue
                                t1, Bt1 = t1r.next()
                                t2, Bt2 = t2r.next()
                                if os.environ.get("K_DBG_T1IN", "") == "const":
                                    S.op("dve", lambda e, t1=t1, pq=pq, rs=rs, g=g: e.tensor_tensor(
                                        out=t1[:], in0=pq[:], in1=posf[:], op=ALU.mult),
                                        reads=[Bpq, B_posf], writes=[Bt1])
                                elif os.environ.get("K_DBG_T1IN", "") == "copy":
                                    S.op("dve", lambda e, t1=t1, pq=pq, rs=rs, g=g: e.tensor_copy(out=t1[:], in_=pq[:]),
                                         reads=[Bpq, Bxq], writes=[Bt1])
                                else:
                                    S.op("dve", lambda e, t1=t1, pq=pq, rs=rs, g=g: e.tensor_tensor(
                                        out=t1[:], in0=pq[:], in1=tabs[:, 2 * rs, g * 512:(g + 1) * 512], op=ALU.mult),
                                        reads=[Bpq, B_tabs], writes=[Bt1])
                                if DBGQ == "t1":
                                    continue
                                S.op("dve", lambda e, t2=t2, pw=pw, rs=rs, g=g: e.tensor_tensor(
                                    out=t2[:], in0=pw[:], in1=tabs[:, 2 * rs + 1, g * 512:(g + 1) * 512], op=ALU.mult),
                                    reads=[Bpw, B_tabs], writes=[Bt2])
                                if DBGQ == "t2":
                                    continue
                                qs, Bqs, cqs = qst.next()
                                S.op(os.environ.get("K_ROPE_ENG", "pool"), lambda e, qs=qs, t1=t1, t2=t2: e.tensor_tensor(out=qs[:], in0=t1[:], in1=t2[:],
                                                                                           op=ALU.add),
                                     reads=[Bt1, Bt2], writes=[Bqs])
                                c0_ = tok0 + g * 512
                                if DBGQ == "rope":
                                    continue
                                S.op("sp", lambda e, dst=dst, qs=qs, h=h0 + hh, c0_=c0_: e.dma_start(
                                    out=dst[h, :, c0_:c0_ + 512], in_=qs[:]), reads=[Bqs], writes=[Bdst], dma_ctr=cqs)
                    else:
                        for tq in range(4):
                            vs, Bvs, cvs = vst.next()
                            for tt in range(4):
                                tile = tq * 4 + tt
                                pv, Bpv = psV.next()
                                for kc in range(KC):
                                    S.op("pe", lambda e, pv=pv, i=i, kc=kc, tile=tile: e.matmul(
                                        pv[:], lhsT=hT[:, kc, tile * 128:(tile + 1) * 128], rhs=wb[i][:, kc, :],
                                        start=(kc == 0), stop=(kc == KC - 1)), reads=[B_wb[i], B_hT[kc]], writes=[Bpv])
                                if tt % 2 == 0:
                                    S.op("act", lambda e, vs=vs, pv=pv, tt=tt: e.activation(out=vs[:, tt, :], in_=pv[:], func=AF.Copy),
                                         reads=[Bpv], writes=[Bvs])
                                else:
                                    S.op("dve", lambda e, vs=vs, pv=pv, tt=tt: e.tensor_copy(out=vs[:, tt, :], in_=pv[:]),
                                         reads=[Bpv], writes=[Bvs])
                            r0 = tok0 + tq * 512
                            S.op("sp", lambda e, dst=dst, vs=vs, r0=r0, h0=h0: e.dma_start(
                                out=dst[r0:r0 + 512, h0:h0 + 512].rearrange("(t p) n -> p t n", p=128), in_=vs[:]),
                                reads=[Bvs], writes=[Bdst], dma_ctr=cvs)
            S.flush()

        if upto < 2:
            raise StopBuild()
        with ExitStack() as ph:
            def sb(name, shape, dt):
                return ph.enter_context(nc.sbuf_tensor("s_" + name, shape, dt))

            def ps(name, shape, dt):
                return ph.enter_context(nc.psum_tensor("p_" + name, shape, dt))
            hb = Ring([(sb(f"qh{i}", [128, NT], BF16), sb(f"kh{i}", [128, 4 * NT], BF16), sb(f"vh{i}", [128, 64, 128], BF16),
                        Buf(f"hb{i}"), S.dma_ctr()) for i in range(2)])
            pTr = Ring([(sb(f"pT_{i}", [128, 1024], BF16), Buf(f"pT_{i}")) for i in range(3)])
            accr = Ring([(sb(f"accD{i}", [128, 1024], F32), Buf(f"accD{i}"), sb(f"accP{i}", [128, 1024], F32), Buf(f"accP{i}"))
                         for i in range(2)])
            rden = [sb(f"rden{m}", [128, 512], F32) for m in range(2)]
            B_rden = [Buf("rden0"), Buf("rden1")]
            o1 = sb("o1", [128, 512], F32)
            o2 = sb("o2", [128, 512], F32)
            obr = Ring([(sb(f"ob{i}", [128, 512], F32), Buf(f"ob{i}")) for i in range(2)])
            sqr = Ring([(sb(f"sq{i}", [128, 512], F32), Buf(f"sq{i}")) for i in range(2)])
            rstd = sb("rstd", [128, 512], F32)
            B_o1, B_o2, B_rstd = Buf("o1"), Buf("o2"), Buf("rstd")
            mst = Ring([(sb(f"mst{i}", [128, NT], BF16), Buf(f"mst{i}"), S.dma_ctr()) for i in range(2)])
            psS = Ring([(ps(f"psS{i}", [128, 1024], F32), Buf(f"psS{i}")) for i in range(2)])
            psO = [ps(f"psO{m}", [128, 512], F32) for m in range(2)]
            B_psO = [Buf("psO0"), Buf("psO1")]
            psD = Ring([(ps(f"psD{i}", [128, 512], F32), Buf(f"psD{i}")) for i in range(2)])

            def load_head_B(h):
                qh, kh, vh, Bh, ch = hb.next()
                S.op("sp", lambda e: e.dma_start(out=qh[:], in_=qbT[h]), reads=[B_scr["qbT"]], writes=[Bh], dma_ctr=ch)
                S.op("sp", lambda e: e.dma_start(out=kh[:], in_=kbT[h]), reads=[B_scr["kbT"]], writes=[Bh], dma_ctr=ch)
                S.op("sp", lambda e: e.dma_start(out=vh[:], in_=vb[:, h * 128:(h + 1) * 128].rearrange("(t p) n -> p t n", p=128)),
                     reads=[B_scr["vb"]], writes=[Bh], dma_ctr=ch)
                return qh, kh, vh, Bh

            heads = {0: load_head_B(0)}
            mstage = {}
            items = []
            for h in range(8):
                for qb in range(4):
                    pairs = []
                    for slot in range(NSLOT):
                        nk = 4 * (qb + 1) if slot == 0 else 16
                        for kc in range(nk):
                            pairs.append((slot, kc))
                    for pi, (slot, kc) in enumerate(pairs):
                        items.append((h, qb, pi, len(pairs), slot, kc))
            qk_out = {}

            def issue_qk(idx):
                h, qb, pi, npairs, slot, kc = items[idx]
                if h not in heads:
                    heads[h] = load_head_B(h)
                qh, kh, vh, Bh = heads[h]
                ktok = slot * NT + kc * 128
                pS, BpS = psS.next()
                for m in range(2):
                    S.op("pe", lambda e, pS=pS, m=m, ktok=ktok, qb=qb, kh=kh, qh=qh: e.matmul(
                        pS[:, m * 512:(m + 1) * 512], lhsT=kh[m * 64:(m + 1) * 64, ktok:ktok + 128],
                        rhs=qh[m * 64:(m + 1) * 64, qb * 512:(qb + 1) * 512], start=True, stop=True), reads=[Bh], writes=[BpS])
                qk_out[idx] = (pS, BpS)

            deferred = []

            def finalize_a(h, qb, accs):
                ms, Bms, cms = mstage[h]
                accD, BaccD, accP, BaccP = accs
                for m in range(2):
                    pD, BpD = psD.next()
                    S.op("pe", lambda e, pD=pD, m=m, accD=accD: e.matmul(pD[:], lhsT=ones_f[:], rhs=accD[:, m * 512:(m + 1) * 512],
                                                                         start=True, stop=False),
                         reads=[BaccD, B_const], writes=[BpD])
                    S.op("pe", lambda e, pD=pD, m=m, accP=accP: e.matmul(pD[:], lhsT=ones_f[:], rhs=accP[:, m * 512:(m + 1) * 512],
                                                                         start=False, stop=True),
                         reads=[BaccP, B_const], writes=[BpD])
                    S.op("dve", lambda e, pD=pD, m=m: e.reciprocal(out=rden[m][:], in_=pD[:]), reads=[BpD], writes=[B_rden[m]])
                S.op("dve", lambda e: e.tensor_tensor(out=o1[:], in0=psO[0][:], in1=rden[0][:], op=ALU.mult),
                     reads=[B_psO[0], B_rden[0]], writes=[B_o1])
                S.op("dve", lambda e: e.tensor_tensor(out=o2[:], in0=psO[1][:], in1=rden[1][:], op=ALU.mult),
                     reads=[B_psO[1], B_rden[1]], writes=[B_o2])
                ob, Bob = obr.next()
                sq, Bsq = sqr.next()
                S.op("dve", lambda e, ob=ob: e.scalar_tensor_tensor(out=ob[:], in0=o2[:], scalar=neglam[:, 0:1], in1=o1[:], op0=ALU.mult,
                                                                     op1=ALU.add), reads=[B_o1, B_o2, B_const], writes=[Bob])
                S.op("act", lambda e, ob=ob, sq=sq: e.activation(out=sq[:], in_=ob[:], func=AF.Square), reads=[Bob], writes=[Bsq])

                def part_b():
                    pD, BpD = psD.next()
                    S.op("pe", lambda e, pD=pD: e.matmul(pD[:], lhsT=ones_f[:], rhs=sq[:], start=True, stop=True),
                         reads=[Bsq, B_const], writes=[BpD])
                    S.op("act", lambda e, pD=pD: e.activation(out=rstd[:], in_=pD[:], func=AF.Sqrt, bias=EPS, scale=1.0 / HD),
                         reads=[BpD], writes=[B_rstd])
                    S.op("dve", lambda e: e.reciprocal(out=rstd[:], in_=rstd[:]), reads=[B_rstd], writes=[B_rstd])
                    S.op("dve", lambda e: e.scalar_tensor_tensor(out=ms[:, qb * 512:(qb + 1) * 512], in0=ob[:],
                                                                  scalar=gsub08[:, 0:1], in1=rstd[:], op0=ALU.mult, op1=ALU.mult),
                         reads=[Bob, B_rstd, B_const], writes=[Bms])
                    if qb == 3:
                        S.op("sp", lambda e: e.dma_start(out=mixs[8 + h], in_=ms[:]), reads=[Bms], writes=[B_scr["mixs"]], dma_ctr=cms)
                return part_b

            issue_qk(0)
            accs = None
            for idx, (h, qb, pi, npairs, slot, kc) in enumerate(items):
                if idx + 1 < len(items):
                    issue_qk(idx + 1)
                if h + 1 < 8 and h + 1 not in heads and pi == 0 and qb == 0:
                    heads[h + 1] = load_head_B(h + 1)
                if h not in mstage:
                    mstage[h] = mst.next()
                qh, kh, vh, Bh = heads[h]
                pS, BpS = qk_out.pop(idx)
                pT, BpT = pTr.next()
                kt = slot * 16 + kc
                S.op("act", lambda e, pT=pT, pS=pS, slot=slot: e.activation(out=pT[:], in_=pS[:], func=AF.Exp,
                                                                             bias=ebias[:, slot:slot + 1], scale=SCALE_B),
                     reads=[BpS, B_const], writes=[BpT])
                if slot == 0 and kc >= 4 * qb:
                    mk = cmask(kc - 4 * qb)
                    S.op("pool", lambda e, pT=pT, mk=mk: e.tensor_tensor(
                        out=pT[:].rearrange("p (m q) -> p m q", m=2), in0=pT[:].rearrange("p (m q) -> p m q", m=2),
                        in1=mk.unsqueeze(1).broadcast_to([128, 2, 512]), op=ALU.mult), reads=[BpT, B_const], writes=[BpT])
                if pi == 0:
                    accs = accr.next()
                on_pool = (pi % 3 == 2)
                acc, Bacc = (accs[2], accs[3]) if on_pool else (accs[0], accs[1])
                aeng = "pool" if on_pool else "dve"
                if pi == 0 or pi == 2:
                    S.op(aeng, lambda e, acc=acc, pT=pT: e.tensor_copy(out=acc[:], in_=pT[:]), reads=[BpT], writes=[Bacc])
                else:
                    S.op(aeng, lambda e, acc=acc, pT=pT: e.tensor_tensor(out=acc[:], in0=acc[:], in1=pT[:], op=ALU.add),
                         reads=[BpT, Bacc], writes=[Bacc])
                for m in range(2):
                    S.op("pe", lambda e, m=m, pT=pT, kt=kt, vh=vh, pi=pi, npairs=npairs: e.matmul(
                        psO[m][:], lhsT=vh[:, kt, :], rhs=pT[:, m * 512:(m + 1) * 512], start=(pi == 0), stop=(pi == npairs - 1)),
                        reads=[BpT, Bh], writes=[B_psO[m]])
                for dd in [x for x in deferred if x[0] <= idx]:
                    deferred.remove(dd)
                    dd[1]()
                if pi == npairs - 1:
                    deferred.append((idx + 4, finalize_a(h, qb, accs)))
            for dd in deferred:
                dd[1]()
            S.flush()

        if upto < 3:
            raise StopBuild()
        with ExitStack() as ph:
            def sb(name, shape, dt):
                return ph.enter_context(nc.sbuf_tensor("s_" + name, shape, dt))

            def ps(name, shape, dt):
                return ph.enter_context(nc.psum_tensor("p_" + name, shape, dt))
            DILS = (1, 4, 16)
            ha = Ring([(sb(f"qa{i}", [128, NT], BF16), sb(f"ka{i}", [128, 2 * NT], BF16),
                        [sb(f"va{i}_{d}", [128, 32, 128], BF16) for d in DILS], Buf(f"ha{i}"), S.dma_ctr()) for i in range(2)])
            numacc = sb("numacc", [128, NT], F32)
            denacc = sb("denacc", [128, NT], F32)
            B_num, B_den = Buf("numacc"), Buf("denacc")
            pAr = [Ring([(sb(f"pA{hf}_{i}", [128, 512], BF16), Buf(f"pA{hf}_{i}")) for i in range(2)]) for hf in range(2)]
            sqa = sb("sqa", [128, 512], F32)
            rsa = sb("rsa", [128, 512], F32)
            B_sqa, B_rsa = Buf("sqa"), Buf("rsa")
            msa = Ring([(sb(f"msa{i}", [128, NT], BF16), Buf(f"msa{i}"), S.dma_ctr()) for i in range(2)])
            psSa = [Ring([(ps(f"psSa{hf}_{i}", [128, 512], F32), Buf(f"psSa{hf}_{i}")) for i in range(2)]) for hf in range(2)]
            psOa = Ring([(ps(f"psOa{i}", [128, 512], F32), Buf(f"psOa{i}")) for i in range(2)])
            psDa = Ring([(ps(f"psDa{i}", [128, 512], F32), Buf(f"psDa{i}")) for i in range(2)])

            def load_head_A(h):
                qh, kh, vhs, Bh, ch = ha.next()
                S.op("sp", lambda e: e.dma_start(out=qh[:], in_=qaT[h]), reads=[B_scr["qaT"]], writes=[Bh], dma_ctr=ch)
                S.op("sp", lambda e: e.dma_start(out=kh[:], in_=kaT[h]), reads=[B_scr["kaT"]], writes=[Bh], dma_ctr=ch)
                for di, d in enumerate(DILS):
                    nb = 32 // d
                    for r in range(d):
                        src = va[:, h * 128:(h + 1) * 128].rearrange("(b p d) n -> d p b n", p=128, d=d)[r]
                        S.op("sp", lambda e, vt=vhs[di], r=r, nb=nb, src=src: e.dma_start(out=vt[:, r * nb:(r + 1) * nb, :], in_=src),
                             reads=[B_scr["va"]], writes=[Bh], dma_ctr=ch)
                return qh, kh, vhs, Bh

            cur = load_head_A(0)
            for h in range(8):
                qh, kh, vhs, Bh = cur
                if h + 1 < 8:
                    cur = load_head_A(h + 1)
                ms, Bms, cms = msa.next()
                for di, d in enumerate(DILS):
                    nb = 32 // d
                    ob0 = nb // 2
                    if d == 1:
                        batches = [[(b, 0) for b in range(b0, b0 + 4)] for b0 in range(ob0, nb, 4)]
                    else:
                        batches = []
                        for b in range(ob0, nb):
                            for r0 in range(0, d, 4):
                                batches.append([(b, r) for r in range(r0, r0 + 4)])
                    for items in batches:
                        pss = [psSa[0].next(), psSa[1].next()]
                        pas = [pAr[0].next(), pAr[1].next()]
                        for hf in range(2):
                            pS, BpS = pss[hf]
                            for it, (b, r) in enumerate(items):
                                bk = b - 1 + hf
                                kcol = bk * 128 * d + r
                                qcol = b * 128 * d + r - NT
                                S.op("pe", lambda e, pS=pS, it=it, kcol=kcol, qcol=qcol, d=d, kh=kh, qh=qh: e.matmul(
                                    pS[:, it * 128:(it + 1) * 128], lhsT=kh[:, kcol:kcol + 127 * d + 1:d],
                                    rhs=qh[:, qcol:qcol + 127 * d + 1:d], start=True, stop=True), reads=[Bh], writes=[BpS])
                            pA, BpA = pas[hf]
                            segs = []
                            for it, (b, r) in enumerate(items):
                                bk = b - 1 + hf
                                segs.append(1 if bk < ob0 else 0)
                            s0 = 0
                            while s0 < 4:
                                s1 = s0
                                while s1 < 4 and segs[s1] == segs[s0]:
                                    s1 += 1
                                S.op("act", lambda e, pA=pA, pS=pS, s0=s0, s1=s1, bs=segs[s0]: e.activation(
                                    out=pA[:, s0 * 128:s1 * 128], in_=pS[:, s0 * 128:s1 * 128], func=AF.Exp,
                                    bias=ebias[:, bs:bs + 1], scale=SCALE_A), reads=[BpS, B_const], writes=[BpA])
                                s0 = s1
                            S.op("pool", lambda e, pA=pA, hf=hf: e.tensor_tensor(
                                out=pA[:].rearrange("p (i q) -> p i q", i=4), in0=pA[:].rearrange("p (i q) -> p i q", i=4),
                                in1=bandm(hf).unsqueeze(1).broadcast_to([128, 4, 128]), op=ALU.mult),
                                reads=[BpA, B_const], writes=[BpA])
                        pO, BpO = psOa.next()
                        pD, BpD = psDa.next()
                        for it, (b, r) in enumerate(items):
                            for hf in range(2):
                                bk = b - 1 + hf
                                pA, BpA = pas[hf]
                                S.op("pe", lambda e, pO=pO, pA=pA, it=it, di=di, vi=r * nb + bk, hf=hf, vhs=vhs: e.matmul(
                                    pO[:, it * 128:(it + 1) * 128], lhsT=vhs[di][:, vi, :], rhs=pA[:, it * 128:(it + 1) * 128],
                                    start=(hf == 0), stop=(hf == 1)), reads=[BpA, Bh], writes=[BpO])
                        for it, (b, r) in enumerate(items):
                            for hf in range(2):
                                pA, BpA = pas[hf]
                                S.op("pe", lambda e, pD=pD, pA=pA, it=it, hf=hf: e.matmul(
                                    pD[:, it * 128:(it + 1) * 128], lhsT=ones_bf[:], rhs=pA[:, it * 128:(it + 1) * 128],
                                    start=(hf == 0), stop=(hf == 1)), reads=[BpA, B_const], writes=[BpD])
                        b0, r0 = items[0]
                        if d == 1:
                            c0_ = b0 * 128 - NT

                            def dstv(t):
                                return t[:, c0_:c0_ + 512].rearrange("e (i p) -> e i p", i=4)
                        else:
                            c0_ = b0 * 128 * d - NT

                            def dstv(t, c0_=c0_, d=d, r0=r0):
                                return t[:, c0_:c0_ + 128 * d].rearrange("e (p r) -> e r p", r=d)[:, r0:r0 + 4, :]
                        first = (di == 0)
                        for (accT, Bacc, pX, BpX, eng) in [(numacc, B_num, pO, BpO, "dve"), (denacc, B_den, pD, BpD, "dve")]:
                            dv = dstv(accT)
                            src = pX[:].rearrange("e (i p) -> e i p", i=4)
                            if first:
                                S.op(eng, lambda e, dv=dv, src=src: e.tensor_copy(out=dv, in_=src), reads=[BpX], writes=[Bacc])
                            else:
                                S.op(eng, lambda e, dv=dv, src=src: e.tensor_tensor(out=dv, in0=src, in1=dv, op=ALU.add),
                                     reads=[BpX, Bacc], writes=[Bacc])
                S.op("dve", lambda e: e.reciprocal(out=denacc[:], in_=denacc[:]), reads=[B_den], writes=[B_den])
                S.op("dve", lambda e: e.tensor_tensor(out=numacc[:], in0=numacc[:], in1=denacc[:], op=ALU.mult),
                     reads=[B_num, B_den], writes=[B_num])
                for qb in range(4):
                    sl = slice(qb * 512, (qb + 1) * 512)
                    S.op("act", lambda e, sl=sl: e.activation(out=sqa[:], in_=numacc[:, sl], func=AF.Square), reads=[B_num], writes=[B_sqa])
                    pD, BpD = psDa.next()
                    S.op("pe", lambda e, pD=pD: e.matmul(pD[:], lhsT=ones_f[:], rhs=sqa[:], start=True, stop=True),
                         reads=[B_sqa, B_const], writes=[BpD])
                    S.op("act", lambda e, pD=pD: e.activation(out=rsa[:], in_=pD[:], func=AF.Sqrt, bias=EPS, scale=1.0 / HD),
                         reads=[BpD], writes=[B_rsa])
                    S.op("dve", lambda e: e.reciprocal(out=rsa[:], in_=rsa[:]), reads=[B_rsa], writes=[B_rsa])
                    S.op("dve", lambda e, ms=ms, sl=sl: e.scalar_tensor_tensor(out=ms[:, sl], in0=numacc[:, sl], scalar=gcol[:, 0:1],
                                                                               in1=rsa[:], op0=ALU.mult, op1=ALU.mult),
                         reads=[B_num, B_rsa, B_const], writes=[Bms])
                S.op("sp", lambda e, ms=ms, h=h: e.dma_start(out=mixs[h], in_=ms[:]), reads=[Bms], writes=[B_scr["mixs"]], dma_ctr=cms)
            S.flush()

        if upto < 4:
            raise StopBuild()
        with ExitStack() as ph:
            def sb(name, shape, dt):
                return ph.enter_context(nc.sbuf_tensor("s_" + name, shape, dt))

            def ps(name, shape, dt):
                return ph.enter_context(nc.psum_tensor("p_" + name, shape, dt))
            wo = sb("wo", [128, KC, D], BF16)
            B_wo = Buf("wo")
            c_wo = S.dma_ctr()
            mg = Ring([(sb(f"mg{i}", [128, 16, 512], BF16), Buf(f"mg{i}"), S.dma_ctr()) for i in range(2)])
            xt4 = Ring([(sb(f"x4_{i}", [128, D], F32), Buf(f"x4_{i}"), S.dma_ctr()) for i in range(2)])
            x1t = [sb(f"x1t{i}", [128, D], F32) for i in range(4)]
            B_x1t = [Buf(f"x1t{i}") for i in range(4)]
            c_x1t = [S.dma_ctr() for _ in range(4)]
            xn = sb("xn4", [128, 4, D], BF16)
            B_xn = Buf("xn4")
            ssq = sb("ssq4", [128, 4], F32)
            B_ssq = [Buf(f"ssq4_{i}") for i in range(4)]
            ssy = sb("ssy", [128, 4], F32)
            rsy = sb("rsy", [128, 1], F32)
            B_ssy, B_rsy = Buf("ssy"), Buf("rsy")
            junk = sb("junk4", [128, 512], BF16)
            B_junk = Buf("junk4")
            GGa = sb("GGa", [128, D], F32)
            c_gg = S.dma_ctr()
            S.op("sp", lambda e: e.dma_start(out=GGa[:], in_=ggs[0]), reads=[B_scr["ggs"]], writes=[B_const], dma_ctr=c_gg)
            h2g = Ring([(sb(f"h2g{i}", [128, KC, 512], BF16), [Buf(f"h2g{i}_{kc}") for kc in range(KC)], S.dma_ctr()) for i in range(1)])
            psY = ps("psY", [128, D], F32)
            B_psY = Buf("psY")
            psT = [ps(f"psT4_{i}", [128, 512], BF16) for i in range(2)]
            B_psT = [Buf("psT4_0"), Buf("psT4_1")]
            for cbk in range(4):
                src = w_out[:, cbk * 512:(cbk + 1) * 512].rearrange("(kc p) n -> p kc n", p=128)
                S.op("pool", lambda e, cbk=cbk, src=src: e.dma_start(out=wo[:, :, cbk * 512:(cbk + 1) * 512], in_=src),
                     writes=[B_wo], dma_ctr=c_wo)
            for g in range(4):
                mgt, Bmg, cmg = mg.next()
                S.op("sp", lambda e, mgt=mgt, g=g: e.dma_start(out=mgt[:], in_=mixs[:, :, g * 512:(g + 1) * 512].rearrange("h p t -> p h t")),
                     reads=[B_scr["mixs"]], writes=[Bmg], dma_ctr=cmg)
                for t in range(4):
                    r0 = g * 512 + t * 128
                    xt, Bxt, cxt = xt4.next()
                    S.op("sp", lambda e, xt=xt, r0=r0: e.dma_start(out=xt[:], in_=xs[0, r0:r0 + 128, :]), writes=[Bxt], dma_ctr=cxt)
                    for cbk in range(4):
                        for hc in range(16):
                            S.op("pe", lambda e, mgt=mgt, t=t, cbk=cbk, hc=hc: e.matmul(
                                psY[:, cbk * 512:(cbk + 1) * 512], lhsT=mgt[:, hc, t * 128:(t + 1) * 128],
                                rhs=wo[:, hc, cbk * 512:(cbk + 1) * 512], start=(hc == 0), stop=(hc == 15)),
                                reads=[Bmg, B_wo], writes=[B_psY])
                    for cbk in range(4):
                        S.op("act", lambda e, cbk=cbk: e.activation(out=junk[:, 0:512], in_=psY[:, cbk * 512:(cbk + 1) * 512],
                                                                    func=AF.Square, accum_out=ssy[:, cbk:cbk + 1]),
                             reads=[B_psY], writes=[B_junk, B_ssy])
                    S.op("dve", lambda e: e.reduce_sum(out=rsy[:], in_=ssy[:], axis=AX.X), reads=[B_ssy], writes=[B_rsy])
                    S.op("act", lambda e: e.activation(out=rsy[:], in_=rsy[:], func=AF.Sqrt, bias=EPS, scale=1.0 / D),
                         reads=[B_rsy], writes=[B_rsy])
                    S.op("dve", lambda e: e.reciprocal(out=rsy[:], in_=rsy[:]), reads=[B_rsy], writes=[B_rsy])
                    x1, Bx1 = x1t[t], B_x1t[t]
                    for cbk in range(4):
                        sl = slice(cbk * 512, (cbk + 1) * 512)
                        S.op("dve", lambda e, x1=x1, sl=sl: e.scalar_tensor_tensor(out=x1[:, sl], in0=psY[:, sl], scalar=rsy[:, 0:1],
                                                                                   in1=GGa[:, sl], op0=ALU.mult, op1=ALU.mult),
                             reads=[B_psY, B_rsy, B_const], writes=[Bx1])
                    S.op("pool", lambda e, x1=x1, xt=xt: e.tensor_tensor(out=x1[:], in0=x1[:], in1=xt[:], op=ALU.add),
                         reads=[Bx1, Bxt], writes=[Bx1])
                    S.op("sp", lambda e, x1=x1, r0=r0: e.dma_start(out=x1s[r0:r0 + 128, :], in_=x1[:]), reads=[Bx1],
                         writes=[B_scr["x1s"]], dma_ctr=c_x1t[t])
                hg, Bhg, chg = h2g.next()
                norm_tiles_to_hT(lambda t: (x1t[t][:], B_x1t[t]), ssq, B_ssq, xn, B_xn, psT, B_psT,
                                 lambda kc, hg=hg: hg[:, kc, :], Bhg, 2, g)
                S.op("sp", lambda e, hg=hg, g=g: e.dma_start(out=h2s[g], in_=hg[:]), reads=Bhg, writes=[B_scr["h2s"]], dma_ctr=chg)
            S.flush()

        if upto < 5:
            raise StopBuild()
        with ExitStack() as ph:
            def sb(name, shape, dt):
                return ph.enter_context(nc.sbuf_tensor("s_" + name, shape, dt))

            def ps(name, shape, dt):
                return ph.enter_context(nc.psum_tensor("p_" + name, shape, dt))
            h2 = sb("h2", [128, KC, 512], BF16)
            B_h2 = Buf("h2")
            c_h2 = S.dma_ctr()
            actT = sb("actT", [128, NFC, 512], BF16)
            B_actT = Buf("actT")
            wgu = Ring([(sb(f"wg{i}", [128, KC, 256], BF16), sb(f"wu{i}", [128, KC, 256], BF16), Buf(f"wgu{i}"), S.dma_ctr())
                        for i in range(3)])
            wd = Ring([(sb(f"wd{i}", [128, 11, 512], BF16), Buf(f"wd{i}"), S.dma_ctr()) for i in range(3)])
            fbuf = sb("fbuf", [128, 4, D], F32)
            B_fbuf = [Buf(f"fbuf{i}") for i in range(4)]
            ssf = sb("ssf", [128, 4, 4], F32)
            B_ssf = Buf("ssf")
            rsf = sb("rsf", [128, 4], F32)
            B_rsf = Buf("rsf")
            sg = Ring([(sb(f"sg{i}", [128, 512], F32), Buf(f"sg{i}")) for i in range(2)])
            junk = sb("junk5", [128, 512], BF16)
            B_junk = Buf("junk5")
            x1r = Ring([(sb(f"x1r{i}", [128, D], F32), Buf(f"x1r{i}"), S.dma_ctr()) for i in range(1)])
            psG = Ring([(ps(f"psG{i}", [128, 512], F32), Buf(f"psG{i}")) for i in range(2)])
            psU = Ring([(ps(f"psU{i}", [128, 512], F32), Buf(f"psU{i}")) for i in range(2)])
            psF = [ps(f"psF{i}", [128, 512], F32) for i in range(4)]
            B_psF = [Buf(f"psF{i}") for i in range(4)]
            c_y = S.dma_ctr()
            GGf = sb("GGf", [128, D], F32)
            c_gg = S.dma_ctr()
            S.op("sp", lambda e: e.dma_start(out=GGf[:], in_=ggs[1]), reads=[B_scr["ggs"]], writes=[B_const], dma_ctr=c_gg)
            wjobs = []
            for g in range(4):
                for fb in range(NFC // 2):
                    wjobs.append(("gu", fb))
                for cbk in range(4):
                    for q4 in range(4):
                        wjobs.append(("d", cbk, q4))
            wloaded = {}

            def load_wj(n):
                if n >= len(wjobs):
                    return
                jb = wjobs[n]
                if jb[0] == "gu":
                    fb = jb[1]
                    wg, wu, Bw, cw_ = wgu.next()
                    srcg = w_gate[:, fb * 256:(fb + 1) * 256].rearrange("(kc p) n -> p kc n", p=128)
                    srcu = w_up[:, fb * 256:(fb + 1) * 256].rearrange("(kc p) n -> p kc n", p=128)
                    S.op("pool", lambda e, wg=wg, srcg=srcg: e.dma_start(out=wg[:], in_=srcg), writes=[Bw], dma_ctr=cw_)
                    S.op("pool", lambda e, wu=wu, srcu=srcu: e.dma_start(out=wu[:], in_=srcu), writes=[Bw], dma_ctr=cw_)
                    wloaded[n] = (wg, wu, Bw)
                else:
                    _, cbk, q4 = jb
                    wdt, Bwd, cwd = wd.next()
                    src = w_down[q4 * 11 * 128:(q4 + 1) * 11 * 128, cbk * 512:(cbk + 1) * 512].rearrange("(c p) n -> p c n", p=128)
                    S.op("pool", lambda e, wdt=wdt, src=src: e.dma_start(out=wdt[:], in_=src), writes=[Bwd], dma_ctr=cwd)
                    wloaded[n] = (wdt, Bwd)
            load_wj(0)
            load_wj(1)
            wn = 0
            for g in range(4):
                S.op("sp", lambda e, g=g: e.dma_start(out=h2[:], in_=h2s[g]), reads=[B_scr["h2s"]], writes=[B_h2], dma_ctr=c_h2)
                for fb in range(NFC // 2):
                    wg, wu, Bw = wloaded.pop(wn)
                    wn += 1
                    load_wj(wn + 1)
                    for j in range(2):
                        fc = fb * 2 + j
                        pG, BpG = psG.next()
                        pU, BpU = psU.next()
                        for kc in range(KC):
                            S.op("pe", lambda e, pG=pG, wg=wg, kc=kc, j=j: e.matmul(pG[:], lhsT=wg[:, kc, j * 128:(j + 1) * 128],
                                                                                    rhs=h2[:, kc, :], start=(kc == 0), stop=(kc == KC - 1)),
                                 reads=[Bw, B_h2], writes=[BpG])
                        for kc in range(KC):
                            S.op("pe", lambda e, pU=pU, wu=wu, kc=kc, j=j: e.matmul(pU[:], lhsT=wu[:, kc, j * 128:(j + 1) * 128],
                                                                                    rhs=h2[:, kc, :], start=(kc == 0), stop=(kc == KC - 1)),
                                 reads=[Bw, B_h2], writes=[BpU])
                        sgt, Bsg = sg.next()
                        S.op("act", lambda e, sgt=sgt, pG=pG: e.activation(out=sgt[:], in_=pG[:], func=AF.Silu), reads=[BpG], writes=[Bsg])
                        S.op("dve", lambda e, sgt=sgt, pU=pU, fc=fc: e.tensor_tensor(out=actT[:, fc, :], in0=sgt[:], in1=pU[:], op=ALU.mult),
                             reads=[Bsg, BpU], writes=[B_actT])
                for cbk in range(4):
                    for q4 in range(4):
                        wdt, Bwd = wloaded.pop(wn)
                        wn += 1
                        load_wj(wn + 1)
                        for c in range(11):
                            fc = q4 * 11 + c
                            for t in range(4):
                                S.op("pe", lambda e, t=t, fc=fc, c=c, wdt=wdt: e.matmul(
                                    psF[t][:], lhsT=actT[:, fc, t * 128:(t + 1) * 128], rhs=wdt[:, c, :],
                                    start=(fc == 0), stop=(fc == NFC - 1)), reads=[B_actT, Bwd], writes=[B_psF[t]])
                    for t in range(4):
                        S.op("act", lambda e, t=t, cbk=cbk: e.activation(out=junk[:], in_=psF[t][:], func=AF.Square,
                                                                         accum_out=ssf[:, t, cbk:cbk + 1]),
                             reads=[B_psF[t]], writes=[B_junk, B_ssf])
                        S.op("dve", lambda e, t=t, cbk=cbk: e.tensor_copy(out=fbuf[:, t, cbk * 512:(cbk + 1) * 512], in_=psF[t][:]),
                             reads=[B_psF[t]], writes=[B_fbuf[t]])
                S.op("dve", lambda e: e.reduce_sum(out=rsf[:], in_=ssf[:], axis=AX.X), reads=[B_ssf], writes=[B_rsf])
                S.op("act", lambda e: e.activation(out=rsf[:], in_=rsf[:], func=AF.Sqrt, bias=EPS, scale=1.0 / D),
                     reads=[B_rsf], writes=[B_rsf])
                S.op("dve", lambda e: e.reciprocal(out=rsf[:], in_=rsf[:]), reads=[B_rsf], writes=[B_rsf])
                for t in range(4):
                    r0 = g * 512 + t * 128
                    x1, Bx1, cx1 = x1r.next()
                    S.op("sp", lambda e, x1=x1, r0=r0: e.dma_start(out=x1[:], in_=x1s[r0:r0 + 128, :]), reads=[B_scr["x1s"]],
                         writes=[Bx1], dma_ctr=cx1)
                    S.op("dve", lambda e, t=t: e.scalar_tensor_tensor(out=fbuf[:, t, :], in0=fbuf[:, t, :], scalar=rsf[:, t:t + 1],
                                                                      in1=GGf[:], op0=ALU.mult, op1=ALU.mult),
                         reads=[B_fbuf[t], B_rsf, B_const], writes=[B_fbuf[t]])
                    S.op("pool", lambda e, t=t, x1=x1: e.tensor_tensor(out=fbuf[:, t, :], in0=fbuf[:, t, :], in1=x1[:], op=ALU.add),
                         reads=[B_fbuf[t], Bx1], writes=[B_fbuf[t]])
                    S.op("sp", lambda e, t=t, r0=r0: e.dma_start(out=y[r0:r0 + 128, :], in_=fbuf[:, t, :]), reads=[B_fbuf[t]],
                         writes=[B_scr["y"]], dma_ctr=c_y)
            S.op("sp", lambda e: None, reads=[B_scr["y"]], noinst=True)
            S.flush()
    except StopBuild:
        pass
    S = build.S
    sched_finish(S)
    build.n_instr = S.n_instr
    build.nsem = S.nvsem
    return nc


def _consts():
    bf = ml_dtypes.bfloat16
    ident = np.eye(128, dtype=np.float32)
    pA = np.zeros((128, 128), np.float32)
    for i in range(128):
        pA[(i + 64) % 128, i] = 1.0
    pB = np.zeros((128, 128), np.float32)
    for i in range(128):
        blk, w = divmod(i, 64)
        pB[blk * 64 + (w + 32) % 64, i] = 1.0
    k = np.arange(128)[:, None]
    q = np.arange(512)[None, :]
    cm = [(q >= (m * 128 + k)).astype(np.float32) for m in range(4)]
    q1 = np.arange(128)[None, :]
    band = [(k >= q1).astype(np.float32), (k <= q1).astype(np.float32)]
    cb16 = np.concatenate([ident, pA, pB] + cm + band, axis=1).astype(bf)
    rc = np.zeros((128, 4), np.float32)
    invA = (10000.0 ** (-(np.arange(64, dtype=np.float32)) / np.float32(64))).astype(np.float32)
    invB = (10000.0 ** (-(np.arange(32, dtype=np.float32)) / np.float32(32))).astype(np.float32)
    p = np.arange(128)
    rc[:, 0] = invA[p % 64]
    rc[:, 1] = invB[p % 32]
    rc[:, 2] = np.where(p < 64, -1.0, 1.0)
    rc[:, 3] = np.where((p % 64) < 32, -1.0, 1.0)
    return cb16, rc


def make_in_maps(x, c, positions, w_ada, b_ada, g_pre_attn, w_in, g_out_a, lambda_q1, lambda_k1, lambda_q2, lambda_k2,
                 g_subln_b, w_out, g_post_attn, g_pre_ffn, w_gate, w_up, w_down, g_post_ffn):
    f32 = np.float32
    x = np.asarray(x, f32)
    c = np.asarray(c, f32)
    positions = np.asarray(positions, np.int32)
    cb16, rc = _consts()
    w_in0 = np.asarray(w_in, f32)[0]
    perm = np.arange(1024).reshape(2, 8, 64).transpose(1, 0, 2).reshape(-1)
    w_in_p = np.concatenate([w_in0[:, 0:3072], w_in0[:, 3072:4096][:, perm], w_in0[:, 4096:5120][:, perm], w_in0[:, 5120:6144]],
                            axis=1)
    w_in_p = np.ascontiguousarray(w_in_p)
    shared = {
        "w_ada": np.ascontiguousarray(np.asarray(w_ada, f32)[0]),
        "b_ada": np.ascontiguousarray(np.asarray(b_ada, f32)[0][None, :]),
        "gpa": np.ascontiguousarray(np.asarray(g_pre_attn, f32)[0].reshape(KC, 128).T),
        "gpf": np.ascontiguousarray(np.asarray(g_pre_ffn, f32)[0].reshape(KC, 128).T),
        "gposta": np.ascontiguousarray(np.asarray(g_post_attn, f32)[0][None, :]),
        "gpostf": np.ascontiguousarray(np.asarray(g_post_ffn, f32)[0][None, :]),
        "w_in": w_in_p,
        "gcol": np.ascontiguousarray(np.stack([np.asarray(g_out_a, f32)[0], np.asarray(g_subln_b, f32)[0]], axis=1)),
        "lamv": np.ascontiguousarray(np.concatenate([np.asarray(a, f32)[0] for a in (lambda_q1, lambda_k1, lambda_q2, lambda_k2)])[None, :]),
        "w_out": np.ascontiguousarray(np.asarray(w_out, f32)[0]),
        "w_gate": np.ascontiguousarray(np.asarray(w_gate, f32)[0]),
        "w_up": np.ascontiguousarray(np.asarray(w_up, f32)[0]),
        "w_down": np.ascontiguousarray(np.asarray(w_down, f32)[0]),
        "rc": rc,
        "cb16": cb16,
    }
    in_maps = []
    for core in range(8):
        b, j = divmod(core, 4)
        chunks = [(j - s) % 4 for s in range(4)]
        xsl = np.stack([x[b, ch * NT:(ch + 1) * NT] for ch in chunks], axis=0)
        pos = np.stack([positions[b, ch * NT:(ch + 1) * NT] for ch in chunks], axis=0)[:, None, :]
        eb = np.zeros((128, 4), f32)
        for s in range(4):
            if s > j:
                eb[:, s] = NEG
        m = dict(shared)
        m["xs"] = np.ascontiguousarray(xsl)
        m["posi"] = np.ascontiguousarray(pos.astype(np.int32))
        m["ebias"] = eb
        m["cT"] = np.ascontiguousarray(c[b].reshape(KC, 128).T)
        in_maps.append(m)
    return in_maps


_NC_CACHE = {}


def kernel(**inputs):
    in_maps = make_in_maps(**inputs)
    if "nc" not in _NC_CACHE:
        _NC_CACHE["nc"] = build(debug=False)
    nc = _NC_CACHE["nc"]
    res = run_bass_kernel_spmd(nc, in_maps, core_ids=list(range(8)))
    out = np.empty((2, 4 * NT, D), np.float32)
    for core in range(8):
        b, j = divmod(core, 4)
        out[b, j * NT:(j + 1) * NT] = np.asarray(res.results[core]["y"], np.float32)
    return out
```

```python
import math
import os
from contextlib import ExitStack

import numpy as np
import ml_dtypes

import concourse.bass as bass
import concourse.mybir as mybir
from concourse.bass_utils import run_bass_kernel_spmd

F32 = mybir.dt.float32
BF16 = mybir.dt.bfloat16
I32 = mybir.dt.int32
AF = mybir.ActivationFunctionType
ALU = mybir.AluOpType
AX = mybir.AxisListType

SEM_LIMIT = 32000
SAME_ENGINE_SYNC = True


class Buf:
    __slots__ = ("name", "writers", "readers")

    def __init__(self, name=""):
        self.name = name
        self.writers = {}
        self.readers = {}


class SemCtr:
    def __init__(self, S):
        self.S = S
        self.vid = S.new_vsem()
        self.count = 0
        self.hist = {}

    def bump(self, inc):
        if self.count + inc > SEM_LIMIT:
            self.hist[self.vid] = self.count
            self.vid = self.S.new_vsem()
            self.count = 0
        self.count += inc
        return self.vid, self.count

    def current_for(self, vid):
        return self.count if vid == self.vid else self.hist[vid]


class Ev:
    __slots__ = ("eng", "fn", "deps", "sem", "val", "flag", "is_dma", "ctr", "phase", "noinst")


ENGS = ("pe", "act", "dve", "pool", "sp")


class Sched:
    def __init__(self, nc, outer):
        self.nc = nc
        self.outer = outer
        self.prog = {e: [] for e in ENGS}
        self.nvsem = 0
        self.phase = 0
        self.sems = []
        self.eng_ctr = {e: SemCtr(self) for e in ENGS}
        self.waited = {e: {} for e in ENGS}
        self.barrier = []
        self.phase_dmas = {}
        self.n_instr = 0

    def new_vsem(self):
        self.nvsem += 1
        return self.nvsem - 1

    def dma_ctr(self):
        return SemCtr(self)

    def op(self, eng, fn, reads=(), writes=(), dma_ctr=None, noinst=False, carry=False):
        ev = Ev()
        ev.noinst = noinst
        ev.eng = eng
        ev.fn = fn
        ev.is_dma = dma_ctr is not None
        ev.ctr = dma_ctr
        ev.flag = ev.is_dma
        ev.sem = None
        ev.val = None
        ev.phase = self.phase
        deps = {}
        for b in reads:
            for w in b.writers.values():
                deps[id(w)] = w
            if b.name.startswith("ps"):
                for k_, r in b.readers.items():
                    if k_ != eng:
                        deps[id(r)] = r
        for b in writes:
            for w in b.writers.values():
                deps[id(w)] = w
            for r in b.readers.values():
                deps[id(r)] = r
        dl = []
        for d in deps.values():
            if d is ev:
                continue
            if (not d.is_dma) and d.phase < self.phase:
                continue
            if (not d.is_dma) and d.eng == eng:
                if eng == "pe" or not SAME_ENGINE_SYNC:
                    continue
            if d.is_dma:
                dl.append((d, d.ctr.current_for(d.sem)))
            else:
                d.flag = True
                dl.append((d, None))
        ev.deps = dl
        if ev.is_dma:
            ev.sem, ev.val = dma_ctr.bump(16)
            if not carry:
                self.phase_dmas[ev.sem] = ev.val
        key = ("d", id(dma_ctr)) if ev.is_dma else eng
        for b in reads:
            b.readers[key] = ev
        for b in writes:
            b.writers[key] = ev
        self.prog[eng].append(ev)
        return ev

    def flush(self):
        nc = self.nc
        prog = self.prog
        new_barrier = []
        for e in ENGS:
            last = None
            for ev in prog[e]:
                if not ev.is_dma and not ev.noinst:
                    last = ev
            if last is not None:
                last.flag = True
        for e in ENGS:
            ctr = self.eng_ctr[e]
            lastev = None
            for ev in prog[e]:
                if ev.is_dma:
                    continue
                if ev.flag and not ev.noinst:
                    ev.sem, ev.val = ctr.bump(1)
                    lastev = ev
            if lastev is not None:
                new_barrier.append((lastev.sem, lastev.val))
        while len(self.sems) < self.nvsem:
            self.sems.append(self.outer.enter_context(nc.semaphore(f"s{len(self.sems)}")))
        sems = self.sems
        old_barrier = self.barrier

        def run(engname):
            def body(eng):
                waited = self.waited[engname]
                for vid, val in old_barrier:
                    if waited.get(vid, 0) < val:
                        eng.wait_ge(sems[vid], val)
                        waited[vid] = val
                for ev in prog[engname]:
                    for d, snap in ev.deps:
                        vid = d.sem
                        val = snap if d.is_dma else d.val
                        if waited.get(vid, 0) < val:
                            eng.wait_ge(sems[vid], val)
                            waited[vid] = val
                    ins = ev.fn(eng)
                    self.n_instr += 1
                    if ev.flag and not ev.noinst:
                        ins.then_inc(sems[ev.sem], 16 if ev.is_dma else 1)
            return body

        with nc.Block() as block:
            block.sync(run("sp"))
            block.scalar(run("act"))
            block.vector(run("dve"))
            block.gpsimd(run("pool"))
            block.tensor(run("pe"))
        new_barrier.extend(self.phase_dmas.items())
        self.phase_dmas = {}
        self.barrier = new_barrier
        self.prog = {e: [] for e in ENGS}
        self.phase += 1


def sched_finish(S):
    nc = S.nc
    sems = S.sems
    items = list(S.barrier)

    def body(eng):
        waited = S.waited["sp"]
        for vid, val in items:
            if waited.get(vid, 0) < val:
                eng.wait_ge(sems[vid], val)
                waited[vid] = val

    with nc.Block() as block:
        block.sync(body)


class Ring:
    def __init__(self, items):
        self.items = items
        self.i = 0

    def next(self):
        it = self.items[self.i % len(self.items)]
        self.i += 1
        return it


D = 2048
KC = 16
NT = 2048
NSLOT = 4
DFF = 5632
NFC = DFF // 128
HD = 128
SCALE_A = HD ** -0.5
SCALE_B = 64 ** -0.5
EPS = 1e-6
LAMBDA_INIT = 0.8 - 0.6 * math.exp(-0.3 * 0)
NEG = -30000.0
INV2PI = float(np.float32(1.0 / (2 * np.pi)))
MAGIC = 12582912.0
C1 = 6.28125
C2 = float(np.float32(2 * np.pi - 6.28125))
HALFPI = float(np.pi / 2)
PI_SAFE = float(np.nextafter(np.float32(np.pi), np.float32(0)))


class StopBuild(Exception):
    pass


def build(debug=False, upto=9):
    nc = bass.Bass("TRN2", target_bir_lowering=False)
    dk = "ExternalOutput" if debug else "Internal"

    def din(name, shape, dt):
        return nc.dram_tensor(name, shape, dt, kind="ExternalInput").ap()

    def dscr(name, shape, dt):
        return nc.dram_tensor(name, shape, dt, kind=dk).ap()

    xs = din("xs", [NSLOT, NT, D], F32)
    posi_d = din("posi", [NSLOT, 1, NT], I32)
    ebias_d = din("ebias", [128, 4], F32)
    cT_d = din("cT", [128, KC], F32)
    w_ada = din("w_ada", [D, 6 * D], F32)
    b_ada = din("b_ada", [1, 6 * D], F32)
    gpa_d = din("gpa", [128, KC], F32)
    gpf_d = din("gpf", [128, KC], F32)
    gposta_d = din("gposta", [1, D], F32)
    gpostf_d = din("gpostf", [1, D], F32)
    w_in = din("w_in", [D, 6144], F32)
    gcol_d = din("gcol", [128, 2], F32)
    lamv_d = din("lamv", [1, 256], F32)
    w_out = din("w_out", [D, D], F32)
    w_gate = din("w_gate", [D, DFF], F32)
    w_up = din("w_up", [D, DFF], F32)
    w_down = din("w_down", [DFF, D], F32)
    rc_d = din("rc", [128, 4], F32)
    cb16_d = din("cb16", [128, 3 * 128 + 4 * 512 + 2 * 128], BF16)
    y = nc.dram_tensor("y", [NT, D], F32, kind="ExternalOutput").ap()

    qaT = dscr("qaT", [8, 128, NT], BF16)
    kaT = dscr("kaT", [8, 128, 2 * NT], BF16)
    va = dscr("va", [2 * NT, 1024], BF16)
    qbT = dscr("qbT", [8, 128, NT], BF16)
    kbT = dscr("kbT", [8, 128, 4 * NT], BF16)
    vb = dscr("vb", [4 * NT, 1024], BF16)
    mixs = dscr("mixs", [16, 128, NT], BF16)
    x1s = dscr("x1s", [NT, D], F32)
    h2s = dscr("h2s", [4, 128, KC, 512], BF16)
    ggs = dscr("ggs", [2, 128, D], F32)

    try:
      with ExitStack() as outer:
        S = Sched(nc, outer)
        build.S = S

        def sbo(name, shape, dt):
            return outer.enter_context(nc.sbuf_tensor("s_" + name, shape, dt))

        ident_t = sbo("ident", [128, 128], BF16)
        pswA_t = sbo("pswA", [128, 128], BF16)
        pswB_t = sbo("pswB", [128, 128], BF16)
        cmask_t = sbo("cmask", [128, 4, 512], BF16)
        band_t = sbo("band", [128, 2, 128], BF16)
        ident = ident_t[:]
        pswA = pswA_t[:]
        pswB = pswB_t[:]

        def cmask(m):
            return cmask_t[:, m, :]

        def bandm(hf):
            return band_t[:, hf, :]
        ones_bf = sbo("ones_bf", [128, 128], BF16)
        ones_f = sbo("ones_f", [128, 128], F32)
        ebias = sbo("ebias", [128, 4], F32)
        rc = sbo("rc", [128, 4], F32)
        halfpi = sbo("halfpi", [128, 1], F32)
        modv = sbo("modv", [128, 4, KC], F32)
        gcol = sbo("gcol", [128, 2], F32)
        gsub08 = sbo("gsub08", [128, 1], F32)
        neglam = sbo("neglam", [128, 1], F32)
        B_const = Buf("const")
        B_scr = {k: Buf(k) for k in ["qaT", "kaT", "va", "qbT", "kbT", "vb", "mixs", "x1s", "h2s", "y", "ggs"]}

        with ExitStack() as ph:
            def sb(name, shape, dt):
                return ph.enter_context(nc.sbuf_tensor("s_" + name, shape, dt))

            def ps(name, shape, dt):
                return ph.enter_context(nc.psum_tensor("p_" + name, shape, dt))
            cT = sb("cT", [128, KC], F32)
            GGa = sb("GGa0", [128, D], F32)
            GGf = sb("GGf0", [128, D], F32)
            scT = sb("scT", [128, KC], BF16)
            brow = sb("brow", [1, 6 * D], F32)
            modrow = sb("modrow", [1, 6 * D], F32)
            gpa = sb("gpa", [128, KC], F32)
            gpf = sb("gpf", [128, KC], F32)
            lamv = sb("lamv", [128, 256], F32)
            lprod = sb("lprod", [128, 128], F32)
            lsum = sb("lsum", [128, 2], F32)
            wbl = [sb(f"wbl{i}", [128, KC, 512], BF16) for i in range(2)]
            psM = ps("psM", [1, 512], F32)
            psC = ps("psC", [128, 4, KC], F32)
            psR = [ps(f"psR{i}", [128, 512], F32) for i in range(2)]
            B_cT, B_scT, B_brow, B_modrow, B_g, B_lam, B_lp, B_ls = (Buf(n) for n in
                                                                       ["cT", "scT", "brow", "modrow", "g", "lam", "lp", "ls"])
            B_wbl = [Buf("wbl0"), Buf("wbl1")]
            B_psM, B_psC = Buf("psM"), Buf("psC")
            B_psR = [Buf("psR0"), Buf("psR1")]
            B_GGa, B_GGf = Buf("GGa"), Buf("GGf")
            c0 = S.dma_ctr()
            for (dst, src) in [(ident, cb16_d[:, 0:128]), (pswA, cb16_d[:, 128:256]), (pswB, cb16_d[:, 256:384]),
                               (cmask_t[:], cb16_d[:, 384:384 + 2048].rearrange("p (m q) -> p m q", m=4)),
                               (band_t[:], cb16_d[:, 384 + 2048:384 + 2048 + 256].rearrange("p (m q) -> p m q", m=2)),
                               (ebias[:], ebias_d), (rc[:], rc_d), (gcol[:], gcol_d)]:
                S.op("sp", lambda e, dst=dst, src=src: e.dma_start(out=dst, in_=src), writes=[B_const], dma_ctr=c0)
            c1 = S.dma_ctr()
            S.op("sp", lambda e: e.dma_start(out=cT[:], in_=cT_d), writes=[B_cT], dma_ctr=c1)
            S.op("sp", lambda e: e.dma_start(out=brow[:], in_=b_ada), writes=[B_brow], dma_ctr=c1)
            S.op("sp", lambda e: e.dma_start(out=gpa[:], in_=gpa_d), writes=[B_g], dma_ctr=c1)
            S.op("sp", lambda e: e.dma_start(out=gpf[:], in_=gpf_d), writes=[B_g], dma_ctr=c1)
            S.op("sp", lambda e: e.dma_start(out=lamv[:], in_=lamv_d.broadcast_to([128, 256])), writes=[B_lam], dma_ctr=c1)
            c2 = S.dma_ctr()
            S.op("sp", lambda e: e.dma_start(out=GGa[:], in_=gposta_d.broadcast_to([128, D])), writes=[B_GGa], dma_ctr=c2)
            S.op("sp", lambda e: e.dma_start(out=GGf[:], in_=gpostf_d.broadcast_to([128, D])), writes=[B_GGf], dma_ctr=c2)
            S.op("pool", lambda e: e.memset(ones_bf[:], 1.0), writes=[B_const])
            S.op("pool", lambda e: e.memset(ones_f[:], 1.0), writes=[B_const])
            S.op("pool", lambda e: e.memset(halfpi[:], HALFPI), writes=[B_const])
            S.op("act", lambda e: e.activation(out=scT[:], in_=cT[:], func=AF.Silu), reads=[B_cT], writes=[B_scT])
            cw = [S.dma_ctr(), S.dma_ctr()]
            def load_ada(blk):
                i = blk % 2
                src = w_ada[:, blk * 512:(blk + 1) * 512].rearrange("(kc p) n -> p kc n", p=128)
                S.op("pool", lambda e, i=i, src=src: e.dma_start(out=wbl[i][:], in_=src), writes=[B_wbl[i]], dma_ctr=cw[i])
            load_ada(0)
            for blk in range(24):
                i = blk % 2
                if blk + 1 < 24:
                    load_ada(blk + 1)
                for kc in range(KC):
                    S.op("pe", lambda e, i=i, kc=kc: e.matmul(psM[:], lhsT=scT[:, kc:kc + 1], rhs=wbl[i][:, kc, :],
                                                             start=(kc == 0), stop=(kc == KC - 1)),
                         reads=[B_scT, B_wbl[i]], writes=[B_psM])
                S.op("dve", lambda e, blk=blk: e.tensor_tensor(out=modrow[0:1, blk * 512:(blk + 1) * 512], in0=psM[:],
                                                               in1=brow[0:1, blk * 512:(blk + 1) * 512], op=ALU.add),
                     reads=[B_psM, B_brow], writes=[B_modrow])
            for vi, sec in enumerate([1, 0, 4, 3]):
                for j in range(KC):
                    o = sec * D + j * 128
                    S.op("pe", lambda e, vi=vi, j=j, o=o: e.matmul(psC[:, vi, j:j + 1], lhsT=modrow[0:1, o:o + 128],
                                                                    rhs=ones_f[0:1, 0:1], start=True, stop=True),
                         reads=[B_modrow, B_const], writes=[B_psC])
            B_modv = B_const
            S.op("dve", lambda e: e.tensor_scalar(out=modv[:, 0, :], in0=psC[:, 0, :], scalar1=1.0, scalar2=None, op0=ALU.add),
                 reads=[B_psC], writes=[B_modv])
            S.op("dve", lambda e: e.tensor_tensor(out=modv[:, 0, :], in0=modv[:, 0, :], in1=gpa[:], op=ALU.mult),
                 reads=[B_modv, B_g], writes=[B_modv])
            S.op("dve", lambda e: e.tensor_copy(out=modv[:, 1, :], in_=psC[:, 1, :]), reads=[B_psC], writes=[B_modv])
            S.op("dve", lambda e: e.tensor_scalar(out=modv[:, 2, :], in0=psC[:, 2, :], scalar1=1.0, scalar2=None, op0=ALU.add),
                 reads=[B_psC], writes=[B_modv])
            S.op("dve", lambda e: e.tensor_tensor(out=modv[:, 2, :], in0=modv[:, 2, :], in1=gpf[:], op=ALU.mult),
                 reads=[B_modv, B_g], writes=[B_modv])
            S.op("dve", lambda e: e.tensor_copy(out=modv[:, 3, :], in_=psC[:, 3, :]), reads=[B_psC], writes=[B_modv])
            k = 0
            for (GG, Bg, sec) in [(GGa, B_GGa, 2), (GGf, B_GGf, 5)]:
                for cbk in range(4):
                    o = sec * D + cbk * 512
                    pr, Bpr = psR[k % 2], B_psR[k % 2]
                    k += 1
                    S.op("pe", lambda e, pr=pr, o=o: e.matmul(pr[:], lhsT=ones_f[0:1, :], rhs=modrow[0:1, o:o + 512],
                                                              start=True, stop=True),
                         reads=[B_modrow, B_const], writes=[Bpr])
                    S.op("dve", lambda e, GG=GG, pr=pr, cbk=cbk: e.tensor_tensor(out=GG[:, cbk * 512:(cbk + 1) * 512], in0=pr[:],
                                                                                 in1=GG[:, cbk * 512:(cbk + 1) * 512], op=ALU.mult),
                         reads=[Bpr, Bg], writes=[Bg])
            c_ggw = S.dma_ctr()
            S.op("sp", lambda e: e.dma_start(out=ggs[0], in_=GGa[:]), reads=[B_GGa], writes=[B_scr["ggs"]], dma_ctr=c_ggw)
            S.op("sp", lambda e: e.dma_start(out=ggs[1], in_=GGf[:]), reads=[B_GGf], writes=[B_scr["ggs"]], dma_ctr=c_ggw)
            S.op("dve", lambda e: e.tensor_tensor(out=lprod[:, 0:64], in0=lamv[:, 0:64], in1=lamv[:, 64:128], op=ALU.mult),
                 reads=[B_lam], writes=[B_lp])
            S.op("dve", lambda e: e.tensor_tensor(out=lprod[:, 64:128], in0=lamv[:, 128:192], in1=lamv[:, 192:256], op=ALU.mult),
                 reads=[B_lam], writes=[B_lp])
            S.op("dve", lambda e: e.reduce_sum(out=lsum[:, 0:1], in_=lprod[:, 0:64], axis=AX.X), reads=[B_lp], writes=[B_ls])
            S.op("dve", lambda e: e.reduce_sum(out=lsum[:, 1:2], in_=lprod[:, 64:128], axis=AX.X), reads=[B_lp], writes=[B_ls])
            S.op("act", lambda e: e.activation(out=lsum[:], in_=lsum[:], func=AF.Exp), reads=[B_ls], writes=[B_ls])
            S.op("dve", lambda e: e.tensor_tensor(out=neglam[:], in0=lsum[:, 1:2], in1=lsum[:, 0:1], op=ALU.subtract),
                 reads=[B_ls], writes=[B_const])
            S.op("dve", lambda e: e.tensor_scalar(out=neglam[:], in0=neglam[:], scalar1=-LAMBDA_INIT, scalar2=None, op0=ALU.add),
                 reads=[B_const], writes=[B_const])
            S.op("dve", lambda e: e.tensor_scalar(out=gsub08[:], in0=gcol[:, 1:2], scalar1=1.0 - LAMBDA_INIT, scalar2=None,
                                                   op0=ALU.mult), reads=[B_const], writes=[B_const])
            S.flush()

        def norm_tiles_to_hT(get_tile, ssq, B_ssq, xn, B_xn, psT, B_psT, hT_ap_fn, B_hT, mi, evk):
            for t in range(4):
                xt, Bx = get_tile(t)
                S.op("act", lambda e, xt=xt, t=t: e.activation(out=xn[:, t, :], in_=xt, func=AF.Square, accum_out=ssq[:, t:t + 1]),
                     reads=[Bx], writes=[B_xn, B_ssq[t]])
                S.op("act", lambda e, t=t: e.activation(out=ssq[:, t:t + 1], in_=ssq[:, t:t + 1], func=AF.Sqrt, bias=EPS,
                                                        scale=1.0 / D), reads=[B_ssq[t]], writes=[B_ssq[t]])
                S.op("dve", lambda e, t=t: e.reciprocal(out=ssq[:, t:t + 1], in_=ssq[:, t:t + 1]), reads=[B_ssq[t]],
                     writes=[B_ssq[t]])
                S.op("dve", lambda e, xt=xt, t=t: e.tensor_scalar(out=xn[:, t, :], in0=xt, scalar1=ssq[:, t:t + 1], scalar2=None,
                                                                   op0=ALU.mult), reads=[Bx, B_ssq[t]], writes=[B_xn])
            for kc in range(KC):
                pt, Bpt = psT[kc % 2], B_psT[kc % 2]
                for t in range(4):
                    S.op("pe", lambda e, pt=pt, t=t, kc=kc: e.transpose(out=pt[:, t * 128:(t + 1) * 128],
                                                                        in_=xn[:, t, kc * 128:(kc + 1) * 128], identity=ident),
                         reads=[B_xn, B_const], writes=[Bpt])
                dst = hT_ap_fn(kc)
                if (kc + evk) % 2 == 0:
                    S.op("act", lambda e, dst=dst, pt=pt, kc=kc: e.activation(out=dst, in_=pt[:], func=AF.Identity,
                                                                              bias=modv[:, mi + 1, kc:kc + 1],
                                                                              scale=modv[:, mi, kc:kc + 1]),
                         reads=[Bpt, B_const], writes=[B_hT[kc]])
                else:
                    S.op("dve", lambda e, dst=dst, pt=pt, kc=kc: e.tensor_scalar(out=dst, in0=pt[:], scalar1=modv[:, mi, kc:kc + 1],
                                                                                 scalar2=modv[:, mi + 1, kc:kc + 1],
                                                                                 op0=ALU.mult, op1=ALU.add),
                         reads=[Bpt, B_const], writes=[B_hT[kc]])

        if upto < 1:
            raise StopBuild()
        with ExitStack() as ph:
            def sb(name, shape, dt):
                return ph.enter_context(nc.sbuf_tensor("s_" + name, shape, dt))

            def ps(name, shape, dt):
                return ph.enter_context(nc.psum_tensor("p_" + name, shape, dt))
            HN = NT // 2
            hTh = [sb(f"hT{i}", [128, KC, HN], BF16) for i in range(2)]
            B_hTh = [[Buf(f"hT{i}_{kc}") for kc in range(KC)] for i in range(2)]
            xtl = [sb(f"xt{i}", [128, D], F32) for i in range(2)]
            B_xtl = [Buf(f"xt{i}") for i in range(2)]
            c_xt = [S.dma_ctr() for _ in range(2)]
            xn = sb("xn", [128, 4, D], BF16)
            B_xn = Buf("xn")
            ssq = sb("ssq", [128, 4], F32)
            B_ssq = [Buf(f"ssq{i}") for i in range(4)]
            wb = [sb(f"wb{i}", [128, KC, 512], BF16) for i in range(2)]
            B_wb = [Buf("wb0"), Buf("wb1")]
            c_wb = [S.dma_ctr(), S.dma_ctr()]
            tabs = sb("tabs", [128, 4, NT], F32)
            B_tabs = Buf("tabs")
            pos_i = sb("pos_i", [128, 512], I32)
            posf = sb("posf", [128, 512], F32)
            ang = sb("ang", [128, 512], F32)
            ru = sb("ru", [128, 512], F32)
            B_posi, B_posf, B_ang, B_ru = Buf("posi"), Buf("posf"), Buf("ang"), Buf("ru")
            c_pos = S.dma_ctr()
            xq = Ring([(sb(f"xq{i}", [128, 512], BF16), Buf(f"xq{i}")) for i in range(2)])
            t1r = Ring([(sb(f"t1_{i}", [128, 512], F32), Buf(f"t1_{i}")) for i in range(2)])
            t2r = Ring([(sb(f"t2_{i}", [128, 512], F32), Buf(f"t2_{i}")) for i in range(2)])
            qst = Ring([(sb(f"qst{i}", [128, 512], BF16), Buf(f"qst{i}"), S.dma_ctr()) for i in range(3)])
            vst = Ring([(sb(f"vst{i}", [128, 4, 512], BF16), Buf(f"vst{i}"), S.dma_ctr()) for i in range(2)])
            psT = [ps(f"psT{i}", [128, 512], BF16) for i in range(2)]
            B_psT = [Buf("psT0"), Buf("psT1")]
            psQ = Ring([(ps(f"psQ{i}", [128, 512], F32), Buf(f"psQ{i}")) for i in range(2)])
            psW = Ring([(ps(f"psW{i}", [128, 512], F32), Buf(f"psW{i}")) for i in range(2)])
            psV = Ring([(ps(f"psV{i}", [128, 512], F32), Buf(f"psV{i}")) for i in range(2)])

            def blocks_for(slot):
                bl = []
                if slot == 0:
                    bl += [("q", 0, qaT, 0, 0), ("q", 512, qaT, 4, 0)]
                if slot <= 1:
                    bl += [("k", 1024, kaT, 0, 0), ("k", 1536, kaT, 4, 0)]
                if slot == 0:
                    bl += [("q", 3072, qbT, 0, 1), ("q", 3584, qbT, 4, 1)]
                bl += [("k", 4096, kbT, 0, 1), ("k", 4608, kbT, 4, 1)]
                if slot <= 1:
                    bl += [("v", 2048, va, 0, 0), ("v", 2560, va, 512, 0)]
                bl += [("v", 5120, vb, 0, 1), ("v", 5632, vb, 512, 1)]
                return bl

            DBG_SLOTS = int(os.environ.get('K_DBG_SLOTS', NSLOT))
            DBG_BLOCKS = int(os.environ.get('K_DBG_BLOCKS', 99))
            DBG_HT = int(os.environ.get('K_DBG_HT', 1))
            jobs = [(slot, bi) for slot in range(DBG_SLOTS) for half in range(2) for bi in blocks_for(slot)[:DBG_BLOCKS]]

            def load_w(n):
                if n >= len(jobs):
                    return
                i = n % 2
                col = jobs[n][1][1]
                src = w_in[:, col:col + 512].rearrange("(kc p) n -> p kc n", p=128)
                S.op("pool", lambda e, i=i, src=src: e.dma_start(out=wb[i][:], in_=src), writes=[B_wb[i]], dma_ctr=c_wb[i])
            load_w(0)
            nwb = 0
            evk = 0
            hidx = 0
            for slot in range(DBG_SLOTS):
                for g in range(4):
                    S.op("sp", lambda e, slot=slot, g=g: e.dma_start(
                        out=pos_i[:], in_=posi_d[slot, :, g * 512:(g + 1) * 512].broadcast_to([128, 512])),
                        writes=[B_posi], dma_ctr=c_pos)
                    S.op("dve", lambda e: e.tensor_copy(out=posf[:], in_=pos_i[:]), reads=[B_posi], writes=[B_posf])
                    for ts in range(2):
                        if ts == 0 and slot > 1:
                            continue
                        S.op("dve", lambda e, ts=ts: e.tensor_scalar(out=ang[:], in0=posf[:], scalar1=rc[:, ts:ts + 1], scalar2=None,
                                                                      op0=ALU.mult), reads=[B_posf, B_const], writes=[B_ang])
                        S.op("dve", lambda e: e.tensor_scalar(out=ru[:], in0=ang[:], scalar1=INV2PI, scalar2=MAGIC, op0=ALU.mult,
                                                               op1=ALU.add), reads=[B_ang], writes=[B_ru])
                        S.op("dve", lambda e: e.tensor_scalar(out=ru[:], in0=ru[:], scalar1=MAGIC, scalar2=None, op0=ALU.subtract),
                             reads=[B_ru], writes=[B_ru])
                        S.op("dve", lambda e: e.scalar_tensor_tensor(out=ang[:], in0=ru[:], scalar=-C1, in1=ang[:], op0=ALU.mult,
                                                                      op1=ALU.add), reads=[B_ru, B_ang], writes=[B_ang])
                        S.op("dve", lambda e: e.scalar_tensor_tensor(out=ang[:], in0=ru[:], scalar=-C2, in1=ang[:], op0=ALU.mult,
                                                                      op1=ALU.add), reads=[B_ru, B_ang], writes=[B_ang])
                        S.op("dve", lambda e: e.tensor_scalar(out=ang[:], in0=ang[:], scalar1=-PI_SAFE, scalar2=PI_SAFE, op0=ALU.max,
                                                               op1=ALU.min), reads=[B_ang], writes=[B_ang])
                        S.op("act", lambda e, ts=ts, g=g: e.activation(out=tabs[:, 2 * ts + 1, g * 512:(g + 1) * 512], in_=ang[:],
                                                                       func=AF.Sin, scale=rc[:, 2 + ts:3 + ts]),
                             reads=[B_ang, B_const], writes=[B_tabs])
                        S.op("act", lambda e: e.activation(out=ru[:], in_=ang[:], func=AF.Abs), reads=[B_ang], writes=[B_ru])
                        S.op("act", lambda e, ts=ts, g=g: e.activation(out=tabs[:, 2 * ts, g * 512:(g + 1) * 512], in_=ru[:],
                                                                       func=AF.Sin, bias=halfpi[:, 0:1], scale=-1.0),
                             reads=[B_ru, B_const], writes=[B_tabs])
                tokA = {0: NT, 1: 0}.get(slot, None)
                tokB = slot * NT
                for half in range(2):
                    hT = hTh[hidx % 2]
                    B_hT = B_hTh[hidx % 2]
                    hidx += 1
                    hb0 = half * HN
                    for g in range(2):
                        def get_tile(t, slot=slot, g=g, hb0=hb0):
                            r0 = hb0 + g * 512 + t * 128
                            S.op("sp", lambda e, slot=slot, r0=r0, t=t: e.dma_start(out=xtl[t % 2][:], in_=xs[slot, r0:r0 + 128, :]),
                                 writes=[B_xtl[t % 2]], dma_ctr=c_xt[t % 2])
                            return xtl[t % 2][:], B_xtl[t % 2]
                        norm_tiles_to_hT(get_tile, ssq, B_ssq, xn, B_xn, psT, B_psT,
                                         lambda kc, g=g, hT=hT: hT[:, kc, g * 512:(g + 1) * 512], B_hT, 0, evk)
                        evk += 1
                    for (kind, col, dst, h0, rs) in blocks_for(slot)[:DBG_BLOCKS]:
                        i = nwb % 2
                        assert jobs[nwb][0] == slot and jobs[nwb][1][1] == col
                        nwb += 1
                        load_w(nwb)
                        tok0 = (tokA if rs == 0 else tokB)
                        if kind == "q":
                            tok0 = 0
                        tok0 += hb0
                        Bdst = B_scr[{id(qaT): "qaT", id(kaT): "kaT", id(va): "va", id(qbT): "qbT", id(kbT): "kbT", id(vb): "vb"}[id(dst)]]
                        if kind in ("q", "k"):
                            psw = pswA if rs == 0 else pswB
                            for hh in range(4):
                                for g in range(2):
                                    tb = hb0 + g * 512
                                    pq, Bpq = psQ.next()
                                    for kc in range(KC):
                                        S.op("pe", lambda e, pq=pq, i=i, kc=kc, hh=hh, g=g, hT=hT: e.matmul(
                                            pq[:], lhsT=wb[i][:, kc, hh * 128:(hh + 1) * 128], rhs=hT[:, kc, g * 512:(g + 1) * 512],
                                            start=(kc == 0), stop=(kc == KC - 1)), reads=[B_wb[i], B_hT[kc]], writes=[Bpq])
                                    xqt, Bxq = xq.next()
                                    S.op("act", lambda e, xqt=xqt, pq=pq: e.activation(out=xqt[:], in_=pq[:], func=AF.Copy),
                                         reads=[Bpq], writes=[Bxq])
                                    pw, Bpw = psW.next()
                                    S.op("pe", lambda e, pw=pw, psw=psw, xqt=xqt: e.matmul(pw[:], lhsT=psw, rhs=xqt[:], start=True,
                                                                                           stop=True),
                                         reads=[Bxq, B_const], writes=[Bpw])
                                    t1, Bt1 = t1r.next()
                                    t2, Bt2 = t2r.next()
                                    S.op("dve", lambda e, t1=t1, pq=pq, rs=rs, tb=tb: e.tensor_tensor(
                                        out=t1[:], in0=pq[:], in1=tabs[:, 2 * rs, tb:tb + 512], op=ALU.mult),
                                        reads=[Bpq, B_tabs], writes=[Bt1])
                                    S.op("dve", lambda e, t2=t2, pw=pw, rs=rs, tb=tb: e.tensor_tensor(
                                        out=t2[:], in0=pw[:], in1=tabs[:, 2 * rs + 1, tb:tb + 512], op=ALU.mult),
                                        reads=[Bpw, B_tabs], writes=[Bt2])
                                    qs, Bqs, cqs = qst.next()
                                    S.op("pool", lambda e, qs=qs, t1=t1, t2=t2: e.tensor_tensor(out=qs[:], in0=t1[:], in1=t2[:],
                                                                                               op=ALU.add),
                                         reads=[Bt1, Bt2], writes=[Bqs])
                                    c0_ = tok0 + g * 512
                                    S.op("sp", lambda e, dst=dst, qs=qs, h=h0 + hh, c0_=c0_: e.dma_start(
                                        out=dst[h, :, c0_:c0_ + 512], in_=qs[:]), reads=[Bqs], writes=[Bdst], dma_ctr=cqs)
                        else:
                            for tq in range(2):
                                vs, Bvs, cvs = vst.next()
                                for tt in range(4):
                                    tile = tq * 4 + tt
                                    pv, Bpv = psV.next()
                                    for kc in range(KC):
                                        S.op("pe", lambda e, pv=pv, i=i, kc=kc, tile=tile, hT=hT: e.matmul(
                                            pv[:], lhsT=hT[:, kc, tile * 128:(tile + 1) * 128], rhs=wb[i][:, kc, :],
                                            start=(kc == 0), stop=(kc == KC - 1)), reads=[B_wb[i], B_hT[kc]], writes=[Bpv])
                                    if tt % 2 == 0:
                                        S.op("act", lambda e, vs=vs, pv=pv, tt=tt: e.activation(out=vs[:, tt, :], in_=pv[:], func=AF.Copy),
                                             reads=[Bpv], writes=[Bvs])
                                    else:
                                        S.op("dve", lambda e, vs=vs, pv=pv, tt=tt: e.tensor_copy(out=vs[:, tt, :], in_=pv[:]),
                                             reads=[Bpv], writes=[Bvs])
                                r0 = tok0 + tq * 512
                                S.op("sp", lambda e, dst=dst, vs=vs, r0=r0, h0=h0: e.dma_start(
                                    out=dst[r0:r0 + 512, h0:h0 + 512].rearrange("(t p) n -> p t n", p=128), in_=vs[:]),
                                    reads=[Bvs], writes=[Bdst], dma_ctr=cvs)
            S.flush()

        if upto < 2:
            raise StopBuild()
        with ExitStack() as ph:
            def sb(name, shape, dt):
                return ph.enter_context(nc.sbuf_tensor("s_" + name, shape, dt))

            def ps(name, shape, dt):
                return ph.enter_context(nc.psum_tensor("p_" + name, shape, dt))
            hb = Ring([(sb(f"qh{i}", [128, NT], BF16), sb(f"kh{i}", [128, 4 * NT], BF16), sb(f"vh{i}", [128, 64, 128], BF16),
                        Buf(f"hb{i}"), S.dma_ctr()) for i in range(2)])
            pTr = Ring([(sb(f"pT_{i}", [128, 1024], BF16), Buf(f"pT_{i}")) for i in range(3)])
            accr = Ring([(sb(f"accD{i}", [128, 1024], F32), Buf(f"accD{i}"), sb(f"accP{i}", [128, 1024], F32), Buf(f"accP{i}"))
                         for i in range(2)])
            rden = [sb(f"rden{m}", [128, 512], F32) for m in range(2)]
            B_rden = [Buf("rden0"), Buf("rden1")]
            o1 = sb("o1", [128, 512], F32)
            o2 = sb("o2", [128, 512], F32)
            obr = Ring([(sb(f"ob{i}", [128, 512], F32), Buf(f"ob{i}")) for i in range(2)])
            sqr = Ring([(sb(f"sq{i}", [128, 512], F32), Buf(f"sq{i}")) for i in range(2)])
            rstd = sb("rstd", [128, 512], F32)
            B_o1, B_o2, B_rstd = Buf("o1"), Buf("o2"), Buf("rstd")
            mst = Ring([(sb(f"mst{i}", [128, NT], BF16), Buf(f"mst{i}"), S.dma_ctr()) for i in range(2)])
            psS = Ring([(ps(f"psS{i}", [128, 1024], F32), Buf(f"psS{i}")) for i in range(2)])
            psO = [ps(f"psO{m}", [128, 512], F32) for m in range(2)]
            B_psO = [Buf("psO0"), Buf("psO1")]
            psD = Ring([(ps(f"psD{i}", [128, 512], F32), Buf(f"psD{i}")) for i in range(2)])

            def load_head_B(h):
                qh, kh, vh, Bh, ch = hb.next()
                S.op("sp", lambda e: e.dma_start(out=qh[:], in_=qbT[h]), reads=[B_scr["qbT"]], writes=[Bh], dma_ctr=ch)
                S.op("sp", lambda e: e.dma_start(out=kh[:], in_=kbT[h]), reads=[B_scr["kbT"]], writes=[Bh], dma_ctr=ch)
                S.op("sp", lambda e: e.dma_start(out=vh[:], in_=vb[:, h * 128:(h + 1) * 128].rearrange("(t p) n -> p t n", p=128)),
                     reads=[B_scr["vb"]], writes=[Bh], dma_ctr=ch)
                return qh, kh, vh, Bh

            heads = {0: load_head_B(0)}
            mstage = {}
            items = []
            for h in range(8):
                for qb in range(4):
                    pairs = []
                    for slot in range(NSLOT):
                        nk = 4 * (qb + 1) if slot == 0 else 16
                        for kc in range(nk):
                            pairs.append((slot, kc))
                    for pi, (slot, kc) in enumerate(pairs):
                        items.append((h, qb, pi, len(pairs), slot, kc))
            qk_out = {}

            def issue_qk(idx):
                h, qb, pi, npairs, slot, kc = items[idx]
                if h not in heads:
                    heads[h] = load_head_B(h)
                qh, kh, vh, Bh = heads[h]
                ktok = slot * NT + kc * 128
                pS, BpS = psS.next()
                for m in range(2):
                    S.op("pe", lambda e, pS=pS, m=m, ktok=ktok, qb=qb, kh=kh, qh=qh: e.matmul(
                        pS[:, m * 512:(m + 1) * 512], lhsT=kh[m * 64:(m + 1) * 64, ktok:ktok + 128],
                        rhs=qh[m * 64:(m + 1) * 64, qb * 512:(qb + 1) * 512], start=True, stop=True), reads=[Bh], writes=[BpS])
                qk_out[idx] = (pS, BpS)

            deferred = []

            def finalize_a(h, qb, accs):
                ms, Bms, cms = mstage[h]
                accD, BaccD, accP, BaccP = accs
                for m in range(2):
                    pD, BpD = psD.next()
                    S.op("pe", lambda e, pD=pD, m=m, accD=accD: e.matmul(pD[:], lhsT=ones_f[:], rhs=accD[:, m * 512:(m + 1) * 512],
                                                                         start=True, stop=False),
                         reads=[BaccD, B_const], writes=[BpD])
                    S.op("pe", lambda e, pD=pD, m=m, accP=accP: e.matmul(pD[:], lhsT=ones_f[:], rhs=accP[:, m * 512:(m + 1) * 512],
                                                                         start=False, stop=True),
                         reads=[BaccP, B_const], writes=[BpD])
                    S.op("dve", lambda e, pD=pD, m=m: e.reciprocal(out=rden[m][:], in_=pD[:]), reads=[BpD], writes=[B_rden[m]])
                S.op("dve", lambda e: e.tensor_tensor(out=o1[:], in0=psO[0][:], in1=rden[0][:], op=ALU.mult),
                     reads=[B_psO[0], B_rden[0]], writes=[B_o1])
                S.op("dve", lambda e: e.tensor_tensor(out=o2[:], in0=psO[1][:], in1=rden[1][:], op=ALU.mult),
                     reads=[B_psO[1], B_rden[1]], writes=[B_o2])
                ob, Bob = obr.next()
                sq, Bsq = sqr.next()
                S.op("dve", lambda e, ob=ob: e.scalar_tensor_tensor(out=ob[:], in0=o2[:], scalar=neglam[:, 0:1], in1=o1[:], op0=ALU.mult,
                                                                     op1=ALU.add), reads=[B_o1, B_o2, B_const], writes=[Bob])
                S.op("act", lambda e, ob=ob, sq=sq: e.activation(out=sq[:], in_=ob[:], func=AF.Square), reads=[Bob], writes=[Bsq])

                def part_b():
                    pD, BpD = psD.next()
                    S.op("pe", lambda e, pD=pD: e.matmul(pD[:], lhsT=ones_f[:], rhs=sq[:], start=True, stop=True),
                         reads=[Bsq, B_const], writes=[BpD])
                    S.op("act", lambda e, pD=pD: e.activation(out=rstd[:], in_=pD[:], func=AF.Sqrt, bias=EPS, scale=1.0 / HD),
                         reads=[BpD], writes=[B_rstd])
                    S.op("dve", lambda e: e.reciprocal(out=rstd[:], in_=rstd[:]), reads=[B_rstd], writes=[B_rstd])
                    S.op("dve", lambda e: e.scalar_tensor_tensor(out=ms[:, qb * 512:(qb + 1) * 512], in0=ob[:],
                                                                  scalar=gsub08[:, 0:1], in1=rstd[:], op0=ALU.mult, op1=ALU.mult),
                         reads=[Bob, B_rstd, B_const], writes=[Bms])
                    if qb == 3:
                        S.op("sp", lambda e: e.dma_start(out=mixs[8 + h], in_=ms[:]), reads=[Bms], writes=[B_scr["mixs"]], dma_ctr=cms)
                return part_b

            issue_qk(0)
            accs = None
            for idx, (h, qb, pi, npairs, slot, kc) in enumerate(items):
                if idx + 1 < len(items):
                    issue_qk(idx + 1)
                if h + 1 < 8 and h + 1 not in heads and pi == 0 and qb == 0:
                    heads[h + 1] = load_head_B(h + 1)
                if h not in mstage:
                    mstage[h] = mst.next()
                qh, kh, vh, Bh = heads[h]
                pS, BpS = qk_out.pop(idx)
                pT, BpT = pTr.next()
                kt = slot * 16 + kc
                S.op("act", lambda e, pT=pT, pS=pS, slot=slot: e.activation(out=pT[:], in_=pS[:], func=AF.Exp,
                                                                             bias=ebias[:, slot:slot + 1], scale=SCALE_B),
                     reads=[BpS, B_const], writes=[BpT])
                if slot == 0 and kc >= 4 * qb:
                    mk = cmask(kc - 4 * qb)
                    S.op("pool", lambda e, pT=pT, mk=mk: e.tensor_tensor(
                        out=pT[:].rearrange("p (m q) -> p m q", m=2), in0=pT[:].rearrange("p (m q) -> p m q", m=2),
                        in1=mk.unsqueeze(1).broadcast_to([128, 2, 512]), op=ALU.mult), reads=[BpT, B_const], writes=[BpT])
                if pi == 0:
                    accs = accr.next()
                on_pool = (pi % 3 == 2)
                acc, Bacc = (accs[2], accs[3]) if on_pool else (accs[0], accs[1])
                aeng = "pool" if on_pool else "dve"
                if pi == 0 or pi == 2:
                    S.op(aeng, lambda e, acc=acc, pT=pT: e.tensor_copy(out=acc[:], in_=pT[:]), reads=[BpT], writes=[Bacc])
                else:
                    S.op(aeng, lambda e, acc=acc, pT=pT: e.tensor_tensor(out=acc[:], in0=acc[:], in1=pT[:], op=ALU.add),
                         reads=[BpT, Bacc], writes=[Bacc])
                for m in range(2):
                    S.op("pe", lambda e, m=m, pT=pT, kt=kt, vh=vh, pi=pi, npairs=npairs: e.matmul(
                        psO[m][:], lhsT=vh[:, kt, :], rhs=pT[:, m * 512:(m + 1) * 512], start=(pi == 0), stop=(pi == npairs - 1)),
                        reads=[BpT, Bh], writes=[B_psO[m]])
                for dd in [x for x in deferred if x[0] <= idx]:
                    deferred.remove(dd)
                    dd[1]()
                if pi == npairs - 1:
                    deferred.append((idx + 4, finalize_a(h, qb, accs)))
            for dd in deferred:
                dd[1]()
            S.flush()

        if upto < 3:
            raise StopBuild()
        with ExitStack() as ph:
            def sb(name, shape, dt):
                return ph.enter_context(nc.sbuf_tensor("s_" + name, shape, dt))

            def ps(name, shape, dt):
                return ph.enter_context(nc.psum_tensor("p_" + name, shape, dt))
            DILS = (1, 4, 16)
            ha = Ring([(sb(f"qa{i}", [128, NT], BF16), sb(f"ka{i}", [128, 2 * NT], BF16),
                        [sb(f"va{i}_{d}", [128, 32, 128], BF16) for d in DILS], Buf(f"ha{i}"), S.dma_ctr()) for i in range(2)])
            numacc = sb("numacc", [128, NT], F32)
            denacc = sb("denacc", [128, NT], F32)
            B_num, B_den = Buf("numacc"), Buf("denacc")
            pAr = [Ring([(sb(f"pA{hf}_{i}", [128, 512], BF16), Buf(f"pA{hf}_{i}")) for i in range(2)]) for hf in range(2)]
            sqa = sb("sqa", [128, 512], F32)
            rsa = sb("rsa", [128, 512], F32)
            B_sqa, B_rsa = Buf("sqa"), Buf("rsa")
            msa = Ring([(sb(f"msa{i}", [128, NT], BF16), Buf(f"msa{i}"), S.dma_ctr()) for i in range(2)])
            psSa = [Ring([(ps(f"psSa{hf}_{i}", [128, 512], F32), Buf(f"psSa{hf}_{i}")) for i in range(2)]) for hf in range(2)]
            psOa = Ring([(ps(f"psOa{i}", [128, 512], F32), Buf(f"psOa{i}")) for i in range(2)])
            psDa = Ring([(ps(f"psDa{i}", [128, 512], F32), Buf(f"psDa{i}")) for i in range(2)])

            def load_head_A(h):
                qh, kh, vhs, Bh, ch = ha.next()
                S.op("sp", lambda e: e.dma_start(out=qh[:], in_=qaT[h]), reads=[B_scr["qaT"]], writes=[Bh], dma_ctr=ch)
                S.op("sp", lambda e: e.dma_start(out=kh[:], in_=kaT[h]), reads=[B_scr["kaT"]], writes=[Bh], dma_ctr=ch)
                for di, d in enumerate(DILS):
                    nb = 32 // d
                    for r in range(d):
                        src = va[:, h * 128:(h + 1) * 128].rearrange("(b p d) n -> d p b n", p=128, d=d)[r]
                        S.op("sp", lambda e, vt=vhs[di], r=r, nb=nb, src=src: e.dma_start(out=vt[:, r * nb:(r + 1) * nb, :], in_=src),
                             reads=[B_scr["va"]], writes=[Bh], dma_ctr=ch)
                return qh, kh, vhs, Bh

            cur = load_head_A(0)
            for h in range(8):
                qh, kh, vhs, Bh = cur
                if h + 1 < 8:
                    cur = load_head_A(h + 1)
                ms, Bms, cms = msa.next()
                for di, d in enumerate(DILS):
                    nb = 32 // d
                    ob0 = nb // 2
                    if d == 1:
                        batches = [[(b, 0) for b in range(b0, b0 + 4)] for b0 in range(ob0, nb, 4)]
                    else:
                        batches = []
                        for b in range(ob0, nb):
                            for r0 in range(0, d, 4):
                                batches.append([(b, r) for r in range(r0, r0 + 4)])
                    for items in batches:
                        pss = [psSa[0].next(), psSa[1].next()]
                        pas = [pAr[0].next(), pAr[1].next()]
                        for hf in range(2):
                            pS, BpS = pss[hf]
                            for it, (b, r) in enumerate(items):
                                bk = b - 1 + hf
                                kcol = bk * 128 * d + r
                                qcol = b * 128 * d + r - NT
                                S.op("pe", lambda e, pS=pS, it=it, kcol=kcol, qcol=qcol, d=d, kh=kh, qh=qh: e.matmul(
                                    pS[:, it * 128:(it + 1) * 128], lhsT=kh[:, kcol:kcol + 127 * d + 1:d],
                                    rhs=qh[:, qcol:qcol + 127 * d + 1:d], start=True, stop=True), reads=[Bh], writes=[BpS])
                            pA, BpA = pas[hf]
                            segs = []
                            for it, (b, r) in enumerate(items):
                                bk = b - 1 + hf
                                segs.append(1 if bk < ob0 else 0)
                            s0 = 0
                            while s0 < 4:
                                s1 = s0
                                while s1 < 4 and segs[s1] == segs[s0]:
                                    s1 += 1
                                S.op("act", lambda e, pA=pA, pS=pS, s0=s0, s1=s1, bs=segs[s0]: e.activation(
                                    out=pA[:, s0 * 128:s1 * 128], in_=pS[:, s0 * 128:s1 * 128], func=AF.Exp,
                                    bias=ebias[:, bs:bs + 1], scale=SCALE_A), reads=[BpS, B_const], writes=[BpA])
                                s0 = s1
                            S.op("pool", lambda e, pA=pA, hf=hf: e.tensor_tensor(
                                out=pA[:].rearrange("p (i q) -> p i q", i=4), in0=pA[:].rearrange("p (i q) -> p i q", i=4),
                                in1=bandm(hf).unsqueeze(1).broadcast_to([128, 4, 128]), op=ALU.mult),
                                reads=[BpA, B_const], writes=[BpA])
                        pO, BpO = psOa.next()
                        pD, BpD = psDa.next()
                        for it, (b, r) in enumerate(items):
                            for hf in range(2):
                                bk = b - 1 + hf
                                pA, BpA = pas[hf]
                                S.op("pe", lambda e, pO=pO, pA=pA, it=it, di=di, vi=r * nb + bk, hf=hf, vhs=vhs: e.matmul(
                                    pO[:, it * 128:(it + 1) * 128], lhsT=vhs[di][:, vi, :], rhs=pA[:, it * 128:(it + 1) * 128],
                                    start=(hf == 0), stop=(hf == 1)), reads=[BpA, Bh], writes=[BpO])
                        for it, (b, r) in enumerate(items):
                            for hf in range(2):
                                pA, BpA = pas[hf]
                                S.op("pe", lambda e, pD=pD, pA=pA, it=it, hf=hf: e.matmul(
                                    pD[:, it * 128:(it + 1) * 128], lhsT=ones_bf[:], rhs=pA[:, it * 128:(it + 1) * 128],
                                    start=(hf == 0), stop=(hf == 1)), reads=[BpA, B_const], writes=[BpD])
                        b0, r0 = items[0]
                        if d == 1:
                            c0_ = b0 * 128 - NT

                            def dstv(t):
                                return t[:, c0_:c0_ + 512].rearrange("e (i p) -> e i p", i=4)
                        else:
                            c0_ = b0 * 128 * d - NT

                            def dstv(t, c0_=c0_, d=d, r0=r0):
                                return t[:, c0_:c0_ + 128 * d].rearrange("e (p r) -> e r p", r=d)[:, r0:r0 + 4, :]
                        first = (di == 0)
                        for (accT, Bacc, pX, BpX, eng) in [(numacc, B_num, pO, BpO, "dve"), (denacc, B_den, pD, BpD, "dve")]:
                            dv = dstv(accT)
                            src = pX[:].rearrange("e (i p) -> e i p", i=4)
                            if first:
                                S.op(eng, lambda e, dv=dv, src=src: e.tensor_copy(out=dv, in_=src), reads=[BpX], writes=[Bacc])
                            else:
                                S.op(eng, lambda e, dv=dv, src=src: e.tensor_tensor(out=dv, in0=src, in1=dv, op=ALU.add),
                                     reads=[BpX, Bacc], writes=[Bacc])
                S.op("dve", lambda e: e.reciprocal(out=denacc[:], in_=denacc[:]), reads=[B_den], writes=[B_den])
                S.op("dve", lambda e: e.tensor_tensor(out=numacc[:], in0=numacc[:], in1=denacc[:], op=ALU.mult),
                     reads=[B_num, B_den], writes=[B_num])
                for qb in range(4):
                    sl = slice(qb * 512, (qb + 1) * 512)
                    S.op("act", lambda e, sl=sl: e.activation(out=sqa[:], in_=numacc[:, sl], func=AF.Square), reads=[B_num], writes=[B_sqa])
                    pD, BpD = psDa.next()
                    S.op("pe", lambda e, pD=pD: e.matmul(pD[:], lhsT=ones_f[:], rhs=sqa[:], start=True, stop=True),
                         reads=[B_sqa, B_const], writes=[BpD])
                    S.op("act", lambda e, pD=pD: e.activation(out=rsa[:], in_=pD[:], func=AF.Sqrt, bias=EPS, scale=1.0 / HD),
                         reads=[BpD], writes=[B_rsa])
                    S.op("dve", lambda e: e.reciprocal(out=rsa[:], in_=rsa[:]), reads=[B_rsa], writes=[B_rsa])
                    S.op("dve", lambda e, ms=ms, sl=sl: e.scalar_tensor_tensor(out=ms[:, sl], in0=numacc[:, sl], scalar=gcol[:, 0:1],
                                                                               in1=rsa[:], op0=ALU.mult, op1=ALU.mult),
                         reads=[B_num, B_rsa, B_const], writes=[Bms])
                S.op("sp", lambda e, ms=ms, h=h: e.dma_start(out=mixs[h], in_=ms[:]), reads=[Bms], writes=[B_scr["mixs"]], dma_ctr=cms)
            S.flush()

        if upto < 4:
            raise StopBuild()
        with ExitStack() as ph:
            def sb(name, shape, dt):
                return ph.enter_context(nc.sbuf_tensor("s_" + name, shape, dt))

            def ps(name, shape, dt):
                return ph.enter_context(nc.psum_tensor("p_" + name, shape, dt))
            wo = sb("wo", [128, KC, D], BF16)
            B_wo = Buf("wo")
            c_wo = S.dma_ctr()
            mg = Ring([(sb(f"mg{i}", [128, 16, 512], BF16), Buf(f"mg{i}"), S.dma_ctr()) for i in range(2)])
            xt4 = Ring([(sb(f"x4_{i}", [128, D], F32), Buf(f"x4_{i}"), S.dma_ctr()) for i in range(2)])
            x1t = [sb(f"x1t{i}", [128, D], F32) for i in range(4)]
            B_x1t = [Buf(f"x1t{i}") for i in range(4)]
            c_x1t = [S.dma_ctr() for _ in range(4)]
            xn = sb("xn4", [128, 4, D], BF16)
            B_xn = Buf("xn4")
            ssq = sb("ssq4", [128, 4], F32)
            B_ssq = [Buf(f"ssq4_{i}") for i in range(4)]
            ssy = sb("ssy", [128, 4], F32)
            rsy = sb("rsy", [128, 1], F32)
            B_ssy, B_rsy = Buf("ssy"), Buf("rsy")
            junk = sb("junk4", [128, 512], BF16)
            B_junk = Buf("junk4")
            GGa = sb("GGa", [128, D], F32)
            c_gg = S.dma_ctr()
            S.op("sp", lambda e: e.dma_start(out=GGa[:], in_=ggs[0]), reads=[B_scr["ggs"]], writes=[B_const], dma_ctr=c_gg)
            h2g = Ring([(sb(f"h2g{i}", [128, KC, 512], BF16), [Buf(f"h2g{i}_{kc}") for kc in range(KC)], S.dma_ctr()) for i in range(1)])
            psY = ps("psY", [128, D], F32)
            B_psY = Buf("psY")
            psT = [ps(f"psT4_{i}", [128, 512], BF16) for i in range(2)]
            B_psT = [Buf("psT4_0"), Buf("psT4_1")]
            for cbk in range(4):
                src = w_out[:, cbk * 512:(cbk + 1) * 512].rearrange("(kc p) n -> p kc n", p=128)
                S.op("pool", lambda e, cbk=cbk, src=src: e.dma_start(out=wo[:, :, cbk * 512:(cbk + 1) * 512], in_=src),
                     writes=[B_wo], dma_ctr=c_wo)
            for g in range(4):
                mgt, Bmg, cmg = mg.next()
                S.op("sp", lambda e, mgt=mgt, g=g: e.dma_start(out=mgt[:], in_=mixs[:, :, g * 512:(g + 1) * 512].rearrange("h p t -> p h t")),
                     reads=[B_scr["mixs"]], writes=[Bmg], dma_ctr=cmg)
                for t in range(4):
                    r0 = g * 512 + t * 128
                    xt, Bxt, cxt = xt4.next()
                    S.op("sp", lambda e, xt=xt, r0=r0: e.dma_start(out=xt[:], in_=xs[0, r0:r0 + 128, :]), writes=[Bxt], dma_ctr=cxt)
                    for cbk in range(4):
                        for hc in range(16):
                            S.op("pe", lambda e, mgt=mgt, t=t, cbk=cbk, hc=hc: e.matmul(
                                psY[:, cbk * 512:(cbk + 1) * 512], lhsT=mgt[:, hc, t * 128:(t + 1) * 128],
                                rhs=wo[:, hc, cbk * 512:(cbk + 1) * 512], start=(hc == 0), stop=(hc == 15)),
                                reads=[Bmg, B_wo], writes=[B_psY])
                    for cbk in range(4):
                        S.op("act", lambda e, cbk=cbk: e.activation(out=junk[:, 0:512], in_=psY[:, cbk * 512:(cbk + 1) * 512],
                                                                    func=AF.Square, accum_out=ssy[:, cbk:cbk + 1]),
                             reads=[B_psY], writes=[B_junk, B_ssy])
                    S.op("dve", lambda e: e.reduce_sum(out=rsy[:], in_=ssy[:], axis=AX.X), reads=[B_ssy], writes=[B_rsy])
                    S.op("act", lambda e: e.activation(out=rsy[:], in_=rsy[:], func=AF.Sqrt, bias=EPS, scale=1.0 / D),
                         reads=[B_rsy], writes=[B_rsy])
                    S.op("dve", lambda e: e.reciprocal(out=rsy[:], in_=rsy[:]), reads=[B_rsy], writes=[B_rsy])
                    x1, Bx1 = x1t[t], B_x1t[t]
                    for cbk in range(4):
                        sl = slice(cbk * 512, (cbk + 1) * 512)
                        S.op("dve", lambda e, x1=x1, sl=sl: e.scalar_tensor_tensor(out=x1[:, sl], in0=psY[:, sl], scalar=rsy[:, 0:1],
                                                                                   in1=GGa[:, sl], op0=ALU.mult, op1=ALU.mult),
                             reads=[B_psY, B_rsy, B_const], writes=[Bx1])
                    S.op("pool", lambda e, x1=x1, xt=xt: e.tensor_tensor(out=x1[:], in0=x1[:], in1=xt[:], op=ALU.add),
                         reads=[Bx1, Bxt], writes=[Bx1])
                    S.op("sp", lambda e, x1=x1, r0=r0: e.dma_start(out=x1s[r0:r0 + 128, :], in_=x1[:]), reads=[Bx1],
                         writes=[B_scr["x1s"]], dma_ctr=c_x1t[t])
                hg, Bhg, chg = h2g.next()
                norm_tiles_to_hT(lambda t: (x1t[t][:], B_x1t[t]), ssq, B_ssq, xn, B_xn, psT, B_psT,
                                 lambda kc, hg=hg: hg[:, kc, :], Bhg, 2, g)
                S.op("sp", lambda e, hg=hg, g=g: e.dma_start(out=h2s[g], in_=hg[:]), reads=Bhg, writes=[B_scr["h2s"]], dma_ctr=chg)
            S.flush()

        if upto < 5:
            raise StopBuild()
        with ExitStack() as ph:
            def sb(name, shape, dt):
                return ph.enter_context(nc.sbuf_tensor("s_" + name, shape, dt))

            def ps(name, shape, dt):
                return ph.enter_context(nc.psum_tensor("p_" + name, shape, dt))
            h2 = sb("h2", [128, KC, 512], BF16)
            B_h2 = Buf("h2")
            c_h2 = S.dma_ctr()
            actT = sb("actT", [128, NFC, 512], BF16)
            B_actT = Buf("actT")
            wgu = Ring([(sb(f"wg{i}", [128, KC, 256], BF16), sb(f"wu{i}", [128, KC, 256], BF16), Buf(f"wgu{i}"), S.dma_ctr())
                        for i in range(3)])
            wd = Ring([(sb(f"wd{i}", [128, 11, 512], BF16), Buf(f"wd{i}"), S.dma_ctr()) for i in range(3)])
            fbuf = sb("fbuf", [128, 4, D], F32)
            B_fbuf = [Buf(f"fbuf{i}") for i in range(4)]
            ssf = sb("ssf", [128, 4, 4], F32)
            B_ssf = Buf("ssf")
            rsf = sb("rsf", [128, 4], F32)
            B_rsf = Buf("rsf")
            sg = Ring([(sb(f"sg{i}", [128, 512], F32), Buf(f"sg{i}")) for i in range(2)])
            junk = sb("junk5", [128, 512], BF16)
            B_junk = Buf("junk5")
            x1r = Ring([(sb(f"x1r{i}", [128, D], F32), Buf(f"x1r{i}"), S.dma_ctr()) for i in range(1)])
            psG = Ring([(ps(f"psG{i}", [128, 512], F32), Buf(f"psG{i}")) for i in range(2)])
            psU = Ring([(ps(f"psU{i}", [128, 512], F32), Buf(f"psU{i}")) for i in range(2)])
            psF = [ps(f"psF{i}", [128, 512], F32) for i in range(4)]
            B_psF = [Buf(f"psF{i}") for i in range(4)]
            c_y = S.dma_ctr()
            GGf = sb("GGf", [128, D], F32)
            c_gg = S.dma_ctr()
            S.op("sp", lambda e: e.dma_start(out=GGf[:], in_=ggs[1]), reads=[B_scr["ggs"]], writes=[B_const], dma_ctr=c_gg)
            wjobs = []
            for g in range(4):
                for fb in range(NFC // 2):
                    wjobs.append(("gu", fb))
                for cbk in range(4):
                    for q4 in range(4):
                        wjobs.append(("d", cbk, q4))
            wloaded = {}

            def load_wj(n):
                if n >= len(wjobs):
                    return
                jb = wjobs[n]
                if jb[0] == "gu":
                    fb = jb[1]
                    wg, wu, Bw, cw_ = wgu.next()
                    srcg = w_gate[:, fb * 256:(fb + 1) * 256].rearrange("(kc p) n -> p kc n", p=128)
                    srcu = w_up[:, fb * 256:(fb + 1) * 256].rearrange("(kc p) n -> p kc n", p=128)
                    S.op("pool", lambda e, wg=wg, srcg=srcg: e.dma_start(out=wg[:], in_=srcg), writes=[Bw], dma_ctr=cw_)
                    S.op("pool", lambda e, wu=wu, srcu=srcu: e.dma_start(out=wu[:], in_=srcu), writes=[Bw], dma_ctr=cw_)
                    wloaded[n] = (wg, wu, Bw)
                else:
                    _, cbk, q4 = jb
                    wdt, Bwd, cwd = wd.next()
                    src = w_down[q4 * 11 * 128:(q4 + 1) * 11 * 128, cbk * 512:(cbk + 1) * 512].rearrange("(c p) n -> p c n", p=128)
                    S.op("pool", lambda e, wdt=wdt, src=src: e.dma_start(out=wdt[:], in_=src), writes=[Bwd], dma_ctr=cwd)
                    wloaded[n] = (wdt, Bwd)
            load_wj(0)
            load_wj(1)
            wn = 0
            for g in range(4):
                S.op("sp", lambda e, g=g: e.dma_start(out=h2[:], in_=h2s[g]), reads=[B_scr["h2s"]], writes=[B_h2], dma_ctr=c_h2)
                for fb in range(NFC // 2):
                    wg, wu, Bw = wloaded.pop(wn)
                    wn += 1
                    load_wj(wn + 1)
                    for j in range(2):
                        fc = fb * 2 + j
                        pG, BpG = psG.next()
                        pU, BpU = psU.next()
                        for kc in range(KC):
                            S.op("pe", lambda e, pG=pG, wg=wg, kc=kc, j=j: e.matmul(pG[:], lhsT=wg[:, kc, j * 128:(j + 1) * 128],
                                                                                    rhs=h2[:, kc, :], start=(kc == 0), stop=(kc == KC - 1)),
                                 reads=[Bw, B_h2], writes=[BpG])
                        for kc in range(KC):
                            S.op("pe", lambda e, pU=pU, wu=wu, kc=kc, j=j: e.matmul(pU[:], lhsT=wu[:, kc, j * 128:(j + 1) * 128],
                                                                                    rhs=h2[:, kc, :], start=(kc == 0), stop=(kc == KC - 1)),
                                 reads=[Bw, B_h2], writes=[BpU])
                        sgt, Bsg = sg.next()
                        S.op("act", lambda e, sgt=sgt, pG=pG: e.activation(out=sgt[:], in_=pG[:], func=AF.Silu), reads=[BpG], writes=[Bsg])
                        S.op("dve", lambda e, sgt=sgt, pU=pU, fc=fc: e.tensor_tensor(out=actT[:, fc, :], in0=sgt[:], in1=pU[:], op=ALU.mult),
                             reads=[Bsg, BpU], writes=[B_actT])
                for cbk in range(4):
                    for q4 in range(4):
                        wdt, Bwd = wloaded.pop(wn)
                        wn += 1
                        load_wj(wn + 1)
                        for c in range(11):
                            fc = q4 * 11 + c
                            for t in range(4):
                                S.op("pe", lambda e, t=t, fc=fc, c=c, wdt=wdt: e.matmul(
                                    psF[t][:], lhsT=actT[:, fc, t * 128:(t + 1) * 128], rhs=wdt[:, c, :],
                                    start=(fc == 0), stop=(fc == NFC - 1)), reads=[B_actT, Bwd], writes=[B_psF[t]])
                    for t in range(4):
                        S.op("act", lambda e, t=t, cbk=cbk: e.activation(out=junk[:], in_=psF[t][:], func=AF.Square,
                                                                         accum_out=ssf[:, t, cbk:cbk + 1]),
                             reads=[B_psF[t]], writes=[B_junk, B_ssf])
                        S.op("dve", lambda e, t=t, cbk=cbk: e.tensor_copy(out=fbuf[:, t, cbk * 512:(cbk + 1) * 512], in_=psF[t][:]),
                             reads=[B_psF[t]], writes=[B_fbuf[t]])
                S.op("dve", lambda e: e.reduce_sum(out=rsf[:], in_=ssf[:], axis=AX.X), reads=[B_ssf], writes=[B_rsf])
                S.op("act", lambda e: e.activation(out=rsf[:], in_=rsf[:], func=AF.Sqrt, bias=EPS, scale=1.0 / D),
                     reads=[B_rsf], writes=[B_rsf])
                S.op("dve", lambda e: e.reciprocal(out=rsf[:], in_=rsf[:]), reads=[B_rsf], writes=[B_rsf])
                for t in range(4):
                    r0 = g * 512 + t * 128
                    x1, Bx1, cx1 = x1r.next()
                    S.op("sp", lambda e, x1=x1, r0=r0: e.dma_start(out=x1[:], in_=x1s[r0:r0 + 128, :]), reads=[B_scr["x1s"]],
                         writes=[Bx1], dma_ctr=cx1)
                    S.op("dve", lambda e, t=t: e.scalar_tensor_tensor(out=fbuf[:, t, :], in0=fbuf[:, t, :], scalar=rsf[:, t:t + 1],
                                                                      in1=GGf[:], op0=ALU.mult, op1=ALU.mult),
                         reads=[B_fbuf[t], B_rsf, B_const], writes=[B_fbuf[t]])
                    S.op("pool", lambda e, t=t, x1=x1: e.tensor_tensor(out=fbuf[:, t, :], in0=fbuf[:, t, :], in1=x1[:], op=ALU.add),
                         reads=[B_fbuf[t], Bx1], writes=[B_fbuf[t]])
                    S.op("sp", lambda e, t=t, r0=r0: e.dma_start(out=y[r0:r0 + 128, :], in_=fbuf[:, t, :]), reads=[B_fbuf[t]],
                         writes=[B_scr["y"]], dma_ctr=c_y)
            S.op("sp", lambda e: None, reads=[B_scr["y"]], noinst=True)
            S.flush()
    except StopBuild:
        pass
    S = build.S
    sched_finish(S)
    build.n_instr = S.n_instr
    build.nsem = S.nvsem
    return nc


def _consts():
    bf = ml_dtypes.bfloat16
    ident = np.eye(128, dtype=np.float32)
    pA = np.zeros((128, 128), np.float32)
    for i in range(128):
        pA[(i + 64) % 128, i] = 1.0
    pB = np.zeros((128, 128), np.float32)
    for i in range(128):
        blk, w = divmod(i, 64)
        pB[blk * 64 + (w + 32) % 64, i] = 1.0
    k = np.arange(128)[:, None]
    q = np.arange(512)[None, :]
    cm = [(q >= (m * 128 + k)).astype(np.float32) for m in range(4)]
    q1 = np.arange(128)[None, :]
    band = [(k >= q1).astype(np.float32), (k <= q1).astype(np.float32)]
    cb16 = np.concatenate([ident, pA, pB] + cm + band, axis=1).astype(bf)
    rc = np.zeros((128, 4), np.float32)
    invA = (10000.0 ** (-(np.arange(64, dtype=np.float32)) / np.float32(64))).astype(np.float32)
    invB = (10000.0 ** (-(np.arange(32, dtype=np.float32)) / np.float32(32))).astype(np.float32)
    p = np.arange(128)
    rc[:, 0] = invA[p % 64]
    rc[:, 1] = invB[p % 32]
    rc[:, 2] = np.where(p < 64, -1.0, 1.0)
    rc[:, 3] = np.where((p % 64) < 32, -1.0, 1.0)
    return cb16, rc


def make_in_maps(x, c, positions, w_ada, b_ada, g_pre_attn, w_in, g_out_a, lambda_q1, lambda_k1, lambda_q2, lambda_k2,
                 g_subln_b, w_out, g_post_attn, g_pre_ffn, w_gate, w_up, w_down, g_post_ffn):
    f32 = np.float32
    x = np.asarray(x, f32)
    c = np.asarray(c, f32)
    positions = np.asarray(positions, np.int32)
    cb16, rc = _consts()
    w_in0 = np.asarray(w_in, f32)[0]
    perm = np.arange(1024).reshape(2, 8, 64).transpose(1, 0, 2).reshape(-1)
    w_in_p = np.concatenate([w_in0[:, 0:3072], w_in0[:, 3072:4096][:, perm], w_in0[:, 4096:5120][:, perm], w_in0[:, 5120:6144]],
                            axis=1)
    w_in_p = np.ascontiguousarray(w_in_p)
    shared = {
        "w_ada": np.ascontiguousarray(np.asarray(w_ada, f32)[0]),
        "b_ada": np.ascontiguousarray(np.asarray(b_ada, f32)[0][None, :]),
        "gpa": np.ascontiguousarray(np.asarray(g_pre_attn, f32)[0].reshape(KC, 128).T),
        "gpf": np.ascontiguousarray(np.asarray(g_pre_ffn, f32)[0].reshape(KC, 128).T),
        "gposta": np.ascontiguousarray(np.asarray(g_post_attn, f32)[0][None, :]),
        "gpostf": np.ascontiguousarray(np.asarray(g_post_ffn, f32)[0][None, :]),
        "w_in": w_in_p,
        "gcol": np.ascontiguousarray(np.stack([np.asarray(g_out_a, f32)[0], np.asarray(g_subln_b, f32)[0]], axis=1)),
        "lamv": np.ascontiguousarray(np.concatenate([np.asarray(a, f32)[0] for a in (lambda_q1, lambda_k1, lambda_q2, lambda_k2)])[None, :]),
        "w_out": np.ascontiguousarray(np.asarray(w_out, f32)[0]),
        "w_gate": np.ascontiguousarray(np.asarray(w_gate, f32)[0]),
        "w_up": np.ascontiguousarray(np.asarray(w_up, f32)[0]),
        "w_down": np.ascontiguousarray(np.asarray(w_down, f32)[0]),
        "rc": rc,
        "cb16": cb16,
    }
    in_maps = []
    for core in range(8):
        b, j = divmod(core, 4)
        chunks = [(j - s) % 4 for s in range(4)]
        xsl = np.stack([x[b, ch * NT:(ch + 1) * NT] for ch in chunks], axis=0)
        pos = np.stack([positions[b, ch * NT:(ch + 1) * NT] for ch in chunks], axis=0)[:, None, :]
        eb = np.zeros((128, 4), f32)
        for s in range(4):
            if s > j:
                eb[:, s] = NEG
        m = dict(shared)
        m["xs"] = np.ascontiguousarray(xsl)
        m["posi"] = np.ascontiguousarray(pos.astype(np.int32))
        m["ebias"] = eb
        m["cT"] = np.ascontiguousarray(c[b].reshape(KC, 128).T)
        in_maps.append(m)
    return in_maps


_NC_CACHE = {}


def kernel(**inputs):
    in_maps = make_in_maps(**inputs)
    if "nc" not in _NC_CACHE:
        _NC_CACHE["nc"] = build(debug=False)
    nc = _NC_CACHE["nc"]
    res = run_bass_kernel_spmd(nc, in_maps, core_ids=list(range(8)))
    out = np.empty((2, 4 * NT, D), np.float32)
    for core in range(8):
        b, j = divmod(core, 4)
        out[b, j * NT:(j + 1) * NT] = np.asarray(res.results[core]["y"], np.float32)
    return out
```

```python
import math
import os
from contextlib import ExitStack

import numpy as np
import ml_dtypes

import concourse.bass as bass
import concourse.mybir as mybir
from concourse.bass_utils import run_bass_kernel_spmd

F32 = mybir.dt.float32
BF16 = mybir.dt.bfloat16
I32 = mybir.dt.int32
AF = mybir.ActivationFunctionType
ALU = mybir.AluOpType
AX = mybir.AxisListType

SEM_LIMIT = 32000
SAME_ENGINE_SYNC = True


class Buf:
    __slots__ = ("name", "writers", "readers")

    def __init__(self, name=""):
        self.name = name
        self.writers = {}
        self.readers = {}


class SemCtr:
    def __init__(self, S):
        self.S = S
        self.vid = S.new_vsem()
        self.count = 0
        self.hist = {}

    def bump(self, inc):
        if self.count + inc > SEM_LIMIT:
            self.hist[self.vid] = self.count
            self.vid = self.S.new_vsem()
            self.count = 0
        self.count += inc
        return self.vid, self.count

    def current_for(self, vid):
        return self.count if vid == self.vid else self.hist[vid]


class Ev:
    __slots__ = ("eng", "fn", "deps", "sem", "val", "flag", "is_dma", "ctr", "phase", "noinst")


ENGS = ("pe", "act", "dve", "pool", "sp")


class Sched:
    def __init__(self, nc, outer):
        self.nc = nc
        self.outer = outer
        self.prog = {e: [] for e in ENGS}
        self.nvsem = 0
        self.phase = 0
        self.sems = []
        self.eng_ctr = {e: SemCtr(self) for e in ENGS}
        self.waited = {e: {} for e in ENGS}
        self.barrier = []
        self.phase_dmas = {}
        self.n_instr = 0

    def new_vsem(self):
        self.nvsem += 1
        return self.nvsem - 1

    def dma_ctr(self):
        return SemCtr(self)

    def op(self, eng, fn, reads=(), writes=(), dma_ctr=None, noinst=False, carry=False):
        ev = Ev()
        ev.noinst = noinst
        ev.eng = eng
        ev.fn = fn
        ev.is_dma = dma_ctr is not None
        ev.ctr = dma_ctr
        ev.flag = ev.is_dma
        ev.sem = None
        ev.val = None
        ev.phase = self.phase
        deps = {}
        for b in reads:
            for w in b.writers.values():
                deps[id(w)] = w
            if b.name.startswith("ps"):
                for k_, r in b.readers.items():
                    if k_ != eng:
                        deps[id(r)] = r
        for b in writes:
            for w in b.writers.values():
                deps[id(w)] = w
            for r in b.readers.values():
                deps[id(r)] = r
        dl = []
        for d in deps.values():
            if d is ev:
                continue
            if (not d.is_dma) and d.phase < self.phase:
                continue
            if (not d.is_dma) and d.eng == eng:
                if eng == "pe" or not SAME_ENGINE_SYNC:
                    continue
            if d.is_dma:
                dl.append((d, d.ctr.current_for(d.sem)))
            else:
                d.flag = True
                dl.append((d, None))
        ev.deps = dl
        if ev.is_dma:
            ev.sem, ev.val = dma_ctr.bump(16)
            if not carry:
                self.phase_dmas[ev.sem] = ev.val
        key = ("d", id(dma_ctr)) if ev.is_dma else eng
        for b in reads:
            b.readers[key] = ev
        for b in writes:
            b.writers[key] = ev
        self.prog[eng].append(ev)
        return ev

    def flush(self):
        nc = self.nc
        prog = self.prog
        new_barrier = []
        for e in ENGS:
            last = None
            for ev in prog[e]:
                if not ev.is_dma and not ev.noinst:
                    last = ev
            if last is not None:
                last.flag = True
        for e in ENGS:
            ctr = self.eng_ctr[e]
            lastev = None
            for ev in prog[e]:
                if ev.is_dma:
                    continue
                if ev.flag and not ev.noinst:
                    ev.sem, ev.val = ctr.bump(1)
                    lastev = ev
            if lastev is not None:
                new_barrier.append((lastev.sem, lastev.val))
        while len(self.sems) < self.nvsem:
            self.sems.append(self.outer.enter_context(nc.semaphore(f"s{len(self.sems)}")))
        sems = self.sems
        old_barrier = self.barrier

        def run(engname):
            def body(eng):
                waited = self.waited[engname]
                for vid, val in old_barrier:
                    if waited.get(vid, 0) < val:
                        eng.wait_ge(sems[vid], val)
                        waited[vid] = val
                for ev in prog[engname]:
                    for d, snap in ev.deps:
                        vid = d.sem
                        val = snap if d.is_dma else d.val
                        if waited.get(vid, 0) < val:
                            eng.wait_ge(sems[vid], val)
                            waited[vid] = val
                    ins = ev.fn(eng)
                    self.n_instr += 1
                    if ev.flag and not ev.noinst:
                        ins.then_inc(sems[ev.sem], 16 if ev.is_dma else 1)
            return body

        with nc.Block() as block:
            block.sync(run("sp"))
            block.scalar(run("act"))
            block.vector(run("dve"))
            block.gpsimd(run("pool"))
            block.tensor(run("pe"))
        new_barrier.extend(self.phase_dmas.items())
        self.phase_dmas = {}
        self.barrier = new_barrier
        self.prog = {e: [] for e in ENGS}
        self.phase += 1


def sched_finish(S):
    nc = S.nc
    sems = S.sems
    items = list(S.barrier)

    def body(eng):
        waited = S.waited["sp"]
        for vid, val in items:
            if waited.get(vid, 0) < val:
                eng.wait_ge(sems[vid], val)
                waited[vid] = val

    with nc.Block() as block:
        block.sync(body)


class Ring:
    def __init__(self, items):
        self.items = items
        self.i = 0

    def next(self):
        it = self.items[self.i % len(self.items)]
        self.i += 1
        return it


D = 2048
KC = 16
NT = 2048
NSLOT = 4
DFF = 5632
NFC = DFF // 128
HD = 128
SCALE_A = HD ** -0.5
SCALE_B = 64 ** -0.5
EPS = 1e-6
LAMBDA_INIT = 0.8 - 0.6 * math.exp(-0.3 * 0)
NEG = -30000.0
INV2PI = float(np.float32(1.0 / (2 * np.pi)))
MAGIC = 12582912.0
C1 = 6.28125
C2 = float(np.float32(2 * np.pi - 6.28125))
HALFPI = float(np.pi / 2)
PI_SAFE = float(np.nextafter(np.float32(np.pi), np.float32(0)))


class StopBuild(Exception):
    pass


def build(debug=False, upto=9):
    nc = bass.Bass("TRN2", target_bir_lowering=False)
    dk = "ExternalOutput" if debug else "Internal"

    def din(name, shape, dt):
        return nc.dram_tensor(name, shape, dt, kind="ExternalInput").ap()

    def dscr(name, shape, dt):
        return nc.dram_tensor(name, shape, dt, kind=dk).ap()

    xs = din("xs", [NSLOT, NT, D], F32)
    posi_d = din("posi", [NSLOT, 1, NT], I32)
    ebias_d = din("ebias", [128, 4], F32)
    cT_d = din("cT", [128, KC], F32)
    w_ada = din("w_ada", [D, 6 * D], F32)
    b_ada = din("b_ada", [1, 6 * D], F32)
    gpa_d = din("gpa", [128, KC], F32)
    gpf_d = din("gpf", [128, KC], F32)
    gposta_d = din("gposta", [1, D], F32)
    gpostf_d = din("gpostf", [1, D], F32)
    w_in = din("w_in", [D, 6144], F32)
    gcol_d = din("gcol", [128, 2], F32)
    lamv_d = din("lamv", [1, 256], F32)
    w_out = din("w_out", [D, D], F32)
    w_gate = din("w_gate", [D, DFF], F32)
    w_up = din("w_up", [D, DFF], F32)
    w_down = din("w_down", [DFF, D], F32)
    rc_d = din("rc", [128, 4], F32)
    cb16_d = din("cb16", [128, 3 * 128 + 4 * 512 + 2 * 128], BF16)
    y = nc.dram_tensor("y", [NT, D], F32, kind="ExternalOutput").ap()

    qaT = dscr("qaT", [8, 128, NT], BF16)
    kaT = dscr("kaT", [8, 128, 2 * NT], BF16)
    va = dscr("va", [2 * NT, 1024], BF16)
    qbT = dscr("qbT", [8, 128, NT], BF16)
    kbT = dscr("kbT", [8, 128, 4 * NT], BF16)
    vb = dscr("vb", [4 * NT, 1024], BF16)
    mixs = dscr("mixs", [16, 128, NT], BF16)
    x1s = dscr("x1s", [NT, D], F32)
    h2s = dscr("h2s", [4, 128, KC, 512], BF16)
    ggs = dscr("ggs", [2, 128, D], F32)

    try:
      with ExitStack() as outer:
        S = Sched(nc, outer)
        build.S = S

        def sbo(name, shape, dt):
            return outer.enter_context(nc.sbuf_tensor("s_" + name, shape, dt))

        ident_t = sbo("ident", [128, 128], BF16)
        pswA_t = sbo("pswA", [128, 128], BF16)
        pswB_t = sbo("pswB", [128, 128], BF16)
        cmask_t = sbo("cmask", [128, 4, 512], BF16)
        band_t = sbo("band", [128, 2, 128], BF16)
        ident = ident_t[:]
        pswA = pswA_t[:]
        pswB = pswB_t[:]

        def cmask(m):
            return cmask_t[:, m, :]

        def bandm(hf):
            return band_t[:, hf, :]
        ones_bf = sbo("ones_bf", [128, 128], BF16)
        ones_f = sbo("ones_f", [128, 128], F32)
        ebias = sbo("ebias", [128, 4], F32)
        rc = sbo("rc", [128, 4], F32)
        halfpi = sbo("halfpi", [128, 1], F32)
        modv = sbo("modv", [128, 4, KC], F32)
        gcol = sbo("gcol", [128, 2], F32)
        gsub08 = sbo("gsub08", [128, 1], F32)
        neglam = sbo("neglam", [128, 1], F32)
        B_const = Buf("const")
        B_scr = {k: Buf(k) for k in ["qaT", "kaT", "va", "qbT", "kbT", "vb", "mixs", "x1s", "h2s", "y", "ggs"]}

        with ExitStack() as ph:
            def sb(name, shape, dt):
                return ph.enter_context(nc.sbuf_tensor("s_" + name, shape, dt))

            def ps(name, shape, dt):
                return ph.enter_context(nc.psum_tensor("p_" + name, shape, dt))
            cT = sb("cT", [128, KC], F32)
            GGa = sb("GGa0", [128, D], F32)
            GGf = sb("GGf0", [128, D], F32)
            scT = sb("scT", [128, KC], BF16)
            brow = sb("brow", [1, 6 * D], F32)
            modrow = sb("modrow", [1, 6 * D], F32)
            gpa = sb("gpa", [128, KC], F32)
            gpf = sb("gpf", [128, KC], F32)
            lamv = sb("lamv", [128, 256], F32)
            lprod = sb("lprod", [128, 128], F32)
            lsum = sb("lsum", [128, 2], F32)
            wbl = [sb(f"wbl{i}", [128, KC, 512], BF16) for i in range(2)]
            psM = ps("psM", [1, 512], F32)
            psC = ps("psC", [128, 4, KC], F32)
            psR = [ps(f"psR{i}", [128, 512], F32) for i in range(2)]
            B_cT, B_scT, B_brow, B_modrow, B_g, B_lam, B_lp, B_ls = (Buf(n) for n in
                                                                       ["cT", "scT", "brow", "modrow", "g", "lam", "lp", "ls"])
            B_wbl = [Buf("wbl0"), Buf("wbl1")]
            B_psM, B_psC = Buf("psM"), Buf("psC")
            B_psR = [Buf("psR0"), Buf("psR1")]
            B_GGa, B_GGf = Buf("GGa"), Buf("GGf")
            c0 = S.dma_ctr()
            for (dst, src) in [(ident, cb16_d[:, 0:128]), (pswA, cb16_d[:, 128:256]), (pswB, cb16_d[:, 256:384]),
                               (cmask_t[:], cb16_d[:, 384:384 + 2048].rearrange("p (m q) -> p m q", m=4)),
                               (band_t[:], cb16_d[:, 384 + 2048:384 + 2048 + 256].rearrange("p (m q) -> p m q", m=2)),
                               (ebias[:], ebias_d), (rc[:], rc_d), (gcol[:], gcol_d)]:
                S.op("sp", lambda e, dst=dst, src=src: e.dma_start(out=dst, in_=src), writes=[B_const], dma_ctr=c0)
            c1 = S.dma_ctr()
            S.op("sp", lambda e: e.dma_start(out=cT[:], in_=cT_d), writes=[B_cT], dma_ctr=c1)
            S.op("sp", lambda e: e.dma_start(out=brow[:], in_=b_ada), writes=[B_brow], dma_ctr=c1)
            S.op("sp", lambda e: e.dma_start(out=gpa[:], in_=gpa_d), writes=[B_g], dma_ctr=c1)
            S.op("sp", lambda e: e.dma_start(out=gpf[:], in_=gpf_d), writes=[B_g], dma_ctr=c1)
            S.op("sp", lambda e: e.dma_start(out=lamv[:], in_=lamv_d.broadcast_to([128, 256])), writes=[B_lam], dma_ctr=c1)
            c2 = S.dma_ctr()
            S.op("sp", lambda e: e.dma_start(out=GGa[:], in_=gposta_d.broadcast_to([128, D])), writes=[B_GGa], dma_ctr=c2)
            S.op("sp", lambda e: e.dma_start(out=GGf[:], in_=gpostf_d.broadcast_to([128, D])), writes=[B_GGf], dma_ctr=c2)
            S.op("pool", lambda e: e.memset(ones_bf[:], 1.0), writes=[B_const])
            S.op("pool", lambda e: e.memset(ones_f[:], 1.0), writes=[B_const])
            S.op("pool", lambda e: e.memset(halfpi[:], HALFPI), writes=[B_const])
            S.op("act", lambda e: e.activation(out=scT[:], in_=cT[:], func=AF.Silu), reads=[B_cT], writes=[B_scT])
            cw = [S.dma_ctr(), S.dma_ctr()]
            def load_ada(blk):
                i = blk % 2
                src = w_ada[:, blk * 512:(blk + 1) * 512].rearrange("(kc p) n -> p kc n", p=128)
                S.op("pool", lambda e, i=i, src=src: e.dma_start(out=wbl[i][:], in_=src), writes=[B_wbl[i]], dma_ctr=cw[i])
            load_ada(0)
            for blk in range(24):
                i = blk % 2
                if blk + 1 < 24:
                    load_ada(blk + 1)
                for kc in range(KC):
                    S.op("pe", lambda e, i=i, kc=kc: e.matmul(psM[:], lhsT=scT[:, kc:kc + 1], rhs=wbl[i][:, kc, :],
                                                             start=(kc == 0), stop=(kc == KC - 1)),
                         reads=[B_scT, B_wbl[i]], writes=[B_psM])
                S.op("dve", lambda e, blk=blk: e.tensor_tensor(out=modrow[0:1, blk * 512:(blk + 1) * 512], in0=psM[:],
                                                               in1=brow[0:1, blk * 512:(blk + 1) * 512], op=ALU.add),
                     reads=[B_psM, B_brow], writes=[B_modrow])
            for vi, sec in enumerate([1, 0, 4, 3]):
                for j in range(KC):
                    o = sec * D + j * 128
                    S.op("pe", lambda e, vi=vi, j=j, o=o: e.matmul(psC[:, vi, j:j + 1], lhsT=modrow[0:1, o:o + 128],
                                                                    rhs=ones_f[0:1, 0:1], start=True, stop=True),
                         reads=[B_modrow, B_const], writes=[B_psC])
            B_modv = B_const
            S.op("dve", lambda e: e.tensor_scalar(out=modv[:, 0, :], in0=psC[:, 0, :], scalar1=1.0, scalar2=None, op0=ALU.add),
                 reads=[B_psC], writes=[B_modv])
            S.op("dve", lambda e: e.tensor_tensor(out=modv[:, 0, :], in0=modv[:, 0, :], in1=gpa[:], op=ALU.mult),
                 reads=[B_modv, B_g], writes=[B_modv])
            S.op("dve", lambda e: e.tensor_copy(out=modv[:, 1, :], in_=psC[:, 1, :]), reads=[B_psC], writes=[B_modv])
            S.op("dve", lambda e: e.tensor_scalar(out=modv[:, 2, :], in0=psC[:, 2, :], scalar1=1.0, scalar2=None, op0=ALU.add),
                 reads=[B_psC], writes=[B_modv])
            S.op("dve", lambda e: e.tensor_tensor(out=modv[:, 2, :], in0=modv[:, 2, :], in1=gpf[:], op=ALU.mult),
                 reads=[B_modv, B_g], writes=[B_modv])
            S.op("dve", lambda e: e.tensor_copy(out=modv[:, 3, :], in_=psC[:, 3, :]), reads=[B_psC], writes=[B_modv])
            k = 0
            for (GG, Bg, sec) in [(GGa, B_GGa, 2), (GGf, B_GGf, 5)]:
                for cbk in range(4):
                    o = sec * D + cbk * 512
                    pr, Bpr = psR[k % 2], B_psR[k % 2]
                    k += 1
                    S.op("pe", lambda e, pr=pr, o=o: e.matmul(pr[:], lhsT=ones_f[0:1, :], rhs=modrow[0:1, o:o + 512],
                                                              start=True, stop=True),
                         reads=[B_modrow, B_const], writes=[Bpr])
                    S.op("dve", lambda e, GG=GG, pr=pr, cbk=cbk: e.tensor_tensor(out=GG[:, cbk * 512:(cbk + 1) * 512], in0=pr[:],
                                                                                 in1=GG[:, cbk * 512:(cbk + 1) * 512], op=ALU.mult),
                         reads=[Bpr, Bg], writes=[Bg])
            c_ggw = S.dma_ctr()
            S.op("sp", lambda e: e.dma_start(out=ggs[0], in_=GGa[:]), reads=[B_GGa], writes=[B_scr["ggs"]], dma_ctr=c_ggw)
            S.op("sp", lambda e: e.dma_start(out=ggs[1], in_=GGf[:]), reads=[B_GGf], writes=[B_scr["ggs"]], dma_ctr=c_ggw)
            S.op("dve", lambda e: e.tensor_tensor(out=lprod[:, 0:64], in0=lamv[:, 0:64], in1=lamv[:, 64:128], op=ALU.mult),
                 reads=[B_lam], writes=[B_lp])
            S.op("dve", lambda e: e.tensor_tensor(out=lprod[:, 64:128], in0=lamv[:, 128:192], in1=lamv[:, 192:256], op=ALU.mult),
                 reads=[B_lam], writes=[B_lp])
            S.op("dve", lambda e: e.reduce_sum(out=lsum[:, 0:1], in_=lprod[:, 0:64], axis=AX.X), reads=[B_lp], writes=[B_ls])
            S.op("dve", lambda e: e.reduce_sum(out=lsum[:, 1:2], in_=lprod[:, 64:128], axis=AX.X), reads=[B_lp], writes=[B_ls])
            S.op("act", lambda e: e.activation(out=lsum[:], in_=lsum[:], func=AF.Exp), reads=[B_ls], writes=[B_ls])
            S.op("dve", lambda e: e.tensor_tensor(out=neglam[:], in0=lsum[:, 1:2], in1=lsum[:, 0:1], op=ALU.subtract),
                 reads=[B_ls], writes=[B_const])
            S.op("dve", lambda e: e.tensor_scalar(out=neglam[:], in0=neglam[:], scalar1=-LAMBDA_INIT, scalar2=None, op0=ALU.add),
                 reads=[B_const], writes=[B_const])
            S.op("dve", lambda e: e.tensor_scalar(out=gsub08[:], in0=gcol[:, 1:2], scalar1=1.0 - LAMBDA_INIT, scalar2=None,
                                                   op0=ALU.mult), reads=[B_const], writes=[B_const])
            S.flush()

        def norm_tiles_to_hT(get_tile, ssq, B_ssq, xn, B_xn, psT, B_psT, hT_ap_fn, B_hT, mi, evk):
            for t in range(4):
                xt, Bx = get_tile(t)
                S.op("act", lambda e, xt=xt, t=t: e.activation(out=xn[:, t, :], in_=xt, func=AF.Square, accum_out=ssq[:, t:t + 1]),
                     reads=[Bx], writes=[B_xn, B_ssq[t]])
                S.op("act", lambda e, t=t: e.activation(out=ssq[:, t:t + 1], in_=ssq[:, t:t + 1], func=AF.Sqrt, bias=EPS,
                                                        scale=1.0 / D), reads=[B_ssq[t]], writes=[B_ssq[t]])
                S.op("dve", lambda e, t=t: e.reciprocal(out=ssq[:, t:t + 1], in_=ssq[:, t:t + 1]), reads=[B_ssq[t]],
                     writes=[B_ssq[t]])
                S.op("dve", lambda e, xt=xt, t=t: e.tensor_scalar(out=xn[:, t, :], in0=xt, scalar1=ssq[:, t:t + 1], scalar2=None,
                                                                   op0=ALU.mult), reads=[Bx, B_ssq[t]], writes=[B_xn])
            for kc in range(KC):
                pt, Bpt = psT[kc % 2], B_psT[kc % 2]
                for t in range(4):
                    S.op("pe", lambda e, pt=pt, t=t, kc=kc: e.transpose(out=pt[:, t * 128:(t + 1) * 128],
                                                                        in_=xn[:, t, kc * 128:(kc + 1) * 128], identity=ident),
                         reads=[B_xn, B_const], writes=[Bpt])
                dst = hT_ap_fn(kc)
                if (kc + evk) % 2 == 0:
                    S.op("act", lambda e, dst=dst, pt=pt, kc=kc: e.activation(out=dst, in_=pt[:], func=AF.Identity,
                                                                              bias=modv[:, mi + 1, kc:kc + 1],
                                                                              scale=modv[:, mi, kc:kc + 1]),
                         reads=[Bpt, B_const], writes=[B_hT[kc]])
                else:
                    S.op("dve", lambda e, dst=dst, pt=pt, kc=kc: e.tensor_scalar(out=dst, in0=pt[:], scalar1=modv[:, mi, kc:kc + 1],
                                                                                 scalar2=modv[:, mi + 1, kc:kc + 1],
                                                                                 op0=ALU.mult, op1=ALU.add),
                         reads=[Bpt, B_const], writes=[B_hT[kc]])

        if upto < 1:
            raise StopBuild()
        with ExitStack() as ph:
            def sb(name, shape, dt):
                return ph.enter_context(nc.sbuf_tensor("s_" + name, shape, dt))

            def ps(name, shape, dt):
                return ph.enter_context(nc.psum_tensor("p_" + name, shape, dt))
            HN = NT // 2
            hTh = [sb(f"hT{i}", [128, KC, HN], BF16) for i in range(2)]
            B_hTh = [[Buf(f"hT{i}_{kc}") for kc in range(KC)] for i in range(2)]
            xtl = [sb(f"xt{i}", [128, D], F32) for i in range(2)]
            B_xtl = [Buf(f"xt{i}") for i in range(2)]
            c_xt = [S.dma_ctr() for _ in range(2)]
            xn = sb("xn", [128, 4, D], BF16)
            B_xn = Buf("xn")
            ssq = sb("ssq", [128, 4], F32)
            B_ssq = [Buf(f"ssq{i}") for i in range(4)]
            wb = [sb(f"wb{i}", [128, KC, 512], BF16) for i in range(2)]
            B_wb = [Buf("wb0"), Buf("wb1")]
            c_wb = [S.dma_ctr(), S.dma_ctr()]
            tabs = sb("tabs", [128, 4, NT], F32)
            B_tabs = Buf("tabs")
            pos_i = sb("pos_i", [128, 512], I32)
            posf = sb("posf", [128, 512], F32)
            ang = sb("ang", [128, 512], F32)
            ru = sb("ru", [128, 512], F32)
            B_posi, B_posf, B_ang, B_ru = Buf("posi"), Buf("posf"), Buf("ang"), Buf("ru")
            c_pos = S.dma_ctr()
            xq = Ring([(sb(f"xq{i}", [128, 512], BF16), Buf(f"xq{i}")) for i in range(2)])
            t1r = Ring([(sb(f"t1_{i}", [128, 512], F32), Buf(f"t1_{i}")) for i in range(2)])
            t2r = Ring([(sb(f"t2_{i}", [128, 512], F32), Buf(f"t2_{i}")) for i in range(2)])
            qst = Ring([(sb(f"qst{i}", [128, 512], BF16), Buf(f"qst{i}"), S.dma_ctr()) for i in range(3)])
            vst = Ring([(sb(f"vst{i}", [128, 4, 512], BF16), Buf(f"vst{i}"), S.dma_ctr()) for i in range(2)])
            psT = [ps(f"psT{i}", [128, 512], BF16) for i in range(2)]
            B_psT = [Buf("psT0"), Buf("psT1")]
            psQ = Ring([(ps(f"psQ{i}", [128, 512], F32), Buf(f"psQ{i}")) for i in range(2)])
            psW = Ring([(ps(f"psW{i}", [128, 512], F32), Buf(f"psW{i}")) for i in range(2)])
            psV = Ring([(ps(f"psV{i}", [128, 512], F32), Buf(f"psV{i}")) for i in range(2)])

            def blocks_for(slot):
                bl = []
                if slot == 0:
                    bl += [("q", 0, qaT, 0, 0), ("q", 512, qaT, 4, 0)]
                if slot <= 1:
                    bl += [("k", 1024, kaT, 0, 0), ("k", 1536, kaT, 4, 0)]
                if slot == 0:
                    bl += [("q", 3072, qbT, 0, 1), ("q", 3584, qbT, 4, 1)]
                bl += [("k", 4096, kbT, 0, 1), ("k", 4608, kbT, 4, 1)]
                if slot <= 1:
                    bl += [("v", 2048, va, 0, 0), ("v", 2560, va, 512, 0)]
                bl += [("v", 5120, vb, 0, 1), ("v", 5632, vb, 512, 1)]
                return bl

            DBG_SLOTS = int(os.environ.get('K_DBG_SLOTS', NSLOT))
            DBG_BLOCKS = int(os.environ.get('K_DBG_BLOCKS', 99))
            DBG_HT = int(os.environ.get('K_DBG_HT', 1))
            jobs = [(slot, bi) for slot in range(DBG_SLOTS) for half in range(2) for bi in blocks_for(slot)[:DBG_BLOCKS]]

            def load_w(n):
                if n >= len(jobs):
                    return
                i = n % 2
                col = jobs[n][1][1]
                src = w_in[:, col:col + 512].rearrange("(kc p) n -> p kc n", p=128)
                S.op("pool", lambda e, i=i, src=src: e.dma_start(out=wb[i][:], in_=src), writes=[B_wb[i]], dma_ctr=c_wb[i])
            load_w(0)
            nwb = 0
            hidx = 0

            pend = []

            def post_qk(pq, Bpq, xqt, Bxq, tb, psw, rs, dst, Bdst, h, c0_):
                pw, Bpw = psW.next()
                S.op("pe", lambda e: e.matmul(pw[:], lhsT=psw, rhs=xqt[:], start=True, stop=True),
                     reads=[Bxq, B_const], writes=[Bpw])
                t1, Bt1 = t1r.next()
                t2, Bt2 = t2r.next()
                S.op("dve", lambda e: e.tensor_tensor(out=t1[:], in0=pq[:], in1=tabs[:, 2 * rs, tb:tb + 512], op=ALU.mult),
                     reads=[Bpq, B_tabs], writes=[Bt1])
                S.op("dve", lambda e: e.tensor_tensor(out=t2[:], in0=pw[:], in1=tabs[:, 2 * rs + 1, tb:tb + 512], op=ALU.mult),
                     reads=[Bpw, B_tabs], writes=[Bt2])
                qs, Bqs, cqs = qst.next()
                S.op("pool", lambda e: e.tensor_tensor(out=qs[:], in0=t1[:], in1=t2[:], op=ALU.add),
                     reads=[Bt1, Bt2], writes=[Bqs])
                S.op("sp", lambda e: e.dma_start(out=dst[h, :, c0_:c0_ + 512], in_=qs[:]), reads=[Bqs], writes=[Bdst], dma_ctr=cqs)

            def emit_norm(hi, g):
                slot_, half_ = divmod(hi, 2)
                hT_ = hTh[hi % 2]

                def get_tile(t):
                    r0 = half_ * HN + g * 512 + t * 128
                    S.op("sp", lambda e, r0=r0, t=t: e.dma_start(out=xtl[t % 2][:], in_=xs[slot_, r0:r0 + 128, :]),
                         writes=[B_xtl[t % 2]], dma_ctr=c_xt[t % 2])
                    return xtl[t % 2][:], B_xtl[t % 2]
                norm_tiles_to_hT(get_tile, ssq, B_ssq, xn, B_xn, psT, B_psT,
                                 lambda kc: hT_[:, kc, g * 512:(g + 1) * 512], B_hTh[hi % 2], 0, hi * 2 + g)
            for slot in range(DBG_SLOTS):
                for g in range(4):
                    S.op("sp", lambda e, slot=slot, g=g: e.dma_start(
                        out=pos_i[:], in_=posi_d[slot, :, g * 512:(g + 1) * 512].broadcast_to([128, 512])),
                        writes=[B_posi], dma_ctr=c_pos)
                    S.op("dve", lambda e: e.tensor_copy(out=posf[:], in_=pos_i[:]), reads=[B_posi], writes=[B_posf])
                    for ts in range(2):
                        if ts == 0 and slot > 1:
                            continue
                        S.op("dve", lambda e, ts=ts: e.tensor_scalar(out=ang[:], in0=posf[:], scalar1=rc[:, ts:ts + 1], scalar2=None,
                                                                      op0=ALU.mult), reads=[B_posf, B_const], writes=[B_ang])
                        S.op("dve", lambda e: e.tensor_scalar(out=ru[:], in0=ang[:], scalar1=INV2PI, scalar2=MAGIC, op0=ALU.mult,
                                                               op1=ALU.add), reads=[B_ang], writes=[B_ru])
                        S.op("dve", lambda e: e.tensor_scalar(out=ru[:], in0=ru[:], scalar1=MAGIC, scalar2=None, op0=ALU.subtract),
                             reads=[B_ru], writes=[B_ru])
                        S.op("dve", lambda e: e.scalar_tensor_tensor(out=ang[:], in0=ru[:], scalar=-C1, in1=ang[:], op0=ALU.mult,
                                                                      op1=ALU.add), reads=[B_ru, B_ang], writes=[B_ang])
                        S.op("dve", lambda e: e.scalar_tensor_tensor(out=ang[:], in0=ru[:], scalar=-C2, in1=ang[:], op0=ALU.mult,
                                                                      op1=ALU.add), reads=[B_ru, B_ang], writes=[B_ang])
                        S.op("dve", lambda e: e.tensor_scalar(out=ang[:], in0=ang[:], scalar1=-PI_SAFE, scalar2=PI_SAFE, op0=ALU.max,
                                                               op1=ALU.min), reads=[B_ang], writes=[B_ang])
                        S.op("act", lambda e, ts=ts, g=g: e.activation(out=tabs[:, 2 * ts + 1, g * 512:(g + 1) * 512], in_=ang[:],
                                                                       func=AF.Sin, scale=rc[:, 2 + ts:3 + ts]),
                             reads=[B_ang, B_const], writes=[B_tabs])
                        S.op("act", lambda e: e.activation(out=ru[:], in_=ang[:], func=AF.Abs), reads=[B_ang], writes=[B_ru])
                        S.op("act", lambda e, ts=ts, g=g: e.activation(out=tabs[:, 2 * ts, g * 512:(g + 1) * 512], in_=ru[:],
                                                                       func=AF.Sin, bias=halfpi[:, 0:1], scale=-1.0),
                             reads=[B_ru, B_const], writes=[B_tabs])
                tokA = {0: NT, 1: 0}.get(slot, None)
                tokB = slot * NT
                for half in range(2):
                    hT = hTh[hidx % 2]
                    B_hT = B_hTh[hidx % 2]
                    hb0 = half * HN
                    if hidx == 0:
                        emit_norm(0, 0)
                        emit_norm(0, 1)
                    hidx += 1
                    for bidx, (kind, col, dst, h0, rs) in enumerate(blocks_for(slot)[:DBG_BLOCKS]):
                        if bidx in (1, 2) and hidx < 2 * DBG_SLOTS:
                            emit_norm(hidx, bidx - 1)
                        i = nwb % 2
                        assert jobs[nwb][0] == slot and jobs[nwb][1][1] == col
                        nwb += 1
                        load_w(nwb)
                        tok0 = (tokA if rs == 0 else tokB)
                        if kind == "q":
                            tok0 = 0
                        tok0 += hb0
                        Bdst = B_scr[{id(qaT): "qaT", id(kaT): "kaT", id(va): "va", id(qbT): "qbT", id(kbT): "kbT", id(vb): "vb"}[id(dst)]]
                        if kind in ("q", "k"):
                            psw = pswA if rs == 0 else pswB
                            for hh in range(4):
                                for g in range(2):
                                    tb = hb0 + g * 512
                                    pq, Bpq = psQ.next()
                                    for kc in range(KC):
                                        S.op("pe", lambda e, pq=pq, i=i, kc=kc, hh=hh, g=g, hT=hT: e.matmul(
                                            pq[:], lhsT=wb[i][:, kc, hh * 128:(hh + 1) * 128], rhs=hT[:, kc, g * 512:(g + 1) * 512],
                                            start=(kc == 0), stop=(kc == KC - 1)), reads=[B_wb[i], B_hT[kc]], writes=[Bpq])
                                    xqt, Bxq = xq.next()
                                    S.op("act", lambda e, xqt=xqt, pq=pq: e.activation(out=xqt[:], in_=pq[:], func=AF.Copy),
                                         reads=[Bpq], writes=[Bxq])
                                    if pend:
                                        pend.pop()()
                                    pend.append(lambda pq=pq, Bpq=Bpq, xqt=xqt, Bxq=Bxq, tb=tb, hh=hh, g=g: post_qk(
                                        pq, Bpq, xqt, Bxq, tb, psw, rs, dst, Bdst, h0 + hh, tok0 + g * 512))
                            if pend:
                                pend.pop()()
                            continue
                            if True:
                                if True:
                                    pw, Bpw = psW.next()
                                    S.op("pe", lambda e, pw=pw, psw=psw, xqt=xqt: e.matmul(pw[:], lhsT=psw, rhs=xqt[:], start=True,
                                                                                           stop=True),
                                         reads=[Bxq, B_const], writes=[Bpw])
                                    t1, Bt1 = t1r.next()
                                    t2, Bt2 = t2r.next()
                                    S.op("dve", lambda e, t1=t1, pq=pq, rs=rs, tb=tb: e.tensor_tensor(
                                        out=t1[:], in0=pq[:], in1=tabs[:, 2 * rs, tb:tb + 512], op=ALU.mult),
                                        reads=[Bpq, B_tabs], writes=[Bt1])
                                    S.op("dve", lambda e, t2=t2, pw=pw, rs=rs, tb=tb: e.tensor_tensor(
                                        out=t2[:], in0=pw[:], in1=tabs[:, 2 * rs + 1, tb:tb + 512], op=ALU.mult),
                                        reads=[Bpw, B_tabs], writes=[Bt2])
                                    qs, Bqs, cqs = qst.next()
                                    S.op("pool", lambda e, qs=qs, t1=t1, t2=t2: e.tensor_tensor(out=qs[:], in0=t1[:], in1=t2[:],
                                                                                               op=ALU.add),
                                         reads=[Bt1, Bt2], writes=[Bqs])
                                    c0_ = tok0 + g * 512
                                    S.op("sp", lambda e, dst=dst, qs=qs, h=h0 + hh, c0_=c0_: e.dma_start(
                                        out=dst[h, :, c0_:c0_ + 512], in_=qs[:]), reads=[Bqs], writes=[Bdst], dma_ctr=cqs)
                        else:
                            for tq in range(2):
                                vs, Bvs, cvs = vst.next()
                                for tt in range(4):
                                    tile = tq * 4 + tt
                                    pv, Bpv = psV.next()
                                    for kc in range(KC):
                                        S.op("pe", lambda e, pv=pv, i=i, kc=kc, tile=tile, hT=hT: e.matmul(
                                            pv[:], lhsT=hT[:, kc, tile * 128:(tile + 1) * 128], rhs=wb[i][:, kc, :],
                                            start=(kc == 0), stop=(kc == KC - 1)), reads=[B_wb[i], B_hT[kc]], writes=[Bpv])
                                    if tt % 2 == 0:
                                        S.op("act", lambda e, vs=vs, pv=pv, tt=tt: e.activation(out=vs[:, tt, :], in_=pv[:], func=AF.Copy),
                                             reads=[Bpv], writes=[Bvs])
                                    else:
                                        S.op("dve", lambda e, vs=vs, pv=pv, tt=tt: e.tensor_copy(out=vs[:, tt, :], in_=pv[:]),
                                             reads=[Bpv], writes=[Bvs])
                                r0 = tok0 + tq * 512
                                S.op("sp", lambda e, dst=dst, vs=vs, r0=r0, h0=h0: e.dma_start(
                                    out=dst[r0:r0 + 512, h0:h0 + 512].rearrange("(t p) n -> p t n", p=128), in_=vs[:]),
                                    reads=[Bvs], writes=[Bdst], dma_ctr=cvs)
            S.flush()

        if upto < 2:
            raise StopBuild()
        with ExitStack() as ph:
            def sb(name, shape, dt):
                return ph.enter_context(nc.sbuf_tensor("s_" + name, shape, dt))

            def ps(name, shape, dt):
                return ph.enter_context(nc.psum_tensor("p_" + name, shape, dt))
            hb = Ring([(sb(f"qh{i}", [128, NT], BF16), sb(f"kh{i}", [128, 4 * NT], BF16), sb(f"vh{i}", [128, 64, 128], BF16),
                        Buf(f"hb{i}"), S.dma_ctr()) for i in range(2)])
            pTr = Ring([(sb(f"pT_{i}", [128, 1024], BF16), Buf(f"pT_{i}")) for i in range(3)])
            accr = Ring([(sb(f"accD{i}", [128, 1024], F32), Buf(f"accD{i}"), sb(f"accP{i}", [128, 1024], F32), Buf(f"accP{i}"))
                         for i in range(2)])
            rden = [sb(f"rden{m}", [128, 512], F32) for m in range(2)]
            B_rden = [Buf("rden0"), Buf("rden1")]
            o1 = sb("o1", [128, 512], F32)
            o2 = sb("o2", [128, 512], F32)
            obr = Ring([(sb(f"ob{i}", [128, 512], F32), Buf(f"ob{i}")) for i in range(2)])
            sqr = Ring([(sb(f"sq{i}", [128, 512], F32), Buf(f"sq{i}")) for i in range(2)])
            rstd = sb("rstd", [128, 512], F32)
            B_o1, B_o2, B_rstd = Buf("o1"), Buf("o2"), Buf("rstd")
            mst = Ring([(sb(f"mst{i}", [128, NT], BF16), Buf(f"mst{i}"), S.dma_ctr()) for i in range(2)])
            psS = Ring([(ps(f"psS{i}", [128, 1024], F32), Buf(f"psS{i}")) for i in range(2)])
            psO = [ps(f"psO{m}", [128, 512], F32) for m in range(2)]
            B_psO = [Buf("psO0"), Buf("psO1")]
            psD = Ring([(ps(f"psD{i}", [128, 512], F32), Buf(f"psD{i}")) for i in range(2)])

            def load_head_B(h):
                qh, kh, vh, Bh, ch = hb.next()
                S.op("sp", lambda e: e.dma_start(out=qh[:], in_=qbT[h]), reads=[B_scr["qbT"]], writes=[Bh], dma_ctr=ch)
                S.op("sp", lambda e: e.dma_start(out=kh[:], in_=kbT[h]), reads=[B_scr["kbT"]], writes=[Bh], dma_ctr=ch)
                S.op("sp", lambda e: e.dma_start(out=vh[:], in_=vb[:, h * 128:(h + 1) * 128].rearrange("(t p) n -> p t n", p=128)),
                     reads=[B_scr["vb"]], writes=[Bh], dma_ctr=ch)
                return qh, kh, vh, Bh

            heads = {0: load_head_B(0)}
            mstage = {}
            items = []
            for h in range(8):
                for qb in range(4):
                    pairs = []
                    for slot in range(NSLOT):
                        nk = 4 * (qb + 1) if slot == 0 else 16
                        for kc in range(nk):
                            pairs.append((slot, kc))
                    for pi, (slot, kc) in enumerate(pairs):
                        items.append((h, qb, pi, len(pairs), slot, kc))
            qk_out = {}

            def issue_qk(idx):
                h, qb, pi, npairs, slot, kc = items[idx]
                if h not in heads:
                    heads[h] = load_head_B(h)
                qh, kh, vh, Bh = heads[h]
                ktok = slot * NT + kc * 128
                pS, BpS = psS.next()
                for m in range(2):
                    S.op("pe", lambda e, pS=pS, m=m, ktok=ktok, qb=qb, kh=kh, qh=qh: e.matmul(
                        pS[:, m * 512:(m + 1) * 512], lhsT=kh[m * 64:(m + 1) * 64, ktok:ktok + 128],
                        rhs=qh[m * 64:(m + 1) * 64, qb * 512:(qb + 1) * 512], start=True, stop=True), reads=[Bh], writes=[BpS])
                qk_out[idx] = (pS, BpS)

            deferred = []

            def finalize_a(h, qb, accs):
                ms, Bms, cms = mstage[h]
                accD, BaccD, accP, BaccP = accs
                for m in range(2):
                    pD, BpD = psD.next()
                    S.op("pe", lambda e, pD=pD, m=m, accD=accD: e.matmul(pD[:], lhsT=ones_f[:], rhs=accD[:, m * 512:(m + 1) * 512],
                                                                         start=True, stop=False),
                         reads=[BaccD, B_const], writes=[BpD])
                    S.op("pe", lambda e, pD=pD, m=m, accP=accP: e.matmul(pD[:], lhsT=ones_f[:], rhs=accP[:, m * 512:(m + 1) * 512],
                                                                         start=False, stop=True),
                         reads=[BaccP, B_const], writes=[BpD])
                    S.op("dve", lambda e, pD=pD, m=m: e.reciprocal(out=rden[m][:], in_=pD[:]), reads=[BpD], writes=[B_rden[m]])
                S.op("dve", lambda e: e.tensor_tensor(out=o1[:], in0=psO[0][:], in1=rden[0][:], op=ALU.mult),
                     reads=[B_psO[0], B_rden[0]], writes=[B_o1])
                S.op("dve", lambda e: e.tensor_tensor(out=o2[:], in0=psO[1][:], in1=rden[1][:], op=ALU.mult),
                     reads=[B_psO[1], B_rden[1]], writes=[B_o2])
                ob, Bob = obr.next()
                sq, Bsq = sqr.next()
                S.op("dve", lambda e, ob=ob: e.scalar_tensor_tensor(out=ob[:], in0=o2[:], scalar=neglam[:, 0:1], in1=o1[:], op0=ALU.mult,
                                                                     op1=ALU.add), reads=[B_o1, B_o2, B_const], writes=[Bob])
                S.op("act", lambda e, ob=ob, sq=sq: e.activation(out=sq[:], in_=ob[:], func=AF.Square), reads=[Bob], writes=[Bsq])

                def part_b():
                    pD, BpD = psD.next()
                    S.op("pe", lambda e, pD=pD: e.matmul(pD[:], lhsT=ones_f[:], rhs=sq[:], start=True, stop=True),
                         reads=[Bsq, B_const], writes=[BpD])
                    S.op("act", lambda e, pD=pD: e.activation(out=rstd[:], in_=pD[:], func=AF.Sqrt, bias=EPS, scale=1.0 / HD),
                         reads=[BpD], writes=[B_rstd])
                    S.op("dve", lambda e: e.reciprocal(out=rstd[:], in_=rstd[:]), reads=[B_rstd], writes=[B_rstd])
                    S.op("dve", lambda e: e.scalar_tensor_tensor(out=ms[:, qb * 512:(qb + 1) * 512], in0=ob[:],
                                                                  scalar=gsub08[:, 0:1], in1=rstd[:], op0=ALU.mult, op1=ALU.mult),
                         reads=[Bob, B_rstd, B_const], writes=[Bms])
                    if qb == 3:
                        S.op("sp", lambda e: e.dma_start(out=mixs[8 + h], in_=ms[:]), reads=[Bms], writes=[B_scr["mixs"]], dma_ctr=cms)
                return part_b

            issue_qk(0)
            accs = None
            for idx, (h, qb, pi, npairs, slot, kc) in enumerate(items):
                if idx + 1 < len(items):
                    issue_qk(idx + 1)
                if h + 1 < 8 and h + 1 not in heads and pi == 0 and qb == 0:
                    heads[h + 1] = load_head_B(h + 1)
                if h not in mstage:
                    mstage[h] = mst.next()
                qh, kh, vh, Bh = heads[h]
                pS, BpS = qk_out.pop(idx)
                pT, BpT = pTr.next()
                kt = slot * 16 + kc
                S.op("act", lambda e, pT=pT, pS=pS, slot=slot: e.activation(out=pT[:], in_=pS[:], func=AF.Exp,
                                                                             bias=ebias[:, slot:slot + 1], scale=SCALE_B),
                     reads=[BpS, B_const], writes=[BpT])
                if slot == 0 and kc >= 4 * qb:
                    mk = cmask(kc - 4 * qb)
                    S.op("pool", lambda e, pT=pT, mk=mk: e.tensor_tensor(
                        out=pT[:].rearrange("p (m q) -> p m q", m=2), in0=pT[:].rearrange("p (m q) -> p m q", m=2),
                        in1=mk.unsqueeze(1).broadcast_to([128, 2, 512]), op=ALU.mult), reads=[BpT, B_const], writes=[BpT])
                if pi == 0:
                    accs = accr.next()
                on_pool = (pi % 3 == 2)
                acc, Bacc = (accs[2], accs[3]) if on_pool else (accs[0], accs[1])
                aeng = "pool" if on_pool else "dve"
                if pi == 0 or pi == 2:
                    S.op(aeng, lambda e, acc=acc, pT=pT: e.tensor_copy(out=acc[:], in_=pT[:]), reads=[BpT], writes=[Bacc])
                else:
                    S.op(aeng, lambda e, acc=acc, pT=pT: e.tensor_tensor(out=acc[:], in0=acc[:], in1=pT[:], op=ALU.add),
                         reads=[BpT, Bacc], writes=[Bacc])
                for m in range(2):
                    S.op("pe", lambda e, m=m, pT=pT, kt=kt, vh=vh, pi=pi, npairs=npairs: e.matmul(
                        psO[m][:], lhsT=vh[:, kt, :], rhs=pT[:, m * 512:(m + 1) * 512], start=(pi == 0), stop=(pi == npairs - 1)),
                        reads=[BpT, Bh], writes=[B_psO[m]])
                for dd in [x for x in deferred if x[0] <= idx]:
                    deferred.remove(dd)
                    dd[1]()
                if pi == npairs - 1:
                    deferred.append((idx + 4, finalize_a(h, qb, accs)))
            for dd in deferred:
                dd[1]()
            S.flush()

        if upto < 3:
            raise StopBuild()
        with ExitStack() as ph:
            def sb(name, shape, dt):
                return ph.enter_context(nc.sbuf_tensor("s_" + name, shape, dt))

            def ps(name, shape, dt):
                return ph.enter_context(nc.psum_tensor("p_" + name, shape, dt))
            DILS = (1, 4, 16)
            ha = Ring([(sb(f"qa{i}", [128, NT], BF16), sb(f"ka{i}", [128, 2 * NT], BF16),
                        [sb(f"va{i}_{d}", [128, 32, 128], BF16) for d in DILS], Buf(f"ha{i}"), S.dma_ctr()) for i in range(2)])
            numacc = sb("numacc", [128, NT], F32)
            denacc = sb("denacc", [128, NT], F32)
            B_num, B_den = Buf("numacc"), Buf("denacc")
            pAr = [Ring([(sb(f"pA{hf}_{i}", [128, 512], BF16), Buf(f"pA{hf}_{i}")) for i in range(2)]) for hf in range(2)]
            sqa = sb("sqa", [128, 512], F32)
            rsa = sb("rsa", [128, 512], F32)
            B_sqa, B_rsa = Buf("sqa"), Buf("rsa")
            msa = Ring([(sb(f"msa{i}", [128, NT], BF16), Buf(f"msa{i}"), S.dma_ctr()) for i in range(2)])
            psSa = [Ring([(ps(f"psSa{hf}_{i}", [128, 512], F32), Buf(f"psSa{hf}_{i}")) for i in range(2)]) for hf in range(2)]
            psOa = Ring([(ps(f"psOa{i}", [128, 512], F32), Buf(f"psOa{i}")) for i in range(2)])
            psDa = Ring([(ps(f"psDa{i}", [128, 512], F32), Buf(f"psDa{i}")) for i in range(2)])

            def load_head_A(h):
                qh, kh, vhs, Bh, ch = ha.next()
                S.op("sp", lambda e: e.dma_start(out=qh[:], in_=qaT[h]), reads=[B_scr["qaT"]], writes=[Bh], dma_ctr=ch)
                S.op("sp", lambda e: e.dma_start(out=kh[:], in_=kaT[h]), reads=[B_scr["kaT"]], writes=[Bh], dma_ctr=ch)
                for di, d in enumerate(DILS):
                    nb = 32 // d
                    for r in range(d):
                        src = va[:, h * 128:(h + 1) * 128].rearrange("(b p d) n -> d p b n", p=128, d=d)[r]
                        S.op("sp", lambda e, vt=vhs[di], r=r, nb=nb, src=src: e.dma_start(out=vt[:, r * nb:(r + 1) * nb, :], in_=src),
                             reads=[B_scr["va"]], writes=[Bh], dma_ctr=ch)
                return qh, kh, vhs, Bh

            cur = load_head_A(0)
            for h in range(8):
                qh, kh, vhs, Bh = cur
                if h + 1 < 8:
                    cur = load_head_A(h + 1)
                ms, Bms, cms = msa.next()
                for di, d in enumerate(DILS):
                    nb = 32 // d
                    ob0 = nb // 2
                    if d == 1:
                        batches = [[(b, 0) for b in range(b0, b0 + 4)] for b0 in range(ob0, nb, 4)]
                    else:
                        batches = []
                        for b in range(ob0, nb):
                            for r0 in range(0, d, 4):
                                batches.append([(b, r) for r in range(r0, r0 + 4)])
                    for items in batches:
                        pss = [psSa[0].next(), psSa[1].next()]
                        pas = [pAr[0].next(), pAr[1].next()]
                        for hf in range(2):
                            pS, BpS = pss[hf]
                            for it, (b, r) in enumerate(items):
                                bk = b - 1 + hf
                                kcol = bk * 128 * d + r
                                qcol = b * 128 * d + r - NT
                                S.op("pe", lambda e, pS=pS, it=it, kcol=kcol, qcol=qcol, d=d, kh=kh, qh=qh: e.matmul(
                                    pS[:, it * 128:(it + 1) * 128], lhsT=kh[:, kcol:kcol + 127 * d + 1:d],
                                    rhs=qh[:, qcol:qcol + 127 * d + 1:d], start=True, stop=True), reads=[Bh], writes=[BpS])
                            pA, BpA = pas[hf]
                            segs = []
                            for it, (b, r) in enumerate(items):
                                bk = b - 1 + hf
                                segs.append(1 if bk < ob0 else 0)
                            s0 = 0
                            while s0 < 4:
                                s1 = s0
                                while s1 < 4 and segs[s1] == segs[s0]:
                                    s1 += 1
                                S.op("act", lambda e, pA=pA, pS=pS, s0=s0, s1=s1, bs=segs[s0]: e.activation(
                                    out=pA[:, s0 * 128:s1 * 128], in_=pS[:, s0 * 128:s1 * 128], func=AF.Exp,
                                    bias=ebias[:, bs:bs + 1], scale=SCALE_A), reads=[BpS, B_const], writes=[BpA])
                                s0 = s1
                            S.op("pool", lambda e, pA=pA, hf=hf: e.tensor_tensor(
                                out=pA[:].rearrange("p (i q) -> p i q", i=4), in0=pA[:].rearrange("p (i q) -> p i q", i=4),
                                in1=bandm(hf).unsqueeze(1).broadcast_to([128, 4, 128]), op=ALU.mult),
                                reads=[BpA, B_const], writes=[BpA])
                        pO, BpO = psOa.next()
                        pD, BpD = psDa.next()
                        for it, (b, r) in enumerate(items):
                            for hf in range(2):
                                bk = b - 1 + hf
                                pA, BpA = pas[hf]
                                S.op("pe", lambda e, pO=pO, pA=pA, it=it, di=di, vi=r * nb + bk, hf=hf, vhs=vhs: e.matmul(
                                    pO[:, it * 128:(it + 1) * 128], lhsT=vhs[di][:, vi, :], rhs=pA[:, it * 128:(it + 1) * 128],
                                    start=(hf == 0), stop=(hf == 1)), reads=[BpA, Bh], writes=[BpO])
                        for it, (b, r) in enumerate(items):
                            for hf in range(2):
                                pA, BpA = pas[hf]
                                S.op("pe", lambda e, pD=pD, pA=pA, it=it, hf=hf: e.matmul(
                                    pD[:, it * 128:(it + 1) * 128], lhsT=ones_bf[:], rhs=pA[:, it * 128:(it + 1) * 128],
                                    start=(hf == 0), stop=(hf == 1)), reads=[BpA, B_const], writes=[BpD])
                        b0, r0 = items[0]
                        if d == 1:
                            c0_ = b0 * 128 - NT

                            def dstv(t):
                                return t[:, c0_:c0_ + 512].rearrange("e (i p) -> e i p", i=4)
                        else:
                            c0_ = b0 * 128 * d - NT

                            def dstv(t, c0_=c0_, d=d, r0=r0):
                                return t[:, c0_:c0_ + 128 * d].rearrange("e (p r) -> e r p", r=d)[:, r0:r0 + 4, :]
                        first = (di == 0)
                        for (accT, Bacc, pX, BpX, eng) in [(numacc, B_num, pO, BpO, "dve"), (denacc, B_den, pD, BpD, "dve")]:
                            dv = dstv(accT)
                            src = pX[:].rearrange("e (i p) -> e i p", i=4)
                            if first:
                                S.op(eng, lambda e, dv=dv, src=src: e.tensor_copy(out=dv, in_=src), reads=[BpX], writes=[Bacc])
                            else:
                                S.op(eng, lambda e, dv=dv, src=src: e.tensor_tensor(out=dv, in0=src, in1=dv, op=ALU.add),
                                     reads=[BpX, Bacc], writes=[Bacc])
                S.op("dve", lambda e: e.reciprocal(out=denacc[:], in_=denacc[:]), reads=[B_den], writes=[B_den])
                S.op("dve", lambda e: e.tensor_tensor(out=numacc[:], in0=numacc[:], in1=denacc[:], op=ALU.mult),
                     reads=[B_num, B_den], writes=[B_num])
                for qb in range(4):
                    sl = slice(qb * 512, (qb + 1) * 512)
                    S.op("act", lambda e, sl=sl: e.activation(out=sqa[:], in_=numacc[:, sl], func=AF.Square), reads=[B_num], writes=[B_sqa])
                    pD, BpD = psDa.next()
                    S.op("pe", lambda e, pD=pD: e.matmul(pD[:], lhsT=ones_f[:], rhs=sqa[:], start=True, stop=True),
                         reads=[B_sqa, B_const], writes=[BpD])
                    S.op("act", lambda e, pD=pD: e.activation(out=rsa[:], in_=pD[:], func=AF.Sqrt, bias=EPS, scale=1.0 / HD),
                         reads=[BpD], writes=[B_rsa])
                    S.op("dve", lambda e: e.reciprocal(out=rsa[:], in_=rsa[:]), reads=[B_rsa], writes=[B_rsa])
                    S.op("dve", lambda e, ms=ms, sl=sl: e.scalar_tensor_tensor(out=ms[:, sl], in0=numacc[:, sl], scalar=gcol[:, 0:1],
                                                                               in1=rsa[:], op0=ALU.mult, op1=ALU.mult),
                         reads=[B_num, B_rsa, B_const], writes=[Bms])
                S.op("sp", lambda e, ms=ms, h=h: e.dma_start(out=mixs[h], in_=ms[:]), reads=[Bms], writes=[B_scr["mixs"]], dma_ctr=cms)
            S.flush()

        if upto < 4:
            raise StopBuild()
        with ExitStack() as ph:
            def sb(name, shape, dt):
                return ph.enter_context(nc.sbuf_tensor("s_" + name, shape, dt))

            def ps(name, shape, dt):
                return ph.enter_context(nc.psum_tensor("p_" + name, shape, dt))
            wo = sb("wo", [128, KC, D], BF16)
            B_wo = Buf("wo")
            c_wo = S.dma_ctr()
            mg = Ring([(sb(f"mg{i}", [128, 16, 512], BF16), Buf(f"mg{i}"), S.dma_ctr()) for i in range(2)])
            xt4 = Ring([(sb(f"x4_{i}", [128, D], F32), Buf(f"x4_{i}"), S.dma_ctr()) for i in range(2)])
            x1t = [sb(f"x1t{i}", [128, D], F32) for i in range(4)]
            B_x1t = [Buf(f"x1t{i}") for i in range(4)]
            c_x1t = [S.dma_ctr() for _ in range(4)]
            xn = sb("xn4", [128, 4, D], BF16)
            B_xn = Buf("xn4")
            ssq = sb("ssq4", [128, 4], F32)
            B_ssq = [Buf(f"ssq4_{i}") for i in range(4)]
            ssy = sb("ssy", [128, 4], F32)
            rsy = sb("rsy", [128, 1], F32)
            B_ssy, B_rsy = Buf("ssy"), Buf("rsy")
            junk = sb("junk4", [128, 512], BF16)
            B_junk = Buf("junk4")
            GGa = sb("GGa", [128, D], F32)
            c_gg = S.dma_ctr()
            S.op("sp", lambda e: e.dma_start(out=GGa[:], in_=ggs[0]), reads=[B_scr["ggs"]], writes=[B_const], dma_ctr=c_gg)
            h2g = Ring([(sb(f"h2g{i}", [128, KC, 512], BF16), [Buf(f"h2g{i}_{kc}") for kc in range(KC)], S.dma_ctr()) for i in range(1)])
            psY = ps("psY", [128, D], F32)
            B_psY = Buf("psY")
            psT = [ps(f"psT4_{i}", [128, 512], BF16) for i in range(2)]
            B_psT = [Buf("psT4_0"), Buf("psT4_1")]
            for cbk in range(4):
                src = w_out[:, cbk * 512:(cbk + 1) * 512].rearrange("(kc p) n -> p kc n", p=128)
                S.op("pool", lambda e, cbk=cbk, src=src: e.dma_start(out=wo[:, :, cbk * 512:(cbk + 1) * 512], in_=src),
                     writes=[B_wo], dma_ctr=c_wo)
            for g in range(4):
                mgt, Bmg, cmg = mg.next()
                S.op("sp", lambda e, mgt=mgt, g=g: e.dma_start(out=mgt[:], in_=mixs[:, :, g * 512:(g + 1) * 512].rearrange("h p t -> p h t")),
                     reads=[B_scr["mixs"]], writes=[Bmg], dma_ctr=cmg)
                for t in range(4):
                    r0 = g * 512 + t * 128
                    xt, Bxt, cxt = xt4.next()
                    S.op("sp", lambda e, xt=xt, r0=r0: e.dma_start(out=xt[:], in_=xs[0, r0:r0 + 128, :]), writes=[Bxt], dma_ctr=cxt)
                    for cbk in range(4):
                        for hc in range(16):
                            S.op("pe", lambda e, mgt=mgt, t=t, cbk=cbk, hc=hc: e.matmul(
                                psY[:, cbk * 512:(cbk + 1) * 512], lhsT=mgt[:, hc, t * 128:(t + 1) * 128],
                                rhs=wo[:, hc, cbk * 512:(cbk + 1) * 512], start=(hc == 0), stop=(hc == 15)),
                                reads=[Bmg, B_wo], writes=[B_psY])
                    for cbk in range(4):
                        S.op("act", lambda e, cbk=cbk: e.activation(out=junk[:, 0:512], in_=psY[:, cbk * 512:(cbk + 1) * 512],
                                                                    func=AF.Square, accum_out=ssy[:, cbk:cbk + 1]),
                             reads=[B_psY], writes=[B_junk, B_ssy])
                    S.op("dve", lambda e: e.reduce_sum(out=rsy[:], in_=ssy[:], axis=AX.X), reads=[B_ssy], writes=[B_rsy])
                    S.op("act", lambda e: e.activation(out=rsy[:], in_=rsy[:], func=AF.Sqrt, bias=EPS, scale=1.0 / D),
                         reads=[B_rsy], writes=[B_rsy])
                    S.op("dve", lambda e: e.reciprocal(out=rsy[:], in_=rsy[:]), reads=[B_rsy], writes=[B_rsy])
                    x1, Bx1 = x1t[t], B_x1t[t]
                    for cbk in range(4):
                        sl = slice(cbk * 512, (cbk + 1) * 512)
                        S.op("dve", lambda e, x1=x1, sl=sl: e.scalar_tensor_tensor(out=x1[:, sl], in0=psY[:, sl], scalar=rsy[:, 0:1],
                                                                                   in1=GGa[:, sl], op0=ALU.mult, op1=ALU.mult),
                             reads=[B_psY, B_rsy, B_const], writes=[Bx1])
                    S.op("pool", lambda e, x1=x1, xt=xt: e.tensor_tensor(out=x1[:], in0=x1[:], in1=xt[:], op=ALU.add),
                         reads=[Bx1, Bxt], writes=[Bx1])
                    S.op("sp", lambda e, x1=x1, r0=r0: e.dma_start(out=x1s[r0:r0 + 128, :], in_=x1[:]), reads=[Bx1],
                         writes=[B_scr["x1s"]], dma_ctr=c_x1t[t])
                hg, Bhg, chg = h2g.next()
                norm_tiles_to_hT(lambda t: (x1t[t][:], B_x1t[t]), ssq, B_ssq, xn, B_xn, psT, B_psT,
                                 lambda kc, hg=hg: hg[:, kc, :], Bhg, 2, g)
                S.op("sp", lambda e, hg=hg, g=g: e.dma_start(out=h2s[g], in_=hg[:]), reads=Bhg, writes=[B_scr["h2s"]], dma_ctr=chg)
            S.flush()

        if upto < 5:
            raise StopBuild()
        with ExitStack() as ph:
            def sb(name, shape, dt):
                return ph.enter_context(nc.sbuf_tensor("s_" + name, shape, dt))

            def ps(name, shape, dt):
                return ph.enter_context(nc.psum_tensor("p_" + name, shape, dt))
            h2 = sb("h2", [128, KC, 512], BF16)
            B_h2 = Buf("h2")
            c_h2 = S.dma_ctr()
            actT = sb("actT", [128, NFC, 512], BF16)
            B_actT = Buf("actT")
            wgu = Ring([(sb(f"wg{i}", [128, KC, 256], BF16), sb(f"wu{i}", [128, KC, 256], BF16), Buf(f"wgu{i}"), S.dma_ctr())
                        for i in range(3)])
            wd = Ring([(sb(f"wd{i}", [128, 11, 512], BF16), Buf(f"wd{i}"), S.dma_ctr()) for i in range(3)])
            fbuf = sb("fbuf", [128, 4, D], F32)
            B_fbuf = [Buf(f"fbuf{i}") for i in range(4)]
            ssf = sb("ssf", [128, 4, 4], F32)
            B_ssf = Buf("ssf")
            rsf = sb("rsf", [128, 4], F32)
            B_rsf = Buf("rsf")
            sg = Ring([(sb(f"sg{i}", [128, 512], F32), Buf(f"sg{i}")) for i in range(2)])
            junk = sb("junk5", [128, 512], BF16)
            B_junk = Buf("junk5")
            x1r = Ring([(sb(f"x1r{i}", [128, D], F32), Buf(f"x1r{i}"), S.dma_ctr()) for i in range(1)])
            psG = Ring([(ps(f"psG{i}", [128, 512], F32), Buf(f"psG{i}")) for i in range(2)])
            psU = Ring([(ps(f"psU{i}", [128, 512], F32), Buf(f"psU{i}")) for i in range(2)])
            psF = [ps(f"psF{i}", [128, 512], F32) for i in range(4)]
            B_psF = [Buf(f"psF{i}") for i in range(4)]
            c_y = S.dma_ctr()
            GGf = sb("GGf", [128, D], F32)
            c_gg = S.dma_ctr()
            S.op("sp", lambda e: e.dma_start(out=GGf[:], in_=ggs[1]), reads=[B_scr["ggs"]], writes=[B_const], dma_ctr=c_gg)
            wjobs = []
            for g in range(4):
                for fb in range(NFC // 2):
                    wjobs.append(("gu", fb))
                for cbk in range(4):
                    for q4 in range(4):
                        wjobs.append(("d", cbk, q4))
            wloaded = {}

            def load_wj(n):
                if n >= len(wjobs):
                    return
                jb = wjobs[n]
                if jb[0] == "gu":
                    fb = jb[1]
                    wg, wu, Bw, cw_ = wgu.next()
                    srcg = w_gate[:, fb * 256:(fb + 1) * 256].rearrange("(kc p) n -> p kc n", p=128)
                    srcu = w_up[:, fb * 256:(fb + 1) * 256].rearrange("(kc p) n -> p kc n", p=128)
                    S.op("pool", lambda e, wg=wg, srcg=srcg: e.dma_start(out=wg[:], in_=srcg), writes=[Bw], dma_ctr=cw_)
                    S.op("pool", lambda e, wu=wu, srcu=srcu: e.dma_start(out=wu[:], in_=srcu), writes=[Bw], dma_ctr=cw_)
                    wloaded[n] = (wg, wu, Bw)
                else:
                    _, cbk, q4 = jb
                    wdt, Bwd, cwd = wd.next()
                    src = w_down[q4 * 11 * 128:(q4 + 1) * 11 * 128, cbk * 512:(cbk + 1) * 512].rearrange("(c p) n -> p c n", p=128)
                    S.op("pool", lambda e, wdt=wdt, src=src: e.dma_start(out=wdt[:], in_=src), writes=[Bwd], dma_ctr=cwd)
                    wloaded[n] = (wdt, Bwd)
            load_wj(0)
            load_wj(1)
            wn = 0
            for g in range(4):
                S.op("sp", lambda e, g=g: e.dma_start(out=h2[:], in_=h2s[g]), reads=[B_scr["h2s"]], writes=[B_h2], dma_ctr=c_h2)
                for fb in range(NFC // 2):
                    wg, wu, Bw = wloaded.pop(wn)
                    wn += 1
                    load_wj(wn + 1)
                    for j in range(2):
                        fc = fb * 2 + j
                        pG, BpG = psG.next()
                        pU, BpU = psU.next()
                        for kc in range(KC):
                            S.op("pe", lambda e, pG=pG, wg=wg, kc=kc, j=j: e.matmul(pG[:], lhsT=wg[:, kc, j * 128:(j + 1) * 128],
                                                                                    rhs=h2[:, kc, :], start=(kc == 0), stop=(kc == KC - 1)),
                                 reads=[Bw, B_h2], writes=[BpG])
                        for kc in range(KC):
                            S.op("pe", lambda e, pU=pU, wu=wu, kc=kc, j=j: e.matmul(pU[:], lhsT=wu[:, kc, j * 128:(j + 1) * 128],
                                                                                    rhs=h2[:, kc, :], start=(kc == 0), stop=(kc == KC - 1)),
                                 reads=[Bw, B_h2], writes=[BpU])
                        sgt, Bsg = sg.next()
                        S.op("act", lambda e, sgt=sgt, pG=pG: e.activation(out=sgt[:], in_=pG[:], func=AF.Silu), reads=[BpG], writes=[Bsg])
                        S.op("dve", lambda e, sgt=sgt, pU=pU, fc=fc: e.tensor_tensor(out=actT[:, fc, :], in0=sgt[:], in1=pU[:], op=ALU.mult),
                             reads=[Bsg, BpU], writes=[B_actT])
                for cbk in range(4):
                    for q4 in range(4):
                        wdt, Bwd = wloaded.pop(wn)
                        wn += 1
                        load_wj(wn + 1)
                        for c in range(11):
                            fc = q4 * 11 + c
                            for t in range(4):
                                S.op("pe", lambda e, t=t, fc=fc, c=c, wdt=wdt: e.matmul(
                                    psF[t][:], lhsT=actT[:, fc, t * 128:(t + 1) * 128], rhs=wdt[:, c, :],
                                    start=(fc == 0), stop=(fc == NFC - 1)), reads=[B_actT, Bwd], writes=[B_psF[t]])
                    for t in range(4):
                        S.op("act", lambda e, t=t, cbk=cbk: e.activation(out=junk[:], in_=psF[t][:], func=AF.Square,
                                                                         accum_out=ssf[:, t, cbk:cbk + 1]),
                             reads=[B_psF[t]], writes=[B_junk, B_ssf])
                        S.op("dve", lambda e, t=t, cbk=cbk: e.tensor_copy(out=fbuf[:, t, cbk * 512:(cbk + 1) * 512], in_=psF[t][:]),
                             reads=[B_psF[t]], writes=[B_fbuf[t]])
                S.op("dve", lambda e: e.reduce_sum(out=rsf[:], in_=ssf[:], axis=AX.X), reads=[B_ssf], writes=[B_rsf])
                S.op("act", lambda e: e.activation(out=rsf[:], in_=rsf[:], func=AF.Sqrt, bias=EPS, scale=1.0 / D),
                     reads=[B_rsf], writes=[B_rsf])
                S.op("dve", lambda e: e.reciprocal(out=rsf[:], in_=rsf[:]), reads=[B_rsf], writes=[B_rsf])
                for t in range(4):
                    r0 = g * 512 + t * 128
                    x1, Bx1, cx1 = x1r.next()
                    S.op("sp", lambda e, x1=x1, r0=r0: e.dma_start(out=x1[:], in_=x1s[r0:r0 + 128, :]), reads=[B_scr["x1s"]],
                         writes=[Bx1], dma_ctr=cx1)
                    S.op("dve", lambda e, t=t: e.scalar_tensor_tensor(out=fbuf[:, t, :], in0=fbuf[:, t, :], scalar=rsf[:, t:t + 1],
                                                                      in1=GGf[:], op0=ALU.mult, op1=ALU.mult),
                         reads=[B_fbuf[t], B_rsf, B_const], writes=[B_fbuf[t]])
                    S.op("pool", lambda e, t=t, x1=x1: e.tensor_tensor(out=fbuf[:, t, :], in0=fbuf[:, t, :], in1=x1[:], op=ALU.add),
                         reads=[B_fbuf[t], Bx1], writes=[B_fbuf[t]])
                    S.op("sp", lambda e, t=t, r0=r0: e.dma_start(out=y[r0:r0 + 128, :], in_=fbuf[:, t, :]), reads=[B_fbuf[t]],
                         writes=[B_scr["y"]], dma_ctr=c_y)
            S.op("sp", lambda e: None, reads=[B_scr["y"]], noinst=True)
            S.flush()
    except StopBuild:
        pass
    S = build.S
    sched_finish(S)
    build.n_instr = S.n_instr
    build.nsem = S.nvsem
    return nc


def _consts():
    bf = ml_dtypes.bfloat16
    ident = np.eye(128, dtype=np.float32)
    pA = np.zeros((128, 128), np.float32)
    for i in range(128):
        pA[(i + 64) % 128, i] = 1.0
    pB = np.zeros((128, 128), np.float32)
    for i in range(128):
        blk, w = divmod(i, 64)
        pB[blk * 64 + (w + 32) % 64, i] = 1.0
    k = np.arange(128)[:, None]
    q = np.arange(512)[None, :]
    cm = [(q >= (m * 128 + k)).astype(np.float32) for m in range(4)]
    q1 = np.arange(128)[None, :]
    band = [(k >= q1).astype(np.float32), (k <= q1).astype(np.float32)]
    cb16 = np.concatenate([ident, pA, pB] + cm + band, axis=1).astype(bf)
    rc = np.zeros((128, 4), np.float32)
    invA = (10000.0 ** (-(np.arange(64, dtype=np.float32)) / np.float32(64))).astype(np.float32)
    invB = (10000.0 ** (-(np.arange(32, dtype=np.float32)) / np.float32(32))).astype(np.float32)
    p = np.arange(128)
    rc[:, 0] = invA[p % 64]
    rc[:, 1] = invB[p % 32]
    rc[:, 2] = np.where(p < 64, -1.0, 1.0)
    rc[:, 3] = np.where((p % 64) < 32, -1.0, 1.0)
    return cb16, rc


def make_in_maps(x, c, positions, w_ada, b_ada, g_pre_attn, w_in, g_out_a, lambda_q1, lambda_k1, lambda_q2, lambda_k2,
                 g_subln_b, w_out, g_post_attn, g_pre_ffn, w_gate, w_up, w_down, g_post_ffn):
    f32 = np.float32
    x = np.asarray(x, f32)
    c = np.asarray(c, f32)
    positions = np.asarray(positions, np.int32)
    cb16, rc = _consts()
    w_in0 = np.asarray(w_in, f32)[0]
    perm = np.arange(1024).reshape(2, 8, 64).transpose(1, 0, 2).reshape(-1)
    w_in_p = np.concatenate([w_in0[:, 0:3072], w_in0[:, 3072:4096][:, perm], w_in0[:, 4096:5120][:, perm], w_in0[:, 5120:6144]],
                            axis=1)
    w_in_p = np.ascontiguousarray(w_in_p)
    shared = {
        "w_ada": np.ascontiguousarray(np.asarray(w_ada, f32)[0]),
        "b_ada": np.ascontiguousarray(np.asarray(b_ada, f32)[0][None, :]),
        "gpa": np.ascontiguousarray(np.asarray(g_pre_attn, f32)[0].reshape(KC, 128).T),
        "gpf": np.ascontiguousarray(np.asarray(g_pre_ffn, f32)[0].reshape(KC, 128).T),
        "gposta": np.ascontiguousarray(np.asarray(g_post_attn, f32)[0][None, :]),
        "gpostf": np.ascontiguousarray(np.asarray(g_post_ffn, f32)[0][None, :]),
        "w_in": w_in_p,
        "gcol": np.ascontiguousarray(np.stack([np.asarray(g_out_a, f32)[0], np.asarray(g_subln_b, f32)[0]], axis=1)),
        "lamv": np.ascontiguousarray(np.concatenate([np.asarray(a, f32)[0] for a in (lambda_q1, lambda_k1, lambda_q2, lambda_k2)])[None, :]),
        "w_out": np.ascontiguousarray(np.asarray(w_out, f32)[0]),
        "w_gate": np.ascontiguousarray(np.asarray(w_gate, f32)[0]),
        "w_up": np.ascontiguousarray(np.asarray(w_up, f32)[0]),
        "w_down": np.ascontiguousarray(np.asarray(w_down, f32)[0]),
        "rc": rc,
        "cb16": cb16,
    }
    in_maps = []
    for core in range(8):
        b, j = divmod(core, 4)
        chunks = [(j - s) % 4 for s in range(4)]
        xsl = np.stack([x[b, ch * NT:(ch + 1) * NT] for ch in chunks], axis=0)
        pos = np.stack([positions[b, ch * NT:(ch + 1) * NT] for ch in chunks], axis=0)[:, None, :]
        eb = np.zeros((128, 4), f32)
        for s in range(4):
            if s > j:
                eb[:, s] = NEG
        m = dict(shared)
        m["xs"] = np.ascontiguousarray(xsl)
        m["posi"] = np.ascontiguousarray(pos.astype(np.int32))
        m["ebias"] = eb
        m["cT"] = np.ascontiguousarray(c[b].reshape(KC, 128).T)
        in_maps.append(m)
    return in_maps


_NC_CACHE = {}


def kernel(**inputs):
    in_maps = make_in_maps(**inputs)
    if "nc" not in _NC_CACHE:
        _NC_CACHE["nc"] = build(debug=False)
    nc = _NC_CACHE["nc"]
    res = run_bass_kernel_spmd(nc, in_maps, core_ids=list(range(8)))
    out = np.empty((2, 4 * NT, D), np.float32)
    for core in range(8):
        b, j = divmod(core, 4)
        out[b, j * NT:(j + 1) * NT] = np.asarray(res.results[core]["y"], np.float32)
    return out
```

```python
import math
import os
from contextlib import ExitStack

import numpy as np
import ml_dtypes

import concourse.bass as bass
import concourse.mybir as mybir
from concourse.bass_utils import run_bass_kernel_spmd

F32 = mybir.dt.float32
BF16 = mybir.dt.bfloat16
I32 = mybir.dt.int32
AF = mybir.ActivationFunctionType
ALU = mybir.AluOpType
AX = mybir.AxisListType

SEM_LIMIT = 32000
SAME_ENGINE_SYNC = True


class Buf:
    __slots__ = ("name", "writers", "readers")

    def __init__(self, name=""):
        self.name = name
        self.writers = {}
        self.readers = {}


class SemCtr:
    def __init__(self, S):
        self.S = S
        self.vid = S.new_vsem()
        self.count = 0
        self.hist = {}

    def bump(self, inc):
        if self.count + inc > SEM_LIMIT:
            self.hist[self.vid] = self.count
            self.vid = self.S.new_vsem()
            self.count = 0
        self.count += inc
        return self.vid, self.count

    def current_for(self, vid):
        return self.count if vid == self.vid else self.hist[vid]


class Ev:
    __slots__ = ("eng", "fn", "deps", "sem", "val", "flag", "is_dma", "ctr", "phase", "noinst")


ENGS = ("pe", "act", "dve", "pool", "sp")


class Sched:
    def __init__(self, nc, outer):
        self.nc = nc
        self.outer = outer
        self.prog = {e: [] for e in ENGS}
        self.nvsem = 0
        self.phase = 0
        self.sems = []
        self.eng_ctr = {e: SemCtr(self) for e in ENGS}
        self.waited = {e: {} for e in ENGS}
        self.barrier = []
        self.phase_dmas = {}
        self.n_instr = 0

    def new_vsem(self):
        self.nvsem += 1
        return self.nvsem - 1

    def dma_ctr(self):
        return SemCtr(self)

    def op(self, eng, fn, reads=(), writes=(), dma_ctr=None, noinst=False, carry=False):
        ev = Ev()
        ev.noinst = noinst
        ev.eng = eng
        ev.fn = fn
        ev.is_dma = dma_ctr is not None
        ev.ctr = dma_ctr
        ev.flag = ev.is_dma
        ev.sem = None
        ev.val = None
        ev.phase = self.phase
        deps = {}
        for b in reads:
            for w in b.writers.values():
                deps[id(w)] = w
            if b.name.startswith("ps"):
                for k_, r in b.readers.items():
                    if k_ != eng:
                        deps[id(r)] = r
        for b in writes:
            for w in b.writers.values():
                deps[id(w)] = w
            for r in b.readers.values():
                deps[id(r)] = r
        dl = []
        for d in deps.values():
            if d is ev:
                continue
            if (not d.is_dma) and d.phase < self.phase:
                continue
            if (not d.is_dma) and d.eng == eng:
                if eng == "pe" or not SAME_ENGINE_SYNC:
                    continue
            if d.is_dma:
                dl.append((d, d.ctr.current_for(d.sem)))
            else:
                d.flag = True
                dl.append((d, None))
        ev.deps = dl
        if ev.is_dma:
            ev.sem, ev.val = dma_ctr.bump(16)
            if not carry:
                self.phase_dmas[ev.sem] = ev.val
        key = ("d", id(dma_ctr)) if ev.is_dma else eng
        for b in reads:
            b.readers[key] = ev
        for b in writes:
            b.writers[key] = ev
        self.prog[eng].append(ev)
        return ev

    def flush(self):
        nc = self.nc
        prog = self.prog
        new_barrier = []
        for e in ENGS:
            last = None
            for ev in prog[e]:
                if not ev.is_dma and not ev.noinst:
                    last = ev
            if last is not None:
                last.flag = True
        for e in ENGS:
            ctr = self.eng_ctr[e]
            lastev = None
            for ev in prog[e]:
                if ev.is_dma:
                    continue
                if ev.flag and not ev.noinst:
                    ev.sem, ev.val = ctr.bump(1)
                    lastev = ev
            if lastev is not None:
                new_barrier.append((lastev.sem, lastev.val))
        while len(self.sems) < self.nvsem:
            self.sems.append(self.outer.enter_context(nc.semaphore(f"s{len(self.sems)}")))
        sems = self.sems
        old_barrier = self.barrier

        def run(engname):
            def body(eng):
                waited = self.waited[engname]
                for vid, val in old_barrier:
                    if waited.get(vid, 0) < val:
                        eng.wait_ge(sems[vid], val)
                        waited[vid] = val
                for ev in prog[engname]:
                    for d, snap in ev.deps:
                        vid = d.sem
                        val = snap if d.is_dma else d.val
                        if waited.get(vid, 0) < val:
                            eng.wait_ge(sems[vid], val)
                            waited[vid] = val
                    ins = ev.fn(eng)
                    self.n_instr += 1
                    if ev.flag and not ev.noinst:
                        ins.then_inc(sems[ev.sem], 16 if ev.is_dma else 1)
            return body

        with nc.Block() as block:
            block.sync(run("sp"))
            block.scalar(run("act"))
            block.vector(run("dve"))
            block.gpsimd(run("pool"))
            block.tensor(run("pe"))
        new_barrier.extend(self.phase_dmas.items())
        self.phase_dmas = {}
        self.barrier = new_barrier
        self.prog = {e: [] for e in ENGS}
        self.phase += 1


def sched_finish(S):
    nc = S.nc
    sems = S.sems
    items = list(S.barrier)

    def body(eng):
        waited = S.waited["sp"]
        for vid, val in items:
            if waited.get(vid, 0) < val:
                eng.wait_ge(sems[vid], val)
                waited[vid] = val

    with nc.Block() as block:
        block.sync(body)


class Ring:
    def __init__(self, items):
        self.items = items
        self.i = 0

    def next(self):
        it = self.items[self.i % len(self.items)]
        self.i += 1
        return it


D = 2048
KC = 16
NT = 2048
NSLOT = 4
DFF = 5632
NFC = DFF // 128
HD = 128
SCALE_A = HD ** -0.5
SCALE_B = 64 ** -0.5
EPS = 1e-6
LAMBDA_INIT = 0.8 - 0.6 * math.exp(-0.3 * 0)
NEG = -30000.0
INV2PI = float(np.float32(1.0 / (2 * np.pi)))
MAGIC = 12582912.0
C1 = 6.28125
C2 = float(np.float32(2 * np.pi - 6.28125))
HALFPI = float(np.pi / 2)
PI_SAFE = float(np.nextafter(np.float32(np.pi), np.float32(0)))


class StopBuild(Exception):
    pass


def build(debug=False, upto=9):
    nc = bass.Bass("TRN2", target_bir_lowering=False)
    dk = "ExternalOutput" if debug else "Internal"

    def din(name, shape, dt):
        return nc.dram_tensor(name, shape, dt, kind="ExternalInput").ap()

    def dscr(name, shape, dt):
        return nc.dram_tensor(name, shape, dt, kind=dk).ap()

    xs = din("xs", [NSLOT, NT, D], F32)
    posi_d = din("posi", [NSLOT, 1, NT], I32)
    ebias_d = din("ebias", [128, 4], F32)
    cT_d = din("cT", [128, KC], F32)
    w_ada = din("w_ada", [D, 6 * D], F32)
    b_ada = din("b_ada", [1, 6 * D], F32)
    gpa_d = din("gpa", [128, KC], F32)
    gpf_d = din("gpf", [128, KC], F32)
    gposta_d = din("gposta", [1, D], F32)
    gpostf_d = din("gpostf", [1, D], F32)
    w_in = din("w_in", [D, 6144], F32)
    gcol_d = din("gcol", [128, 2], F32)
    lamv_d = din("lamv", [1, 256], F32)
    w_out = din("w_out", [D, D], F32)
    w_gate = din("w_gate", [D, DFF], F32)
    w_up = din("w_up", [D, DFF], F32)
    w_down = din("w_down", [DFF, D], F32)
    rc_d = din("rc", [128, 4], F32)
    cb16_d = din("cb16", [128, 3 * 128 + 4 * 512 + 2 * 128], BF16)
    y = nc.dram_tensor("y", [NT, D], F32, kind="ExternalOutput").ap()

    qaT = dscr("qaT", [8, 128, NT], BF16)
    kaT = dscr("kaT", [8, 128, 2 * NT], BF16)
    va = dscr("va", [2 * NT, 1024], BF16)
    qbT = dscr("qbT", [8, 128, NT], BF16)
    kbT = dscr("kbT", [8, 128, 4 * NT], BF16)
    vb = dscr("vb", [4 * NT, 1024], BF16)
    mixs = dscr("mixs", [16, 128, NT], BF16)
    x1s = dscr("x1s", [NT, D], F32)
    h2s = dscr("h2s", [4, 128, KC, 512], BF16)
    ggs = dscr("ggs", [2, 128, D], F32)

    try:
      with ExitStack() as outer:
        S = Sched(nc, outer)
        build.S = S

        def sbo(name, shape, dt):
            return outer.enter_context(nc.sbuf_tensor("s_" + name, shape, dt))

        ident_t = sbo("ident", [128, 128], BF16)
        pswA_t = sbo("pswA", [128, 128], BF16)
        pswB_t = sbo("pswB", [128, 128], BF16)
        cmask_t = sbo("cmask", [128, 4, 512], BF16)
        band_t = sbo("band", [128, 2, 128], BF16)
        ident = ident_t[:]
        pswA = pswA_t[:]
        pswB = pswB_t[:]

        def cmask(m):
            return cmask_t[:, m, :]

        def bandm(hf):
            return band_t[:, hf, :]
        ones_bf = sbo("ones_bf", [128, 128], BF16)
        ones_f = sbo("ones_f", [128, 128], F32)
        ebias = sbo("ebias", [128, 4], F32)
        rc = sbo("rc", [128, 4], F32)
        halfpi = sbo("halfpi", [128, 1], F32)
        modv = sbo("modv", [128, 4, KC], F32)
        gcol = sbo("gcol", [128, 2], F32)
        gsub08 = sbo("gsub08", [128, 1], F32)
        neglam = sbo("neglam", [128, 1], F32)
        B_const = Buf("const")
        B_scr = {k: Buf(k) for k in ["qaT", "kaT", "va", "qbT", "kbT", "vb", "mixs", "x1s", "h2s", "y", "ggs"]}

        with ExitStack() as ph:
            def sb(name, shape, dt):
                return ph.enter_context(nc.sbuf_tensor("s_" + name, shape, dt))

            def ps(name, shape, dt):
                return ph.enter_context(nc.psum_tensor("p_" + name, shape, dt))
            cT = sb("cT", [128, KC], F32)
            GGa = sb("GGa0", [128, D], F32)
            GGf = sb("GGf0", [128, D], F32)
            scT = sb("scT", [128, KC], BF16)
            brow = sb("brow", [1, 6 * D], F32)
            modrow = sb("modrow", [1, 6 * D], F32)
            gpa = sb("gpa", [128, KC], F32)
            gpf = sb("gpf", [128, KC], F32)
            lamv = sb("lamv", [128, 256], F32)
            lprod = sb("lprod", [128, 128], F32)
            lsum = sb("lsum", [128, 2], F32)
            wbl = [sb(f"wbl{i}", [128, KC, 512], BF16) for i in range(2)]
            psM = ps("psM", [1, 512], F32)
            psC = ps("psC", [128, 4, KC], F32)
            psR = [ps(f"psR{i}", [128, 512], F32) for i in range(2)]
            B_cT, B_scT, B_brow, B_modrow, B_g, B_lam, B_lp, B_ls = (Buf(n) for n in
                                                                       ["cT", "scT", "brow", "modrow", "g", "lam", "lp", "ls"])
            B_wbl = [Buf("wbl0"), Buf("wbl1")]
            B_psM, B_psC = Buf("psM"), Buf("psC")
            B_psR = [Buf("psR0"), Buf("psR1")]
            B_GGa, B_GGf = Buf("GGa"), Buf("GGf")
            c0 = S.dma_ctr()
            for (dst, src) in [(ident, cb16_d[:, 0:128]), (pswA, cb16_d[:, 128:256]), (pswB, cb16_d[:, 256:384]),
                               (cmask_t[:], cb16_d[:, 384:384 + 2048].rearrange("p (m q) -> p m q", m=4)),
                               (band_t[:], cb16_d[:, 384 + 2048:384 + 2048 + 256].rearrange("p (m q) -> p m q", m=2)),
                               (ebias[:], ebias_d), (rc[:], rc_d), (gcol[:], gcol_d)]:
                S.op("sp", lambda e, dst=dst, src=src: e.dma_start(out=dst, in_=src), writes=[B_const], dma_ctr=c0)
            c1 = S.dma_ctr()
            S.op("sp", lambda e: e.dma_start(out=cT[:], in_=cT_d), writes=[B_cT], dma_ctr=c1)
            S.op("sp", lambda e: e.dma_start(out=brow[:], in_=b_ada), writes=[B_brow], dma_ctr=c1)
            S.op("sp", lambda e: e.dma_start(out=gpa[:], in_=gpa_d), writes=[B_g], dma_ctr=c1)
            S.op("sp", lambda e: e.dma_start(out=gpf[:], in_=gpf_d), writes=[B_g], dma_ctr=c1)
            S.op("sp", lambda e: e.dma_start(out=lamv[:], in_=lamv_d.broadcast_to([128, 256])), writes=[B_lam], dma_ctr=c1)
            c2 = S.dma_ctr()
            S.op("sp", lambda e: e.dma_start(out=GGa[:], in_=gposta_d.broadcast_to([128, D])), writes=[B_GGa], dma_ctr=c2)
            S.op("sp", lambda e: e.dma_start(out=GGf[:], in_=gpostf_d.broadcast_to([128, D])), writes=[B_GGf], dma_ctr=c2)
            S.op("pool", lambda e: e.memset(ones_bf[:], 1.0), writes=[B_const])
            S.op("pool", lambda e: e.memset(ones_f[:], 1.0), writes=[B_const])
            S.op("pool", lambda e: e.memset(halfpi[:], HALFPI), writes=[B_const])
            S.op("act", lambda e: e.activation(out=scT[:], in_=cT[:], func=AF.Silu), reads=[B_cT], writes=[B_scT])
            cw = [S.dma_ctr(), S.dma_ctr()]
            def load_ada(blk):
                i = blk % 2
                src = w_ada[:, blk * 512:(blk + 1) * 512].rearrange("(kc p) n -> p kc n", p=128)
                S.op("pool", lambda e, i=i, src=src: e.dma_start(out=wbl[i][:], in_=src), writes=[B_wbl[i]], dma_ctr=cw[i])
            load_ada(0)
            for blk in range(24):
                i = blk % 2
                if blk + 1 < 24:
                    load_ada(blk + 1)
                for kc in range(KC):
                    S.op("pe", lambda e, i=i, kc=kc: e.matmul(psM[:], lhsT=scT[:, kc:kc + 1], rhs=wbl[i][:, kc, :],
                                                             start=(kc == 0), stop=(kc == KC - 1)),
                         reads=[B_scT, B_wbl[i]], writes=[B_psM])
                S.op("dve", lambda e, blk=blk: e.tensor_tensor(out=modrow[0:1, blk * 512:(blk + 1) * 512], in0=psM[:],
                                                               in1=brow[0:1, blk * 512:(blk + 1) * 512], op=ALU.add),
                     reads=[B_psM, B_brow], writes=[B_modrow])
            for vi, sec in enumerate([1, 0, 4, 3]):
                for j in range(KC):
                    o = sec * D + j * 128
                    S.op("pe", lambda e, vi=vi, j=j, o=o: e.matmul(psC[:, vi, j:j + 1], lhsT=modrow[0:1, o:o + 128],
                                                                    rhs=ones_f[0:1, 0:1], start=True, stop=True),
                         reads=[B_modrow, B_const], writes=[B_psC])
            B_modv = B_const
            S.op("dve", lambda e: e.tensor_scalar(out=modv[:, 0, :], in0=psC[:, 0, :], scalar1=1.0, scalar2=None, op0=ALU.add),
                 reads=[B_psC], writes=[B_modv])
            S.op("dve", lambda e: e.tensor_tensor(out=modv[:, 0, :], in0=modv[:, 0, :], in1=gpa[:], op=ALU.mult),
                 reads=[B_modv, B_g], writes=[B_modv])
            S.op("dve", lambda e: e.tensor_copy(out=modv[:, 1, :], in_=psC[:, 1, :]), reads=[B_psC], writes=[B_modv])
            S.op("dve", lambda e: e.tensor_scalar(out=modv[:, 2, :], in0=psC[:, 2, :], scalar1=1.0, scalar2=None, op0=ALU.add),
                 reads=[B_psC], writes=[B_modv])
            S.op("dve", lambda e: e.tensor_tensor(out=modv[:, 2, :], in0=modv[:, 2, :], in1=gpf[:], op=ALU.mult),
                 reads=[B_modv, B_g], writes=[B_modv])
            S.op("dve", lambda e: e.tensor_copy(out=modv[:, 3, :], in_=psC[:, 3, :]), reads=[B_psC], writes=[B_modv])
            k = 0
            for (GG, Bg, sec) in [(GGa, B_GGa, 2), (GGf, B_GGf, 5)]:
                for cbk in range(4):
                    o = sec * D + cbk * 512
                    pr, Bpr = psR[k % 2], B_psR[k % 2]
                    k += 1
                    S.op("pe", lambda e, pr=pr, o=o: e.matmul(pr[:], lhsT=ones_f[0:1, :], rhs=modrow[0:1, o:o + 512],
                                                              start=True, stop=True),
                         reads=[B_modrow, B_const], writes=[Bpr])
                    S.op("dve", lambda e, GG=GG, pr=pr, cbk=cbk: e.tensor_tensor(out=GG[:, cbk * 512:(cbk + 1) * 512], in0=pr[:],
                                                                                 in1=GG[:, cbk * 512:(cbk + 1) * 512], op=ALU.mult),
                         reads=[Bpr, Bg], writes=[Bg])
            c_ggw = S.dma_ctr()
            S.op("sp", lambda e: e.dma_start(out=ggs[0], in_=GGa[:]), reads=[B_GGa], writes=[B_scr["ggs"]], dma_ctr=c_ggw)
            S.op("sp", lambda e: e.dma_start(out=ggs[1], in_=GGf[:]), reads=[B_GGf], writes=[B_scr["ggs"]], dma_ctr=c_ggw)
            S.op("dve", lambda e: e.tensor_tensor(out=lprod[:, 0:64], in0=lamv[:, 0:64], in1=lamv[:, 64:128], op=ALU.mult),
                 reads=[B_lam], writes=[B_lp])
            S.op("dve", lambda e: e.tensor_tensor(out=lprod[:, 64:128], in0=lamv[:, 128:192], in1=lamv[:, 192:256], op=ALU.mult),
                 reads=[B_lam], writes=[B_lp])
            S.op("dve", lambda e: e.reduce_sum(out=lsum[:, 0:1], in_=lprod[:, 0:64], axis=AX.X), reads=[B_lp], writes=[B_ls])
            S.op("dve", lambda e: e.reduce_sum(out=lsum[:, 1:2], in_=lprod[:, 64:128], axis=AX.X), reads=[B_lp], writes=[B_ls])
            S.op("act", lambda e: e.activation(out=lsum[:], in_=lsum[:], func=AF.Exp), reads=[B_ls], writes=[B_ls])
            S.op("dve", lambda e: e.tensor_tensor(out=neglam[:], in0=lsum[:, 1:2], in1=lsum[:, 0:1], op=ALU.subtract),
                 reads=[B_ls], writes=[B_const])
            S.op("dve", lambda e: e.tensor_scalar(out=neglam[:], in0=neglam[:], scalar1=-LAMBDA_INIT, scalar2=None, op0=ALU.add),
                 reads=[B_const], writes=[B_const])
            S.op("dve", lambda e: e.tensor_scalar(out=gsub08[:], in0=gcol[:, 1:2], scalar1=1.0 - LAMBDA_INIT, scalar2=None,
                                                   op0=ALU.mult), reads=[B_const], writes=[B_const])
            S.flush()

        def norm_tiles_to_hT(get_tile, ssq, B_ssq, xn, B_xn, psT, B_psT, hT_ap_fn, B_hT, mi, evk):
            for t in range(4):
                xt, Bx = get_tile(t)
                S.op("act", lambda e, xt=xt, t=t: e.activation(out=xn[:, t, :], in_=xt, func=AF.Square, accum_out=ssq[:, t:t + 1]),
                     reads=[Bx], writes=[B_xn, B_ssq[t]])
                S.op("act", lambda e, t=t: e.activation(out=ssq[:, t:t + 1], in_=ssq[:, t:t + 1], func=AF.Sqrt, bias=EPS,
                                                        scale=1.0 / D), reads=[B_ssq[t]], writes=[B_ssq[t]])
                S.op("dve", lambda e, t=t: e.reciprocal(out=ssq[:, t:t + 1], in_=ssq[:, t:t + 1]), reads=[B_ssq[t]],
                     writes=[B_ssq[t]])
                S.op("dve", lambda e, xt=xt, t=t: e.tensor_scalar(out=xn[:, t, :], in0=xt, scalar1=ssq[:, t:t + 1], scalar2=None,
                                                                   op0=ALU.mult), reads=[Bx, B_ssq[t]], writes=[B_xn])
            for kc in range(KC):
                pt, Bpt = psT[kc % 2], B_psT[kc % 2]
                for t in range(4):
                    S.op("pe", lambda e, pt=pt, t=t, kc=kc: e.transpose(out=pt[:, t * 128:(t + 1) * 128],
                                                                        in_=xn[:, t, kc * 128:(kc + 1) * 128], identity=ident),
                         reads=[B_xn, B_const], writes=[Bpt])
                dst = hT_ap_fn(kc)
                if (kc + evk) % 2 == 0:
                    S.op("act", lambda e, dst=dst, pt=pt, kc=kc: e.activation(out=dst, in_=pt[:], func=AF.Identity,
                                                                              bias=modv[:, mi + 1, kc:kc + 1],
                                                                              scale=modv[:, mi, kc:kc + 1]),
                         reads=[Bpt, B_const], writes=[B_hT[kc]])
                else:
                    S.op("dve", lambda e, dst=dst, pt=pt, kc=kc: e.tensor_scalar(out=dst, in0=pt[:], scalar1=modv[:, mi, kc:kc + 1],
                                                                                 scalar2=modv[:, mi + 1, kc:kc + 1],
                                                                                 op0=ALU.mult, op1=ALU.add),
                         reads=[Bpt, B_const], writes=[B_hT[kc]])

        if upto < 1:
            raise StopBuild()
        with ExitStack() as ph:
            def sb(name, shape, dt):
                return ph.enter_context(nc.sbuf_tensor("s_" + name, shape, dt))

            def ps(name, shape, dt):
                return ph.enter_context(nc.psum_tensor("p_" + name, shape, dt))
            HN = NT // 2
            hTh = [sb(f"hT{i}", [128, KC, HN], BF16) for i in range(2)]
            B_hTh = [[Buf(f"hT{i}_{kc}") for kc in range(KC)] for i in range(2)]
            xtl = [sb(f"xt{i}", [128, D], F32) for i in range(2)]
            B_xtl = [Buf(f"xt{i}") for i in range(2)]
            c_xt = [S.dma_ctr() for _ in range(2)]
            xn = sb("xn", [128, 4, D], BF16)
            B_xn = Buf("xn")
            ssq = sb("ssq", [128, 4], F32)
            B_ssq = [Buf(f"ssq{i}") for i in range(4)]
            wb = [sb(f"wb{i}", [128, KC, 512], BF16) for i in range(2)]
            B_wb = [Buf("wb0"), Buf("wb1")]
            c_wb = [S.dma_ctr(), S.dma_ctr()]
            tabs = sb("tabs", [128, 4, NT], F32)
            B_tabs = Buf("tabs")
            pos_i = sb("pos_i", [128, 512], I32)
            posf = sb("posf", [128, 512], F32)
            ang = sb("ang", [128, 512], F32)
            ru = sb("ru", [128, 512], F32)
            B_posi, B_posf, B_ang, B_ru = Buf("posi"), Buf("posf"), Buf("ang"), Buf("ru")
            c_pos = S.dma_ctr()
            xq = Ring([(sb(f"xq{i}", [128, 512], BF16), Buf(f"xq{i}")) for i in range(2)])
            t1r = Ring([(sb(f"t1_{i}", [128, 512], F32), Buf(f"t1_{i}")) for i in range(2)])
            t2r = Ring([(sb(f"t2_{i}", [128, 512], F32), Buf(f"t2_{i}")) for i in range(2)])
            qst = Ring([(sb(f"qst{i}", [128, 512], BF16), Buf(f"qst{i}"), S.dma_ctr()) for i in range(3)])
            vst = Ring([(sb(f"vst{i}", [128, 4, 512], BF16), Buf(f"vst{i}"), S.dma_ctr()) for i in range(2)])
            psT = [ps(f"psT{i}", [128, 512], BF16) for i in range(2)]
            B_psT = [Buf("psT0"), Buf("psT1")]
            psQ = Ring([(ps(f"psQ{i}", [128, 512], F32), Buf(f"psQ{i}")) for i in range(2)])
            psW = Ring([(ps(f"psW{i}", [128, 512], F32), Buf(f"psW{i}")) for i in range(2)])
            psV = Ring([(ps(f"psV{i}", [128, 512], F32), Buf(f"psV{i}")) for i in range(2)])

            def blocks_for(slot):
                bl = []
                if slot == 0:
                    bl += [("q", 0, qaT, 0, 0), ("q", 512, qaT, 4, 0)]
                if slot <= 1:
                    bl += [("k", 1024, kaT, 0, 0), ("k", 1536, kaT, 4, 0)]
                if slot == 0:
                    bl += [("q", 3072, qbT, 0, 1), ("q", 3584, qbT, 4, 1)]
                bl += [("k", 4096, kbT, 0, 1), ("k", 4608, kbT, 4, 1)]
                if slot <= 1:
                    bl += [("v", 2048, va, 0, 0), ("v", 2560, va, 512, 0)]
                bl += [("v", 5120, vb, 0, 1), ("v", 5632, vb, 512, 1)]
                return bl

            DBG_SLOTS = int(os.environ.get('K_DBG_SLOTS', NSLOT))
            DBG_BLOCKS = int(os.environ.get('K_DBG_BLOCKS', 99))
            DBG_HT = int(os.environ.get('K_DBG_HT', 1))
            jobs = [(slot, bi) for slot in range(DBG_SLOTS) for half in range(2) for bi in blocks_for(slot)[:DBG_BLOCKS]]

            def load_w(n):
                if n >= len(jobs):
                    return
                i = n % 2
                col = jobs[n][1][1]
                src = w_in[:, col:col + 512].rearrange("(kc p) n -> p kc n", p=128)
                S.op("pool", lambda e, i=i, src=src: e.dma_start(out=wb[i][:], in_=src), writes=[B_wb[i]], dma_ctr=c_wb[i])
            load_w(0)
            nwb = 0
            hidx = 0

            pend = []

            def post_qk(pq, Bpq, xqt, Bxq, tb, psw, rs, dst, Bdst, h, c0_):
                pw, Bpw = psW.next()
                S.op("pe", lambda e: e.matmul(pw[:], lhsT=psw, rhs=xqt[:], start=True, stop=True),
                     reads=[Bxq, B_const], writes=[Bpw])
                t1, Bt1 = t1r.next()
                t2, Bt2 = t2r.next()
                S.op("dve", lambda e: e.tensor_tensor(out=t1[:], in0=pq[:], in1=tabs[:, 2 * rs, tb:tb + 512], op=ALU.mult),
                     reads=[Bpq, B_tabs], writes=[Bt1])
                S.op("dve", lambda e: e.tensor_tensor(out=t2[:], in0=pw[:], in1=tabs[:, 2 * rs + 1, tb:tb + 512], op=ALU.mult),
                     reads=[Bpw, B_tabs], writes=[Bt2])
                qs, Bqs, cqs = qst.next()
                S.op("pool", lambda e: e.tensor_tensor(out=qs[:], in0=t1[:], in1=t2[:], op=ALU.add),
                     reads=[Bt1, Bt2], writes=[Bqs])
                S.op("sp", lambda e: e.dma_start(out=dst[h, :, c0_:c0_ + 512], in_=qs[:]), reads=[Bqs], writes=[Bdst], dma_ctr=cqs)

            def emit_norm(hi, g):
                slot_, half_ = divmod(hi, 2)
                hT_ = hTh[hi % 2]

                def get_tile(t):
                    r0 = half_ * HN + g * 512 + t * 128
                    S.op("sp", lambda e, r0=r0, t=t: e.dma_start(out=xtl[t % 2][:], in_=xs[slot_, r0:r0 + 128, :]),
                         writes=[B_xtl[t % 2]], dma_ctr=c_xt[t % 2])
                    return xtl[t % 2][:], B_xtl[t % 2]
                norm_tiles_to_hT(get_tile, ssq, B_ssq, xn, B_xn, psT, B_psT,
                                 lambda kc: hT_[:, kc, g * 512:(g + 1) * 512], B_hTh[hi % 2], 0, hi * 2 + g)
            for slot in range(DBG_SLOTS):
                for g in range(4):
                    S.op("sp", lambda e, slot=slot, g=g: e.dma_start(
                        out=pos_i[:], in_=posi_d[slot, :, g * 512:(g + 1) * 512].broadcast_to([128, 512])),
                        writes=[B_posi], dma_ctr=c_pos)
                    S.op("dve", lambda e: e.tensor_copy(out=posf[:], in_=pos_i[:]), reads=[B_posi], writes=[B_posf])
                    for ts in range(2):
                        if ts == 0 and slot > 1:
                            continue
                        S.op("dve", lambda e, ts=ts: e.tensor_scalar(out=ang[:], in0=posf[:], scalar1=rc[:, ts:ts + 1], scalar2=None,
                                                                      op0=ALU.mult), reads=[B_posf, B_const], writes=[B_ang])
                        S.op("dve", lambda e: e.tensor_scalar(out=ru[:], in0=ang[:], scalar1=INV2PI, scalar2=MAGIC, op0=ALU.mult,
                                                               op1=ALU.add), reads=[B_ang], writes=[B_ru])
                        S.op("dve", lambda e: e.tensor_scalar(out=ru[:], in0=ru[:], scalar1=MAGIC, scalar2=None, op0=ALU.subtract),
                             reads=[B_ru], writes=[B_ru])
                        S.op("dve", lambda e: e.scalar_tensor_tensor(out=ang[:], in0=ru[:], scalar=-C1, in1=ang[:], op0=ALU.mult,
                                                                      op1=ALU.add), reads=[B_ru, B_ang], writes=[B_ang])
                        S.op("dve", lambda e: e.scalar_tensor_tensor(out=ang[:], in0=ru[:], scalar=-C2, in1=ang[:], op0=ALU.mult,
                                                                      op1=ALU.add), reads=[B_ru, B_ang], writes=[B_ang])
                        S.op("dve", lambda e: e.tensor_scalar(out=ang[:], in0=ang[:], scalar1=-PI_SAFE, scalar2=PI_SAFE, op0=ALU.max,
                                                               op1=ALU.min), reads=[B_ang], writes=[B_ang])
                        S.op("act", lambda e, ts=ts, g=g: e.activation(out=tabs[:, 2 * ts + 1, g * 512:(g + 1) * 512], in_=ang[:],
                                                                       func=AF.Sin, scale=rc[:, 2 + ts:3 + ts]),
                             reads=[B_ang, B_const], writes=[B_tabs])
                        S.op("act", lambda e: e.activation(out=ru[:], in_=ang[:], func=AF.Abs), reads=[B_ang], writes=[B_ru])
                        S.op("act", lambda e, ts=ts, g=g: e.activation(out=tabs[:, 2 * ts, g * 512:(g + 1) * 512], in_=ru[:],
                                                                       func=AF.Sin, bias=halfpi[:, 0:1], scale=-1.0),
                             reads=[B_ru, B_const], writes=[B_tabs])
                tokA = {0: NT, 1: 0}.get(slot, None)
                tokB = slot * NT
                for half in range(2):
                    hT = hTh[hidx % 2]
                    B_hT = B_hTh[hidx % 2]
                    hb0 = half * HN
                    if hidx == 0:
                        emit_norm(0, 0)
                        emit_norm(0, 1)
                    hidx += 1
                    for bidx, (kind, col, dst, h0, rs) in enumerate(blocks_for(slot)[:DBG_BLOCKS]):
                        if bidx in (1, 2) and hidx < 2 * DBG_SLOTS:
                            emit_norm(hidx, bidx - 1)
                        i = nwb % 2
                        assert jobs[nwb][0] == slot and jobs[nwb][1][1] == col
                        nwb += 1
                        load_w(nwb)
                        tok0 = (tokA if rs == 0 else tokB)
                        if kind == "q":
                            tok0 = 0
                        tok0 += hb0
                        Bdst = B_scr[{id(qaT): "qaT", id(kaT): "kaT", id(va): "va", id(qbT): "qbT", id(kbT): "kbT", id(vb): "vb"}[id(dst)]]
                        if kind in ("q", "k"):
                            psw = pswA if rs == 0 else pswB
                            for hh in range(4):
                                for g in range(2):
                                    tb = hb0 + g * 512
                                    pq, Bpq = psQ.next()
                                    for kc in range(KC):
                                        S.op("pe", lambda e, pq=pq, i=i, kc=kc, hh=hh, g=g, hT=hT: e.matmul(
                                            pq[:], lhsT=wb[i][:, kc, hh * 128:(hh + 1) * 128], rhs=hT[:, kc, g * 512:(g + 1) * 512],
                                            start=(kc == 0), stop=(kc == KC - 1)), reads=[B_wb[i], B_hT[kc]], writes=[Bpq])
                                    xqt, Bxq = xq.next()
                                    S.op("act", lambda e, xqt=xqt, pq=pq: e.activation(out=xqt[:], in_=pq[:], func=AF.Copy),
                                         reads=[Bpq], writes=[Bxq])
                                    if pend:
                                        pend.pop()()
                                    pend.append(lambda pq=pq, Bpq=Bpq, xqt=xqt, Bxq=Bxq, tb=tb, hh=hh, g=g: post_qk(
                                        pq, Bpq, xqt, Bxq, tb, psw, rs, dst, Bdst, h0 + hh, tok0 + g * 512))
                            if pend:
                                pend.pop()()
                            continue
                            if True:
                                if True:
                                    pw, Bpw = psW.next()
                                    S.op("pe", lambda e, pw=pw, psw=psw, xqt=xqt: e.matmul(pw[:], lhsT=psw, rhs=xqt[:], start=True,
                                                                                           stop=True),
                                         reads=[Bxq, B_const], writes=[Bpw])
                                    t1, Bt1 = t1r.next()
                                    t2, Bt2 = t2r.next()
                                    S.op("dve", lambda e, t1=t1, pq=pq, rs=rs, tb=tb: e.tensor_tensor(
                                        out=t1[:], in0=pq[:], in1=tabs[:, 2 * rs, tb:tb + 512], op=ALU.mult),
                                        reads=[Bpq, B_tabs], writes=[Bt1])
                                    S.op("dve", lambda e, t2=t2, pw=pw, rs=rs, tb=tb: e.tensor_tensor(
                                        out=t2[:], in0=pw[:], in1=tabs[:, 2 * rs + 1, tb:tb + 512], op=ALU.mult),
                                        reads=[Bpw, B_tabs], writes=[Bt2])
                                    qs, Bqs, cqs = qst.next()
                                    S.op("pool", lambda e, qs=qs, t1=t1, t2=t2: e.tensor_tensor(out=qs[:], in0=t1[:], in1=t2[:],
                                                                                               op=ALU.add),
                                         reads=[Bt1, Bt2], writes=[Bqs])
                                    c0_ = tok0 + g * 512
                                    S.op("sp", lambda e, dst=dst, qs=qs, h=h0 + hh, c0_=c0_: e.dma_start(
                                        out=dst[h, :, c0_:c0_ + 512], in_=qs[:]), reads=[Bqs], writes=[Bdst], dma_ctr=cqs)
                        else:
                            for tq in range(2):
                                vs, Bvs, cvs = vst.next()
                                for tt in range(4):
                                    tile = tq * 4 + tt
                                    pv, Bpv = psV.next()
                                    for kc in range(KC):
                                        S.op("pe", lambda e, pv=pv, i=i, kc=kc, tile=tile, hT=hT: e.matmul(
                                            pv[:], lhsT=hT[:, kc, tile * 128:(tile + 1) * 128], rhs=wb[i][:, kc, :],
                                            start=(kc == 0), stop=(kc == KC - 1)), reads=[B_wb[i], B_hT[kc]], writes=[Bpv])
                                    if tt % 2 == 0:
                                        S.op("act", lambda e, vs=vs, pv=pv, tt=tt: e.activation(out=vs[:, tt, :], in_=pv[:], func=AF.Copy),
                                             reads=[Bpv], writes=[Bvs])
                                    else:
                                        S.op("dve", lambda e, vs=vs, pv=pv, tt=tt: e.tensor_copy(out=vs[:, tt, :], in_=pv[:]),
                                             reads=[Bpv], writes=[Bvs])
                                r0 = tok0 + tq * 512
                                S.op("sp", lambda e, dst=dst, vs=vs, r0=r0, h0=h0: e.dma_start(
                                    out=dst[r0:r0 + 512, h0:h0 + 512].rearrange("(t p) n -> p t n", p=128), in_=vs[:]),
                                    reads=[Bvs], writes=[Bdst], dma_ctr=cvs)
            S.flush()

        if upto < 2:
            raise StopBuild()
        with ExitStack() as ph:
            def sb(name, shape, dt):
                return ph.enter_context(nc.sbuf_tensor("s_" + name, shape, dt))

            def ps(name, shape, dt):
                return ph.enter_context(nc.psum_tensor("p_" + name, shape, dt))
            hb = Ring([(sb(f"qh{i}", [128, NT], BF16), sb(f"kh{i}", [128, 4 * NT], BF16), sb(f"vh{i}", [128, 64, 128], BF16),
                        Buf(f"hb{i}"), S.dma_ctr()) for i in range(2)])
            pTr = Ring([(sb(f"pT_{i}", [128, 1024], BF16), Buf(f"pT_{i}")) for i in range(3)])
            accr = Ring([(sb(f"accD{i}", [128, 1024], F32), Buf(f"accD{i}")) for i in range(2)])
            rden = [sb(f"rden{m}", [128, 512], F32) for m in range(2)]
            B_rden = [Buf("rden0"), Buf("rden1")]
            o1 = sb("o1", [128, 512], F32)
            o2 = sb("o2", [128, 512], F32)
            obr = Ring([(sb(f"ob{i}", [128, 512], F32), Buf(f"ob{i}")) for i in range(2)])
            sqr = Ring([(sb(f"sq{i}", [128, 512], F32), Buf(f"sq{i}")) for i in range(2)])
            rstd = sb("rstd", [128, 512], F32)
            B_o1, B_o2, B_rstd = Buf("o1"), Buf("o2"), Buf("rstd")
            mst = Ring([(sb(f"mst{i}", [128, NT], BF16), Buf(f"mst{i}"), S.dma_ctr()) for i in range(2)])
            psS = Ring([(ps(f"psS{i}", [128, 1024], F32), Buf(f"psS{i}")) for i in range(2)])
            psO = [ps(f"psO{m}", [128, 512], F32) for m in range(2)]
            B_psO = [Buf("psO0"), Buf("psO1")]
            psDen = ps("psDen", [128, 1024], F32)
            B_psDen = Buf("psDen")

            def load_head_B(h):
                qh, kh, vh, Bh, ch = hb.next()
                S.op("sp", lambda e: e.dma_start(out=qh[:], in_=qbT[h]), reads=[B_scr["qbT"]], writes=[Bh], dma_ctr=ch)
                S.op("sp", lambda e: e.dma_start(out=kh[:], in_=kbT[h]), reads=[B_scr["kbT"]], writes=[Bh], dma_ctr=ch)
                S.op("sp", lambda e: e.dma_start(out=vh[:], in_=vb[:, h * 128:(h + 1) * 128].rearrange("(t p) n -> p t n", p=128)),
                     reads=[B_scr["vb"]], writes=[Bh], dma_ctr=ch)
                return qh, kh, vh, Bh

            heads = {0: load_head_B(0)}
            mstage = {}
            items = []
            for h in range(8):
                for qb in range(4):
                    pairs = []
                    for slot in range(NSLOT):
                        nk = 4 * (qb + 1) if slot == 0 else 16
                        for kc in range(nk):
                            pairs.append((slot, kc))
                    for pi, (slot, kc) in enumerate(pairs):
                        items.append((h, qb, pi, len(pairs), slot, kc))
            qk_out = {}

            def issue_qk(idx):
                h, qb, pi, npairs, slot, kc = items[idx]
                if h not in heads:
                    heads[h] = load_head_B(h)
                qh, kh, vh, Bh = heads[h]
                ktok = slot * NT + kc * 128
                pS, BpS = psS.next()
                for m in range(2):
                    S.op("pe", lambda e, pS=pS, m=m, ktok=ktok, qb=qb, kh=kh, qh=qh: e.matmul(
                        pS[:, m * 512:(m + 1) * 512], lhsT=kh[m * 64:(m + 1) * 64, ktok:ktok + 128],
                        rhs=qh[m * 64:(m + 1) * 64, qb * 512:(qb + 1) * 512], start=True, stop=True), reads=[Bh], writes=[BpS])
                qk_out[idx] = (pS, BpS)

            deferred = []

            def finalize_a(h, qb, accs):
                ms, Bms, cms = mstage[h]
                accD, BaccD = accs
                for m in range(2):
                    S.op("pe", lambda e, m=m, accD=accD: e.matmul(psDen[:, m * 512:(m + 1) * 512], lhsT=ones_f[:],
                                                                  rhs=accD[:, m * 512:(m + 1) * 512], start=False, stop=True),
                         reads=[BaccD, B_const], writes=[B_psDen])
                for m in range(2):
                    S.op("dve", lambda e, m=m: e.reciprocal(out=rden[m][:], in_=psDen[:, m * 512:(m + 1) * 512]),
                         reads=[B_psDen], writes=[B_rden[m]])
                S.op("dve", lambda e: e.tensor_tensor(out=o1[:], in0=psO[0][:], in1=rden[0][:], op=ALU.mult),
                     reads=[B_psO[0], B_rden[0]], writes=[B_o1])
                S.op("dve", lambda e: e.tensor_tensor(out=o2[:], in0=psO[1][:], in1=rden[1][:], op=ALU.mult),
                     reads=[B_psO[1], B_rden[1]], writes=[B_o2])
                ob, Bob = obr.next()
                sq, Bsq = sqr.next()
                S.op("dve", lambda e, ob=ob: e.scalar_tensor_tensor(out=ob[:], in0=o2[:], scalar=neglam[:, 0:1], in1=o1[:], op0=ALU.mult,
                                                                     op1=ALU.add), reads=[B_o1, B_o2, B_const], writes=[Bob])
                S.op("act", lambda e, ob=ob, sq=sq: e.activation(out=sq[:], in_=ob[:], func=AF.Square), reads=[Bob], writes=[Bsq])

                def part_b():
                    pD, BpD = psS.next()
                    psS.next()
                    S.op("pe", lambda e, pD=pD: e.matmul(pD[:, 0:512], lhsT=ones_f[:], rhs=sq[:], start=True, stop=True),
                         reads=[Bsq, B_const], writes=[BpD])
                    S.op("act", lambda e, pD=pD: e.activation(out=rstd[:], in_=pD[:, 0:512], func=AF.Sqrt, bias=EPS, scale=1.0 / HD),
                         reads=[BpD], writes=[B_rstd])
                    S.op("dve", lambda e: e.reciprocal(out=rstd[:], in_=rstd[:]), reads=[B_rstd], writes=[B_rstd])
                    S.op("dve", lambda e: e.scalar_tensor_tensor(out=ms[:, qb * 512:(qb + 1) * 512], in0=ob[:],
                                                                  scalar=gsub08[:, 0:1], in1=rstd[:], op0=ALU.mult, op1=ALU.mult),
                         reads=[Bob, B_rstd, B_const], writes=[Bms])
                    if qb == 3:
                        S.op("sp", lambda e: e.dma_start(out=mixs[8 + h], in_=ms[:]), reads=[Bms], writes=[B_scr["mixs"]], dma_ctr=cms)
                return part_b

            issue_qk(0)
            accs = None
            for idx, (h, qb, pi, npairs, slot, kc) in enumerate(items):
                if idx + 1 < len(items):
                    issue_qk(idx + 1)
                if h + 1 < 8 and h + 1 not in heads and pi == 0 and qb == 0:
                    heads[h + 1] = load_head_B(h + 1)
                if h not in mstage:
                    mstage[h] = mst.next()
                qh, kh, vh, Bh = heads[h]
                pS, BpS = qk_out.pop(idx)
                pT, BpT = pTr.next()
                kt = slot * 16 + kc
                S.op("act", lambda e, pT=pT, pS=pS, slot=slot: e.activation(out=pT[:], in_=pS[:], func=AF.Exp,
                                                                             bias=ebias[:, slot:slot + 1], scale=SCALE_B),
                     reads=[BpS, B_const], writes=[BpT])
                if slot == 0 and kc >= 4 * qb:
                    mk = cmask(kc - 4 * qb)
                    S.op("pool", lambda e, pT=pT, mk=mk: e.tensor_tensor(
                        out=pT[:].rearrange("p (m q) -> p m q", m=2), in0=pT[:].rearrange("p (m q) -> p m q", m=2),
                        in1=mk.unsqueeze(1).broadcast_to([128, 2, 512]), op=ALU.mult), reads=[BpT, B_const], writes=[BpT])
                if pi == 0:
                    accs = accr.next()
                acc, Bacc = accs
                if pi % 3 == 2:
                    for m in range(2):
                        S.op("pe", lambda e, m=m, pT=pT, pi=pi: e.matmul(psDen[:, m * 512:(m + 1) * 512], lhsT=ones_bf[:],
                                                                         rhs=pT[:, m * 512:(m + 1) * 512], start=(pi == 2), stop=False),
                             reads=[BpT, B_const], writes=[B_psDen])
                elif pi == 0:
                    S.op("dve", lambda e, acc=acc, pT=pT: e.tensor_copy(out=acc[:], in_=pT[:]), reads=[BpT], writes=[Bacc])
                else:
                    S.op("dve", lambda e, acc=acc, pT=pT: e.tensor_tensor(out=acc[:], in0=acc[:], in1=pT[:], op=ALU.add),
                         reads=[BpT, Bacc], writes=[Bacc])
                for m in range(2):
                    S.op("pe", lambda e, m=m, pT=pT, kt=kt, vh=vh, pi=pi, npairs=npairs: e.matmul(
                        psO[m][:], lhsT=vh[:, kt, :], rhs=pT[:, m * 512:(m + 1) * 512], start=(pi == 0), stop=(pi == npairs - 1)),
                        reads=[BpT, Bh], writes=[B_psO[m]])
                for dd in [x for x in deferred if x[0] <= idx]:
                    deferred.remove(dd)
                    dd[1]()
                if pi == npairs - 1:
                    deferred.append((idx + 4, finalize_a(h, qb, accs)))
            for dd in deferred:
                dd[1]()
            S.flush()

        if upto < 3:
            raise StopBuild()
        with ExitStack() as ph:
            def sb(name, shape, dt):
                return ph.enter_context(nc.sbuf_tensor("s_" + name, shape, dt))

            def ps(name, shape, dt):
                return ph.enter_context(nc.psum_tensor("p_" + name, shape, dt))
            DILS = (1, 4, 16)
            ha = Ring([(sb(f"qa{i}", [128, NT], BF16), sb(f"ka{i}", [128, 2 * NT], BF16),
                        [sb(f"va{i}_{d}", [128, 32, 128], BF16) for d in DILS], Buf(f"ha{i}"), S.dma_ctr()) for i in range(2)])
            numacc = sb("numacc", [128, NT], F32)
            denacc = sb("denacc", [128, NT], F32)
            B_num, B_den = Buf("numacc"), Buf("denacc")
            pAr = [Ring([(sb(f"pA{hf}_{i}", [128, 512], BF16), Buf(f"pA{hf}_{i}")) for i in range(2)]) for hf in range(2)]
            sqa = sb("sqa", [128, 512], F32)
            rsa = sb("rsa", [128, 512], F32)
            B_sqa, B_rsa = Buf("sqa"), Buf("rsa")
            msa = Ring([(sb(f"msa{i}", [128, NT], BF16), Buf(f"msa{i}"), S.dma_ctr()) for i in range(2)])
            psSa = [Ring([(ps(f"psSa{hf}_{i}", [128, 512], F32), Buf(f"psSa{hf}_{i}")) for i in range(2)]) for hf in range(2)]
            psOa = Ring([(ps(f"psOa{i}", [128, 512], F32), Buf(f"psOa{i}")) for i in range(2)])
            psDa = Ring([(ps(f"psDa{i}", [128, 512], F32), Buf(f"psDa{i}")) for i in range(2)])

            def load_head_A(h):
                qh, kh, vhs, Bh, ch = ha.next()
                S.op("sp", lambda e: e.dma_start(out=qh[:], in_=qaT[h]), reads=[B_scr["qaT"]], writes=[Bh], dma_ctr=ch)
                S.op("sp", lambda e: e.dma_start(out=kh[:], in_=kaT[h]), reads=[B_scr["kaT"]], writes=[Bh], dma_ctr=ch)
                for di, d in enumerate(DILS):
                    nb = 32 // d
                    for r in range(d):
                        src = va[:, h * 128:(h + 1) * 128].rearrange("(b p d) n -> d p b n", p=128, d=d)[r]
                        S.op("sp", lambda e, vt=vhs[di], r=r, nb=nb, src=src: e.dma_start(out=vt[:, r * nb:(r + 1) * nb, :], in_=src),
                             reads=[B_scr["va"]], writes=[Bh], dma_ctr=ch)
                return qh, kh, vhs, Bh

            cur = load_head_A(0)
            for h in range(8):
                qh, kh, vhs, Bh = cur
                if h + 1 < 8:
                    cur = load_head_A(h + 1)
                ms, Bms, cms = msa.next()
                for di, d in enumerate(DILS):
                    nb = 32 // d
                    ob0 = nb // 2
                    if d == 1:
                        batches = [[(b, 0) for b in range(b0, b0 + 4)] for b0 in range(ob0, nb, 4)]
                    else:
                        batches = []
                        for b in range(ob0, nb):
                            for r0 in range(0, d, 4):
                                batches.append([(b, r) for r in range(r0, r0 + 4)])
                    for items in batches:
                        pss = [psSa[0].next(), psSa[1].next()]
                        pas = [pAr[0].next(), pAr[1].next()]
                        for hf in range(2):
                            pS, BpS = pss[hf]
                            for it, (b, r) in enumerate(items):
                                bk = b - 1 + hf
                                kcol = bk * 128 * d + r
                                qcol = b * 128 * d + r - NT
                                S.op("pe", lambda e, pS=pS, it=it, kcol=kcol, qcol=qcol, d=d, kh=kh, qh=qh: e.matmul(
                                    pS[:, it * 128:(it + 1) * 128], lhsT=kh[:, kcol:kcol + 127 * d + 1:d],
                                    rhs=qh[:, qcol:qcol + 127 * d + 1:d], start=True, stop=True), reads=[Bh], writes=[BpS])
                            pA, BpA = pas[hf]
                            segs = []
                            for it, (b, r) in enumerate(items):
                                bk = b - 1 + hf
                                segs.append(1 if bk < ob0 else 0)
                            s0 = 0
                            while s0 < 4:
                                s1 = s0
                                while s1 < 4 and segs[s1] == segs[s0]:
                                    s1 += 1
                                S.op("act", lambda e, pA=pA, pS=pS, s0=s0, s1=s1, bs=segs[s0]: e.activation(
                                    out=pA[:, s0 * 128:s1 * 128], in_=pS[:, s0 * 128:s1 * 128], func=AF.Exp,
                                    bias=ebias[:, bs:bs + 1], scale=SCALE_A), reads=[BpS, B_const], writes=[BpA])
                                s0 = s1
                            S.op("pool", lambda e, pA=pA, hf=hf: e.tensor_tensor(
                                out=pA[:].rearrange("p (i q) -> p i q", i=4), in0=pA[:].rearrange("p (i q) -> p i q", i=4),
                                in1=bandm(hf).unsqueeze(1).broadcast_to([128, 4, 128]), op=ALU.mult),
                                reads=[BpA, B_const], writes=[BpA])
                        pO, BpO = psOa.next()
                        pD, BpD = psDa.next()
                        for it, (b, r) in enumerate(items):
                            for hf in range(2):
                                bk = b - 1 + hf
                                pA, BpA = pas[hf]
                                S.op("pe", lambda e, pO=pO, pA=pA, it=it, di=di, vi=r * nb + bk, hf=hf, vhs=vhs: e.matmul(
                                    pO[:, it * 128:(it + 1) * 128], lhsT=vhs[di][:, vi, :], rhs=pA[:, it * 128:(it + 1) * 128],
                                    start=(hf == 0), stop=(hf == 1)), reads=[BpA, Bh], writes=[BpO])
                        for it, (b, r) in enumerate(items):
                            for hf in range(2):
                                pA, BpA = pas[hf]
                                S.op("pe", lambda e, pD=pD, pA=pA, it=it, hf=hf: e.matmul(
                                    pD[:, it * 128:(it + 1) * 128], lhsT=ones_bf[:], rhs=pA[:, it * 128:(it + 1) * 128],
                                    start=(hf == 0), stop=(hf == 1)), reads=[BpA, B_const], writes=[BpD])
                        b0, r0 = items[0]
                        if d == 1:
                            c0_ = b0 * 128 - NT

                            def dstv(t):
                                return t[:, c0_:c0_ + 512].rearrange("e (i p) -> e i p", i=4)
                        else:
                            c0_ = b0 * 128 * d - NT

                            def dstv(t, c0_=c0_, d=d, r0=r0):
                                return t[:, c0_:c0_ + 128 * d].rearrange("e (p r) -> e r p", r=d)[:, r0:r0 + 4, :]
                        first = (di == 0)
                        for (accT, Bacc, pX, BpX, eng) in [(numacc, B_num, pO, BpO, "dve"), (denacc, B_den, pD, BpD, "dve")]:
                            dv = dstv(accT)
                            src = pX[:].rearrange("e (i p) -> e i p", i=4)
                            if first:
                                S.op(eng, lambda e, dv=dv, src=src: e.tensor_copy(out=dv, in_=src), reads=[BpX], writes=[Bacc])
                            else:
                                S.op(eng, lambda e, dv=dv, src=src: e.tensor_tensor(out=dv, in0=src, in1=dv, op=ALU.add),
                                     reads=[BpX, Bacc], writes=[Bacc])
                S.op("dve", lambda e: e.reciprocal(out=denacc[:], in_=denacc[:]), reads=[B_den], writes=[B_den])
                S.op("dve", lambda e: e.tensor_tensor(out=numacc[:], in0=numacc[:], in1=denacc[:], op=ALU.mult),
                     reads=[B_num, B_den], writes=[B_num])
                for qb in range(4):
                    sl = slice(qb * 512, (qb + 1) * 512)
                    S.op("act", lambda e, sl=sl: e.activation(out=sqa[:], in_=numacc[:, sl], func=AF.Square), reads=[B_num], writes=[B_sqa])
                    pD, BpD = psDa.next()
                    S.op("pe", lambda e, pD=pD: e.matmul(pD[:], lhsT=ones_f[:], rhs=sqa[:], start=True, stop=True),
                         reads=[B_sqa, B_const], writes=[BpD])
                    S.op("act", lambda e, pD=pD: e.activation(out=rsa[:], in_=pD[:], func=AF.Sqrt, bias=EPS, scale=1.0 / HD),
                         reads=[BpD], writes=[B_rsa])
                    S.op("dve", lambda e: e.reciprocal(out=rsa[:], in_=rsa[:]), reads=[B_rsa], writes=[B_rsa])
                    S.op("dve", lambda e, ms=ms, sl=sl: e.scalar_tensor_tensor(out=ms[:, sl], in0=numacc[:, sl], scalar=gcol[:, 0:1],
                                                                               in1=rsa[:], op0=ALU.mult, op1=ALU.mult),
                         reads=[B_num, B_rsa, B_const], writes=[Bms])
                S.op("sp", lambda e, ms=ms, h=h: e.dma_start(out=mixs[h], in_=ms[:]), reads=[Bms], writes=[B_scr["mixs"]], dma_ctr=cms)
            S.flush()

        if upto < 4:
            raise StopBuild()
        with ExitStack() as ph:
            def sb(name, shape, dt):
                return ph.enter_context(nc.sbuf_tensor("s_" + name, shape, dt))

            def ps(name, shape, dt):
                return ph.enter_context(nc.psum_tensor("p_" + name, shape, dt))
            wo = sb("wo", [128, KC, D], BF16)
            B_wo = Buf("wo")
            c_wo = S.dma_ctr()
            mg = Ring([(sb(f"mg{i}", [128, 16, 512], BF16), Buf(f"mg{i}"), S.dma_ctr()) for i in range(2)])
            xt4 = Ring([(sb(f"x4_{i}", [128, D], F32), Buf(f"x4_{i}"), S.dma_ctr()) for i in range(2)])
            x1t = [sb(f"x1t{i}", [128, D], F32) for i in range(4)]
            B_x1t = [Buf(f"x1t{i}") for i in range(4)]
            c_x1t = [S.dma_ctr() for _ in range(4)]
            xn = sb("xn4", [128, 4, D], BF16)
            B_xn = Buf("xn4")
            ssq = sb("ssq4", [128, 4], F32)
            B_ssq = [Buf(f"ssq4_{i}") for i in range(4)]
            ssy = sb("ssy", [128, 4], F32)
            rsy = sb("rsy", [128, 1], F32)
            B_ssy, B_rsy = Buf("ssy"), Buf("rsy")
            junk = sb("junk4", [128, 512], BF16)
            B_junk = Buf("junk4")
            GGa = sb("GGa", [128, D], F32)
            c_gg = S.dma_ctr()
            S.op("sp", lambda e: e.dma_start(out=GGa[:], in_=ggs[0]), reads=[B_scr["ggs"]], writes=[B_const], dma_ctr=c_gg)
            h2g = Ring([(sb(f"h2g{i}", [128, KC, 512], BF16), [Buf(f"h2g{i}_{kc}") for kc in range(KC)], S.dma_ctr()) for i in range(1)])
            psY = ps("psY", [128, D], F32)
            B_psY = Buf("psY")
            psT = [ps(f"psT4_{i}", [128, 512], BF16) for i in range(2)]
            B_psT = [Buf("psT4_0"), Buf("psT4_1")]
            for cbk in range(4):
                src = w_out[:, cbk * 512:(cbk + 1) * 512].rearrange("(kc p) n -> p kc n", p=128)
                S.op("pool", lambda e, cbk=cbk, src=src: e.dma_start(out=wo[:, :, cbk * 512:(cbk + 1) * 512], in_=src),
                     writes=[B_wo], dma_ctr=c_wo)
            for g in range(4):
                mgt, Bmg, cmg = mg.next()
                S.op("sp", lambda e, mgt=mgt, g=g: e.dma_start(out=mgt[:], in_=mixs[:, :, g * 512:(g + 1) * 512].rearrange("h p t -> p h t")),
                     reads=[B_scr["mixs"]], writes=[Bmg], dma_ctr=cmg)
                for t in range(4):
                    r0 = g * 512 + t * 128
                    xt, Bxt, cxt = xt4.next()
                    S.op("sp", lambda e, xt=xt, r0=r0: e.dma_start(out=xt[:], in_=xs[0, r0:r0 + 128, :]), writes=[Bxt], dma_ctr=cxt)
                    for cbk in range(4):
                        for hc in range(16):
                            S.op("pe", lambda e, mgt=mgt, t=t, cbk=cbk, hc=hc: e.matmul(
                                psY[:, cbk * 512:(cbk + 1) * 512], lhsT=mgt[:, hc, t * 128:(t + 1) * 128],
                                rhs=wo[:, hc, cbk * 512:(cbk + 1) * 512], start=(hc == 0), stop=(hc == 15)),
                                reads=[Bmg, B_wo], writes=[B_psY])
                    for cbk in range(4):
                        S.op("act", lambda e, cbk=cbk: e.activation(out=junk[:, 0:512], in_=psY[:, cbk * 512:(cbk + 1) * 512],
                                                                    func=AF.Square, accum_out=ssy[:, cbk:cbk + 1]),
                             reads=[B_psY], writes=[B_junk, B_ssy])
                    S.op("dve", lambda e: e.reduce_sum(out=rsy[:], in_=ssy[:], axis=AX.X), reads=[B_ssy], writes=[B_rsy])
                    S.op("act", lambda e: e.activation(out=rsy[:], in_=rsy[:], func=AF.Sqrt, bias=EPS, scale=1.0 / D),
                         reads=[B_rsy], writes=[B_rsy])
                    S.op("dve", lambda e: e.reciprocal(out=rsy[:], in_=rsy[:]), reads=[B_rsy], writes=[B_rsy])
                    x1, Bx1 = x1t[t], B_x1t[t]
                    for cbk in range(4):
                        sl = slice(cbk * 512, (cbk + 1) * 512)
                        S.op("dve", lambda e, x1=x1, sl=sl: e.scalar_tensor_tensor(out=x1[:, sl], in0=psY[:, sl], scalar=rsy[:, 0:1],
                                                                                   in1=GGa[:, sl], op0=ALU.mult, op1=ALU.mult),
                             reads=[B_psY, B_rsy, B_const], writes=[Bx1])
                    S.op("pool", lambda e, x1=x1, xt=xt: e.tensor_tensor(out=x1[:], in0=x1[:], in1=xt[:], op=ALU.add),
                         reads=[Bx1, Bxt], writes=[Bx1])
                    S.op("sp", lambda e, x1=x1, r0=r0: e.dma_start(out=x1s[r0:r0 + 128, :], in_=x1[:]), reads=[Bx1],
                         writes=[B_scr["x1s"]], dma_ctr=c_x1t[t])
                hg, Bhg, chg = h2g.next()
                norm_tiles_to_hT(lambda t: (x1t[t][:], B_x1t[t]), ssq, B_ssq, xn, B_xn, psT, B_psT,
                                 lambda kc, hg=hg: hg[:, kc, :], Bhg, 2, g)
                S.op("sp", lambda e, hg=hg, g=g: e.dma_start(out=h2s[g], in_=hg[:]), reads=Bhg, writes=[B_scr["h2s"]], dma_ctr=chg)
            S.flush()

        if upto < 5:
            raise StopBuild()
        with ExitStack() as ph:
            def sb(name, shape, dt):
                return ph.enter_context(nc.sbuf_tensor("s_" + name, shape, dt))

            def ps(name, shape, dt):
                return ph.enter_context(nc.psum_tensor("p_" + name, shape, dt))
            h2 = sb("h2", [128, KC, 512], BF16)
            B_h2 = Buf("h2")
            c_h2 = S.dma_ctr()
            actT = sb("actT", [128, NFC, 512], BF16)
            B_actT = Buf("actT")
            wgu = Ring([(sb(f"wg{i}", [128, KC, 256], BF16), sb(f"wu{i}", [128, KC, 256], BF16), Buf(f"wgu{i}"), S.dma_ctr())
                        for i in range(3)])
            wd = Ring([(sb(f"wd{i}", [128, 11, 512], BF16), Buf(f"wd{i}"), S.dma_ctr()) for i in range(3)])
            fbuf = sb("fbuf", [128, 4, D], F32)
            B_fbuf = [Buf(f"fbuf{i}") for i in range(4)]
            ssf = sb("ssf", [128, 4, 4], F32)
            B_ssf = Buf("ssf")
            rsf = sb("rsf", [128, 4], F32)
            B_rsf = Buf("rsf")
            sg = Ring([(sb(f"sg{i}", [128, 512], F32), Buf(f"sg{i}")) for i in range(2)])
            junk = sb("junk5", [128, 512], BF16)
            B_junk = Buf("junk5")
            x1r = Ring([(sb(f"x1r{i}", [128, D], F32), Buf(f"x1r{i}"), S.dma_ctr()) for i in range(1)])
            psG = Ring([(ps(f"psG{i}", [128, 512], F32), Buf(f"psG{i}")) for i in range(2)])
            psU = Ring([(ps(f"psU{i}", [128, 512], F32), Buf(f"psU{i}")) for i in range(2)])
            psF = [ps(f"psF{i}", [128, 512], F32) for i in range(4)]
            B_psF = [Buf(f"psF{i}") for i in range(4)]
            c_y = S.dma_ctr()
            GGf = sb("GGf", [128, D], F32)
            c_gg = S.dma_ctr()
            S.op("sp", lambda e: e.dma_start(out=GGf[:], in_=ggs[1]), reads=[B_scr["ggs"]], writes=[B_const], dma_ctr=c_gg)
            wjobs = []
            for g in range(4):
                for fb in range(NFC // 2):
                    wjobs.append(("gu", fb))
                for cbk in range(4):
                    for q4 in range(4):
                        wjobs.append(("d", cbk, q4))
            wloaded = {}

            def load_wj(n):
                if n >= len(wjobs):
                    return
                jb = wjobs[n]
                if jb[0] == "gu":
                    fb = jb[1]
                    wg, wu, Bw, cw_ = wgu.next()
                    srcg = w_gate[:, fb * 256:(fb + 1) * 256].rearrange("(kc p) n -> p kc n", p=128)
                    srcu = w_up[:, fb * 256:(fb + 1) * 256].rearrange("(kc p) n -> p kc n", p=128)
                    S.op("pool", lambda e, wg=wg, srcg=srcg: e.dma_start(out=wg[:], in_=srcg), writes=[Bw], dma_ctr=cw_)
                    S.op("pool", lambda e, wu=wu, srcu=srcu: e.dma_start(out=wu[:], in_=srcu), writes=[Bw], dma_ctr=cw_)
                    wloaded[n] = (wg, wu, Bw)
                else:
                    _, cbk, q4 = jb
                    wdt, Bwd, cwd = wd.next()
                    src = w_down[q4 * 11 * 128:(q4 + 1) * 11 * 128, cbk * 512:(cbk + 1) * 512].rearrange("(c p) n -> p c n", p=128)
                    S.op("pool", lambda e, wdt=wdt, src=src: e.dma_start(out=wdt[:], in_=src), writes=[Bwd], dma_ctr=cwd)
                    wloaded[n] = (wdt, Bwd)
            load_wj(0)
            load_wj(1)
            wn = 0
            for g in range(4):
                S.op("sp", lambda e, g=g: e.dma_start(out=h2[:], in_=h2s[g]), reads=[B_scr["h2s"]], writes=[B_h2], dma_ctr=c_h2)
                for fb in range(NFC // 2):
                    wg, wu, Bw = wloaded.pop(wn)
                    wn += 1
                    load_wj(wn + 1)
                    for j in range(2):
                        fc = fb * 2 + j
                        pG, BpG = psG.next()
                        pU, BpU = psU.next()
                        for kc in range(KC):
                            S.op("pe", lambda e, pG=pG, wg=wg, kc=kc, j=j: e.matmul(pG[:], lhsT=wg[:, kc, j * 128:(j + 1) * 128],
                                                                                    rhs=h2[:, kc, :], start=(kc == 0), stop=(kc == KC - 1)),
                                 reads=[Bw, B_h2], writes=[BpG])
                        for kc in range(KC):
                            S.op("pe", lambda e, pU=pU, wu=wu, kc=kc, j=j: e.matmul(pU[:], lhsT=wu[:, kc, j * 128:(j + 1) * 128],
                                                                                    rhs=h2[:, kc, :], start=(kc == 0), stop=(kc == KC - 1)),
                                 reads=[Bw, B_h2], writes=[BpU])
                        sgt, Bsg = sg.next()
                        S.op("act", lambda e, sgt=sgt, pG=pG: e.activation(out=sgt[:], in_=pG[:], func=AF.Silu), reads=[BpG], writes=[Bsg])
                        S.op("dve", lambda e, sgt=sgt, pU=pU, fc=fc: e.tensor_tensor(out=actT[:, fc, :], in0=sgt[:], in1=pU[:], op=ALU.mult),
                             reads=[Bsg, BpU], writes=[B_actT])
                for cbk in range(4):
                    for q4 in range(4):
                        wdt, Bwd = wloaded.pop(wn)
                        wn += 1
                        load_wj(wn + 1)
                        for c in range(11):
                            fc = q4 * 11 + c
                            for t in range(4):
                                S.op("pe", lambda e, t=t, fc=fc, c=c, wdt=wdt: e.matmul(
                                    psF[t][:], lhsT=actT[:, fc, t * 128:(t + 1) * 128], rhs=wdt[:, c, :],
                                    start=(fc == 0), stop=(fc == NFC - 1)), reads=[B_actT, Bwd], writes=[B_psF[t]])
                    for t in range(4):
                        S.op("act", lambda e, t=t, cbk=cbk: e.activation(out=junk[:], in_=psF[t][:], func=AF.Square,
                                                                         accum_out=ssf[:, t, cbk:cbk + 1]),
                             reads=[B_psF[t]], writes=[B_junk, B_ssf])
                        S.op("dve", lambda e, t=t, cbk=cbk: e.tensor_copy(out=fbuf[:, t, cbk * 512:(cbk + 1) * 512], in_=psF[t][:]),
                             reads=[B_psF[t]], writes=[B_fbuf[t]])
                S.op("dve", lambda e: e.reduce_sum(out=rsf[:], in_=ssf[:], axis=AX.X), reads=[B_ssf], writes=[B_rsf])
                S.op("act", lambda e: e.activation(out=rsf[:], in_=rsf[:], func=AF.Sqrt, bias=EPS, scale=1.0 / D),
                     reads=[B_rsf], writes=[B_rsf])
                S.op("dve", lambda e: e.reciprocal(out=rsf[:], in_=rsf[:]), reads=[B_rsf], writes=[B_rsf])
                for t in range(4):
                    r0 = g * 512 + t * 128
                    x1, Bx1, cx1 = x1r.next()
                    S.op("sp", lambda e, x1=x1, r0=r0: e.dma_start(out=x1[:], in_=x1s[r0:r0 + 128, :]), reads=[B_scr["x1s"]],
                         writes=[Bx1], dma_ctr=cx1)
                    S.op("dve", lambda e, t=t: e.scalar_tensor_tensor(out=fbuf[:, t, :], in0=fbuf[:, t, :], scalar=rsf[:, t:t + 1],
                                                                      in1=GGf[:], op0=ALU.mult, op1=ALU.mult),
                         reads=[B_fbuf[t], B_rsf, B_const], writes=[B_fbuf[t]])
                    S.op("pool", lambda e, t=t, x1=x1: e.tensor_tensor(out=fbuf[:, t, :], in0=fbuf[:, t, :], in1=x1[:], op=ALU.add),
                         reads=[B_fbuf[t], Bx1], writes=[B_fbuf[t]])
                    S.op("sp", lambda e, t=t, r0=r0: e.dma_start(out=y[r0:r0 + 128, :], in_=fbuf[:, t, :]), reads=[B_fbuf[t]],
                         writes=[B_scr["y"]], dma_ctr=c_y)
            S.op("sp", lambda e: None, reads=[B_scr["y"]], noinst=True)
            S.flush()
    except StopBuild:
        pass
    S = build.S
    sched_finish(S)
    build.n_instr = S.n_instr
    build.nsem = S.nvsem
    return nc


def _consts():
    bf = ml_dtypes.bfloat16
    ident = np.eye(128, dtype=np.float32)
    pA = np.zeros((128, 128), np.float32)
    for i in range(128):
        pA[(i + 64) % 128, i] = 1.0
    pB = np.zeros((128, 128), np.float32)
    for i in range(128):
        blk, w = divmod(i, 64)
        pB[blk * 64 + (w + 32) % 64, i] = 1.0
    k = np.arange(128)[:, None]
    q = np.arange(512)[None, :]
    cm = [(q >= (m * 128 + k)).astype(np.float32) for m in range(4)]
    q1 = np.arange(128)[None, :]
    band = [(k >= q1).astype(np.float32), (k <= q1).astype(np.float32)]
    cb16 = np.concatenate([ident, pA, pB] + cm + band, axis=1).astype(bf)
    rc = np.zeros((128, 4), np.float32)
    invA = (10000.0 ** (-(np.arange(64, dtype=np.float32)) / np.float32(64))).astype(np.float32)
    invB = (10000.0 ** (-(np.arange(32, dtype=np.float32)) / np.float32(32))).astype(np.float32)
    p = np.arange(128)
    rc[:, 0] = invA[p % 64]
    rc[:, 1] = invB[p % 32]
    rc[:, 2] = np.where(p < 64, -1.0, 1.0)
    rc[:, 3] = np.where((p % 64) < 32, -1.0, 1.0)
    return cb16, rc


def make_in_maps(x, c, positions, w_ada, b_ada, g_pre_attn, w_in, g_out_a, lambda_q1, lambda_k1, lambda_q2, lambda_k2,
                 g_subln_b, w_out, g_post_attn, g_pre_ffn, w_gate, w_up, w_down, g_post_ffn):
    f32 = np.float32
    x = np.asarray(x, f32)
    c = np.asarray(c, f32)
    positions = np.asarray(positions, np.int32)
    cb16, rc = _consts()
    w_in0 = np.asarray(w_in, f32)[0]
    perm = np.arange(1024).reshape(2, 8, 64).transpose(1, 0, 2).reshape(-1)
    w_in_p = np.concatenate([w_in0[:, 0:3072], w_in0[:, 3072:4096][:, perm], w_in0[:, 4096:5120][:, perm], w_in0[:, 5120:6144]],
                            axis=1)
    w_in_p = np.ascontiguousarray(w_in_p)
    shared = {
        "w_ada": np.ascontiguousarray(np.asarray(w_ada, f32)[0]),
        "b_ada": np.ascontiguousarray(np.asarray(b_ada, f32)[0][None, :]),
        "gpa": np.ascontiguousarray(np.asarray(g_pre_attn, f32)[0].reshape(KC, 128).T),
        "gpf": np.ascontiguousarray(np.asarray(g_pre_ffn, f32)[0].reshape(KC, 128).T),
        "gposta": np.ascontiguousarray(np.asarray(g_post_attn, f32)[0][None, :]),
        "gpostf": np.ascontiguousarray(np.asarray(g_post_ffn, f32)[0][None, :]),
        "w_in": w_in_p,
        "gcol": np.ascontiguousarray(np.stack([np.asarray(g_out_a, f32)[0], np.asarray(g_subln_b, f32)[0]], axis=1)),
        "lamv": np.ascontiguousarray(np.concatenate([np.asarray(a, f32)[0] for a in (lambda_q1, lambda_k1, lambda_q2, lambda_k2)])[None, :]),
        "w_out": np.ascontiguousarray(np.asarray(w_out, f32)[0]),
        "w_gate": np.ascontiguousarray(np.asarray(w_gate, f32)[0]),
        "w_up": np.ascontiguousarray(np.asarray(w_up, f32)[0]),
        "w_down": np.ascontiguousarray(np.asarray(w_down, f32)[0]),
        "rc": rc,
        "cb16": cb16,
    }
    in_maps = []
    for core in range(8):
        b, j = divmod(core, 4)
        chunks = [(j - s) % 4 for s in range(4)]
        xsl = np.stack([x[b, ch * NT:(ch + 1) * NT] for ch in chunks], axis=0)
        pos = np.stack([positions[b, ch * NT:(ch + 1) * NT] for ch in chunks], axis=0)[:, None, :]
        eb = np.zeros((128, 4), f32)
        for s in range(4):
            if s > j:
                eb[:, s] = NEG
        m = dict(shared)
        m["xs"] = np.ascontiguousarray(xsl)
        m["posi"] = np.ascontiguousarray(pos.astype(np.int32))
        m["ebias"] = eb
        m["cT"] = np.ascontiguousarray(c[b].reshape(KC, 128).T)
        in_maps.append(m)
    return in_maps


_NC_CACHE = {}


def kernel(**inputs):
    in_maps = make_in_maps(**inputs)
    if "nc" not in _NC_CACHE:
        _NC_CACHE["nc"] = build(debug=False)
    nc = _NC_CACHE["nc"]
    res = run_bass_kernel_spmd(nc, in_maps, core_ids=list(range(8)))
    out = np.empty((2, 4 * NT, D), np.float32)
    for core in range(8):
        b, j = divmod(core, 4)
        out[b, j * NT:(j + 1) * NT] = np.asarray(res.results[core]["y"], np.float32)
    return out
```

```python
import math
import os
from contextlib import ExitStack

import numpy as np
import ml_dtypes

import concourse.bass as bass
import concourse.mybir as mybir
from concourse.bass_utils import run_bass_kernel_spmd

F32 = mybir.dt.float32
BF16 = mybir.dt.bfloat16
I32 = mybir.dt.int32
AF = mybir.ActivationFunctionType
ALU = mybir.AluOpType
AX = mybir.AxisListType

SEM_LIMIT = 32000
SAME_ENGINE_SYNC = True


class Buf:
    __slots__ = ("name", "writers", "readers")

    def __init__(self, name=""):
        self.name = name
        self.writers = {}
        self.readers = {}


class SemCtr:
    def __init__(self, S):
        self.S = S
        self.vid = S.new_vsem()
        self.count = 0
        self.hist = {}

    def bump(self, inc):
        if self.count + inc > SEM_LIMIT:
            self.hist[self.vid] = self.count
            self.vid = self.S.new_vsem()
            self.count = 0
        self.count += inc
        return self.vid, self.count

    def current_for(self, vid):
        return self.count if vid == self.vid else self.hist[vid]


class Ev:
    __slots__ = ("eng", "fn", "deps", "sem", "val", "flag", "is_dma", "ctr", "phase", "noinst")


ENGS = ("pe", "act", "dve", "pool", "sp")


class Sched:
    def __init__(self, nc, outer):
        self.nc = nc
        self.outer = outer
        self.prog = {e: [] for e in ENGS}
        self.nvsem = 0
        self.phase = 0
        self.sems = []
        self.eng_ctr = {e: SemCtr(self) for e in ENGS}
        self.waited = {e: {} for e in ENGS}
        self.barrier = []
        self.phase_dmas = {}
        self.n_instr = 0

    def new_vsem(self):
        self.nvsem += 1
        return self.nvsem - 1

    def dma_ctr(self):
        return SemCtr(self)

    def op(self, eng, fn, reads=(), writes=(), dma_ctr=None, noinst=False, carry=False):
        ev = Ev()
        ev.noinst = noinst
        ev.eng = eng
        ev.fn = fn
        ev.is_dma = dma_ctr is not None
        ev.ctr = dma_ctr
        ev.flag = ev.is_dma
        ev.sem = None
        ev.val = None
        ev.phase = self.phase
        deps = {}
        for b in reads:
            for w in b.writers.values():
                deps[id(w)] = w
            if b.name.startswith("ps"):
                for k_, r in b.readers.items():
                    if k_ != eng:
                        deps[id(r)] = r
        for b in writes:
            for w in b.writers.values():
                deps[id(w)] = w
            for r in b.readers.values():
                deps[id(r)] = r
        dl = []
        for d in deps.values():
            if d is ev:
                continue
            if (not d.is_dma) and d.phase < self.phase:
                continue
            if (not d.is_dma) and d.eng == eng:
                if eng == "pe" or not SAME_ENGINE_SYNC:
                    continue
            if d.is_dma:
                dl.append((d, d.ctr.current_for(d.sem)))
            else:
                d.flag = True
                dl.append((d, None))
        ev.deps = dl
        if ev.is_dma:
            ev.sem, ev.val = dma_ctr.bump(16)
            if not carry:
                self.phase_dmas[ev.sem] = ev.val
        key = ("d", id(dma_ctr)) if ev.is_dma else eng
        for b in reads:
            b.readers[key] = ev
        for b in writes:
            b.writers[key] = ev
        self.prog[eng].append(ev)
        return ev

    def flush(self):
        nc = self.nc
        prog = self.prog
        new_barrier = []
        for e in ENGS:
            last = None
            for ev in prog[e]:
                if not ev.is_dma and not ev.noinst:
                    last = ev
            if last is not None:
                last.flag = True
        for e in ENGS:
            ctr = self.eng_ctr[e]
            lastev = None
            for ev in prog[e]:
                if ev.is_dma:
                    continue
                if ev.flag and not ev.noinst:
                    ev.sem, ev.val = ctr.bump(1)
                    lastev = ev
            if lastev is not None:
                new_barrier.append((lastev.sem, lastev.val))
        while len(self.sems) < self.nvsem:
            self.sems.append(self.outer.enter_context(nc.semaphore(f"s{len(self.sems)}")))
        sems = self.sems
        old_barrier = self.barrier

        def run(engname):
            def body(eng):
                waited = self.waited[engname]
                for vid, val in old_barrier:
                    if waited.get(vid, 0) < val:
                        eng.wait_ge(sems[vid], val)
                        waited[vid] = val
                for ev in prog[engname]:
                    for d, snap in ev.deps:
                        vid = d.sem
                        val = snap if d.is_dma else d.val
                        if waited.get(vid, 0) < val:
                            eng.wait_ge(sems[vid], val)
                            waited[vid] = val
                    ins = ev.fn(eng)
                    self.n_instr += 1
                    if ev.flag and not ev.noinst:
                        ins.then_inc(sems[ev.sem], 16 if ev.is_dma else 1)
            return body

        with nc.Block() as block:
            block.sync(run("sp"))
            block.scalar(run("act"))
            block.vector(run("dve"))
            block.gpsimd(run("pool"))
            block.tensor(run("pe"))
        new_barrier.extend(self.phase_dmas.items())
        self.phase_dmas = {}
        self.barrier = new_barrier
        self.prog = {e: [] for e in ENGS}
        self.phase += 1


def sched_finish(S):
    nc = S.nc
    sems = S.sems
    items = list(S.barrier)

    def body(eng):
        waited = S.waited["sp"]
        for vid, val in items:
            if waited.get(vid, 0) < val:
                eng.wait_ge(sems[vid], val)
                waited[vid] = val

    with nc.Block() as block:
        block.sync(body)


class Ring:
    def __init__(self, items):
        self.items = items
        self.i = 0

    def next(self):
        it = self.items[self.i % len(self.items)]
        self.i += 1
        return it


D = 2048
KC = 16
NT = 2048
NSLOT = 4
DFF = 5632
NFC = DFF // 128
HD = 128
SCALE_A = HD ** -0.5
SCALE_B = 64 ** -0.5
EPS = 1e-6
LAMBDA_INIT = 0.8 - 0.6 * math.exp(-0.3 * 0)
NEG = -30000.0
INV2PI = float(np.float32(1.0 / (2 * np.pi)))
MAGIC = 12582912.0
C1 = 6.28125
C2 = float(np.float32(2 * np.pi - 6.28125))
HALFPI = float(np.pi / 2)
PI_SAFE = float(np.nextafter(np.float32(np.pi), np.float32(0)))


class StopBuild(Exception):
    pass


def build(debug=False, upto=9):
    nc = bass.Bass("TRN2", target_bir_lowering=False)
    dk = "ExternalOutput" if debug else "Internal"

    def din(name, shape, dt):
        return nc.dram_tensor(name, shape, dt, kind="ExternalInput").ap()

    def dscr(name, shape, dt):
        return nc.dram_tensor(name, shape, dt, kind=dk).ap()

    xs = din("xs", [NSLOT, NT, D], F32)
    posi_d = din("posi", [NSLOT, 1, NT], I32)
    ebias_d = din("ebias", [128, 4], F32)
    cT_d = din("cT", [128, KC], F32)
    w_ada = din("w_ada", [D, 6 * D], F32)
    b_ada = din("b_ada", [1, 6 * D], F32)
    gpa_d = din("gpa", [128, KC], F32)
    gpf_d = din("gpf", [128, KC], F32)
    gposta_d = din("gposta", [1, D], F32)
    gpostf_d = din("gpostf", [1, D], F32)
    w_in = din("w_in", [D, 6144], F32)
    gcol_d = din("gcol", [128, 2], F32)
    lamv_d = din("lamv", [1, 256], F32)
    w_out = din("w_out", [D, D], F32)
    w_gate = din("w_gate", [D, DFF], F32)
    w_up = din("w_up", [D, DFF], F32)
    w_down = din("w_down", [DFF, D], F32)
    rc_d = din("rc", [128, 4], F32)
    cb16_d = din("cb16", [128, 3 * 128 + 4 * 512 + 2 * 128], BF16)
    y = nc.dram_tensor("y", [NT, D], F32, kind="ExternalOutput").ap()

    qaT = dscr("qaT", [8, 128, NT], BF16)
    kaT = dscr("kaT", [8, 128, 2 * NT], BF16)
    va = dscr("va", [2 * NT, 1024], BF16)
    qbT = dscr("qbT", [8, 128, NT], BF16)
    kbT = dscr("kbT", [8, 128, 4 * NT], BF16)
    vb = dscr("vb", [4 * NT, 1024], BF16)
    mixs = dscr("mixs", [16, 128, NT], BF16)
    x1s = dscr("x1s", [NT, D], F32)
    h2s = dscr("h2s", [4, 128, KC, 512], BF16)
    ggs = dscr("ggs", [2, 128, D], F32)

    try:
      with ExitStack() as outer:
        S = Sched(nc, outer)
        build.S = S

        def sbo(name, shape, dt):
            return outer.enter_context(nc.sbuf_tensor("s_" + name, shape, dt))

        ident_t = sbo("ident", [128, 128], BF16)
        pswA_t = sbo("pswA", [128, 128], BF16)
        pswB_t = sbo("pswB", [128, 128], BF16)
        cmask_t = sbo("cmask", [128, 4, 512], BF16)
        band_t = sbo("band", [128, 2, 128], BF16)
        ident = ident_t[:]
        pswA = pswA_t[:]
        pswB = pswB_t[:]

        def cmask(m):
            return cmask_t[:, m, :]

        def bandm(hf):
            return band_t[:, hf, :]
        ones_bf = sbo("ones_bf", [128, 128], BF16)
        ones_f = sbo("ones_f", [128, 128], F32)
        ebias = sbo("ebias", [128, 4], F32)
        rc = sbo("rc", [128, 4], F32)
        halfpi = sbo("halfpi", [128, 1], F32)
        modv = sbo("modv", [128, 4, KC], F32)
        gcol = sbo("gcol", [128, 2], F32)
        gsub08 = sbo("gsub08", [128, 1], F32)
        neglam = sbo("neglam", [128, 1], F32)
        B_const = Buf("const")
        B_scr = {k: Buf(k) for k in ["qaT", "kaT", "va", "qbT", "kbT", "vb", "mixs", "x1s", "h2s", "y", "ggs"]}

        with ExitStack() as ph:
            def sb(name, shape, dt):
                return ph.enter_context(nc.sbuf_tensor("s_" + name, shape, dt))

            def ps(name, shape, dt):
                return ph.enter_context(nc.psum_tensor("p_" + name, shape, dt))
            cT = sb("cT", [128, KC], F32)
            GGa = sb("GGa0", [128, D], F32)
            GGf = sb("GGf0", [128, D], F32)
            scT = sb("scT", [128, KC], BF16)
            brow = sb("brow", [1, 6 * D], F32)
            modrow = sb("modrow", [1, 6 * D], F32)
            gpa = sb("gpa", [128, KC], F32)
            gpf = sb("gpf", [128, KC], F32)
            lamv = sb("lamv", [128, 256], F32)
            lprod = sb("lprod", [128, 128], F32)
            lsum = sb("lsum", [128, 2], F32)
            wbl = [sb(f"wbl{i}", [128, KC, 512], BF16) for i in range(2)]
            psM = ps("psM", [1, 512], F32)
            psC = ps("psC", [128, 4, KC], F32)
            psR = [ps(f"psR{i}", [128, 512], F32) for i in range(2)]
            B_cT, B_scT, B_brow, B_modrow, B_g, B_lam, B_lp, B_ls = (Buf(n) for n in
                                                                       ["cT", "scT", "brow", "modrow", "g", "lam", "lp", "ls"])
            B_wbl = [Buf("wbl0"), Buf("wbl1")]
            B_psM, B_psC = Buf("psM"), Buf("psC")
            B_psR = [Buf("psR0"), Buf("psR1")]
            B_GGa, B_GGf = Buf("GGa"), Buf("GGf")
            c0 = S.dma_ctr()
            for (dst, src) in [(ident, cb16_d[:, 0:128]), (pswA, cb16_d[:, 128:256]), (pswB, cb16_d[:, 256:384]),
                               (cmask_t[:], cb16_d[:, 384:384 + 2048].rearrange("p (m q) -> p m q", m=4)),
                               (band_t[:], cb16_d[:, 384 + 2048:384 + 2048 + 256].rearrange("p (m q) -> p m q", m=2)),
                               (ebias[:], ebias_d), (rc[:], rc_d), (gcol[:], gcol_d)]:
                S.op("sp", lambda e, dst=dst, src=src: e.dma_start(out=dst, in_=src), writes=[B_const], dma_ctr=c0)
            c1 = S.dma_ctr()
            S.op("sp", lambda e: e.dma_start(out=cT[:], in_=cT_d), writes=[B_cT], dma_ctr=c1)
            S.op("sp", lambda e: e.dma_start(out=brow[:], in_=b_ada), writes=[B_brow], dma_ctr=c1)
            S.op("sp", lambda e: e.dma_start(out=gpa[:], in_=gpa_d), writes=[B_g], dma_ctr=c1)
            S.op("sp", lambda e: e.dma_start(out=gpf[:], in_=gpf_d), writes=[B_g], dma_ctr=c1)
            S.op("sp", lambda e: e.dma_start(out=lamv[:], in_=lamv_d.broadcast_to([128, 256])), writes=[B_lam], dma_ctr=c1)
            c2 = S.dma_ctr()
            S.op("sp", lambda e: e.dma_start(out=GGa[:], in_=gposta_d.broadcast_to([128, D])), writes=[B_GGa], dma_ctr=c2)
            S.op("sp", lambda e: e.dma_start(out=GGf[:], in_=gpostf_d.broadcast_to([128, D])), writes=[B_GGf], dma_ctr=c2)
            S.op("pool", lambda e: e.memset(ones_bf[:], 1.0), writes=[B_const])
            S.op("pool", lambda e: e.memset(ones_f[:], 1.0), writes=[B_const])
            S.op("pool", lambda e: e.memset(halfpi[:], HALFPI), writes=[B_const])
            S.op("act", lambda e: e.activation(out=scT[:], in_=cT[:], func=AF.Silu), reads=[B_cT], writes=[B_scT])
            cw = [S.dma_ctr(), S.dma_ctr()]
            def load_ada(blk):
                i = blk % 2
                src = w_ada[:, blk * 512:(blk + 1) * 512].rearrange("(kc p) n -> p kc n", p=128)
                S.op("pool", lambda e, i=i, src=src: e.dma_start(out=wbl[i][:], in_=src), writes=[B_wbl[i]], dma_ctr=cw[i])
            load_ada(0)
            for blk in range(24):
                i = blk % 2
                if blk + 1 < 24:
                    load_ada(blk + 1)
                for kc in range(KC):
                    S.op("pe", lambda e, i=i, kc=kc: e.matmul(psM[:], lhsT=scT[:, kc:kc + 1], rhs=wbl[i][:, kc, :],
                                                             start=(kc == 0), stop=(kc == KC - 1)),
                         reads=[B_scT, B_wbl[i]], writes=[B_psM])
                S.op("dve", lambda e, blk=blk: e.tensor_tensor(out=modrow[0:1, blk * 512:(blk + 1) * 512], in0=psM[:],
                                                               in1=brow[0:1, blk * 512:(blk + 1) * 512], op=ALU.add),
                     reads=[B_psM, B_brow], writes=[B_modrow])
            for vi, sec in enumerate([1, 0, 4, 3]):
                for j in range(KC):
                    o = sec * D + j * 128
                    S.op("pe", lambda e, vi=vi, j=j, o=o: e.matmul(psC[:, vi, j:j + 1], lhsT=modrow[0:1, o:o + 128],
                                                                    rhs=ones_f[0:1, 0:1], start=True, stop=True),
                         reads=[B_modrow, B_const], writes=[B_psC])
            B_modv = B_const
            S.op("dve", lambda e: e.tensor_scalar(out=modv[:, 0, :], in0=psC[:, 0, :], scalar1=1.0, scalar2=None, op0=ALU.add),
                 reads=[B_psC], writes=[B_modv])
            S.op("dve", lambda e: e.tensor_tensor(out=modv[:, 0, :], in0=modv[:, 0, :], in1=gpa[:], op=ALU.mult),
                 reads=[B_modv, B_g], writes=[B_modv])
            S.op("dve", lambda e: e.tensor_copy(out=modv[:, 1, :], in_=psC[:, 1, :]), reads=[B_psC], writes=[B_modv])
            S.op("dve", lambda e: e.tensor_scalar(out=modv[:, 2, :], in0=psC[:, 2, :], scalar1=1.0, scalar2=None, op0=ALU.add),
                 reads=[B_psC], writes=[B_modv])
            S.op("dve", lambda e: e.tensor_tensor(out=modv[:, 2, :], in0=modv[:, 2, :], in1=gpf[:], op=ALU.mult),
                 reads=[B_modv, B_g], writes=[B_modv])
            S.op("dve", lambda e: e.tensor_copy(out=modv[:, 3, :], in_=psC[:, 3, :]), reads=[B_psC], writes=[B_modv])
            k = 0
            for (GG, Bg, sec) in [(GGa, B_GGa, 2), (GGf, B_GGf, 5)]:
                for cbk in range(4):
                    o = sec * D + cbk * 512
                    pr, Bpr = psR[k % 2], B_psR[k % 2]
                    k += 1
                    S.op("pe", lambda e, pr=pr, o=o: e.matmul(pr[:], lhsT=ones_f[0:1, :], rhs=modrow[0:1, o:o + 512],
                                                              start=True, stop=True),
                         reads=[B_modrow, B_const], writes=[Bpr])
                    S.op("dve", lambda e, GG=GG, pr=pr, cbk=cbk: e.tensor_tensor(out=GG[:, cbk * 512:(cbk + 1) * 512], in0=pr[:],
                                                                                 in1=GG[:, cbk * 512:(cbk + 1) * 512], op=ALU.mult),
                         reads=[Bpr, Bg], writes=[Bg])
            c_ggw = S.dma_ctr()
            S.op("sp", lambda e: e.dma_start(out=ggs[0], in_=GGa[:]), reads=[B_GGa], writes=[B_scr["ggs"]], dma_ctr=c_ggw)
            S.op("sp", lambda e: e.dma_start(out=ggs[1], in_=GGf[:]), reads=[B_GGf], writes=[B_scr["ggs"]], dma_ctr=c_ggw)
            S.op("dve", lambda e: e.tensor_tensor(out=lprod[:, 0:64], in0=lamv[:, 0:64], in1=lamv[:, 64:128], op=ALU.mult),
                 reads=[B_lam], writes=[B_lp])
            S.op("dve", lambda e: e.tensor_tensor(out=lprod[:, 64:128], in0=lamv[:, 128:192], in1=lamv[:, 192:256], op=ALU.mult),
                 reads=[B_lam], writes=[B_lp])
            S.op("dve", lambda e: e.reduce_sum(out=lsum[:, 0:1], in_=lprod[:, 0:64], axis=AX.X), reads=[B_lp], writes=[B_ls])
            S.op("dve", lambda e: e.reduce_sum(out=lsum[:, 1:2], in_=lprod[:, 64:128], axis=AX.X), reads=[B_lp], writes=[B_ls])
            S.op("act", lambda e: e.activation(out=lsum[:], in_=lsum[:], func=AF.Exp), reads=[B_ls], writes=[B_ls])
            S.op("dve", lambda e: e.tensor_tensor(out=neglam[:], in0=lsum[:, 1:2], in1=lsum[:, 0:1], op=ALU.subtract),
                 reads=[B_ls], writes=[B_const])
            S.op("dve", lambda e: e.tensor_scalar(out=neglam[:], in0=neglam[:], scalar1=-LAMBDA_INIT, scalar2=None, op0=ALU.add),
                 reads=[B_const], writes=[B_const])
            S.op("dve", lambda e: e.tensor_scalar(out=gsub08[:], in0=gcol[:, 1:2], scalar1=1.0 - LAMBDA_INIT, scalar2=None,
                                                   op0=ALU.mult), reads=[B_const], writes=[B_const])
            S.flush()

        def norm_tiles_to_hT(get_tile, ssq, B_ssq, xn, B_xn, psT, B_psT, hT_ap_fn, B_hT, mi, evk):
            for t in range(4):
                xt, Bx = get_tile(t)
                S.op("act", lambda e, xt=xt, t=t: e.activation(out=xn[:, t, :], in_=xt, func=AF.Square, accum_out=ssq[:, t:t + 1]),
                     reads=[Bx], writes=[B_xn, B_ssq[t]])
                S.op("act", lambda e, t=t: e.activation(out=ssq[:, t:t + 1], in_=ssq[:, t:t + 1], func=AF.Sqrt, bias=EPS,
                                                        scale=1.0 / D), reads=[B_ssq[t]], writes=[B_ssq[t]])
                S.op("dve", lambda e, t=t: e.reciprocal(out=ssq[:, t:t + 1], in_=ssq[:, t:t + 1]), reads=[B_ssq[t]],
                     writes=[B_ssq[t]])
                S.op("dve", lambda e, xt=xt, t=t: e.tensor_scalar(out=xn[:, t, :], in0=xt, scalar1=ssq[:, t:t + 1], scalar2=None,
                                                                   op0=ALU.mult), reads=[Bx, B_ssq[t]], writes=[B_xn])
            for kc in range(KC):
                pt, Bpt = psT[kc % 2], B_psT[kc % 2]
                for t in range(4):
                    S.op("pe", lambda e, pt=pt, t=t, kc=kc: e.transpose(out=pt[:, t * 128:(t + 1) * 128],
                                                                        in_=xn[:, t, kc * 128:(kc + 1) * 128], identity=ident),
                         reads=[B_xn, B_const], writes=[Bpt])
                dst = hT_ap_fn(kc)
                if (kc + evk) % 2 == 0:
                    S.op("act", lambda e, dst=dst, pt=pt, kc=kc: e.activation(out=dst, in_=pt[:], func=AF.Identity,
                                                                              bias=modv[:, mi + 1, kc:kc + 1],
                                                                              scale=modv[:, mi, kc:kc + 1]),
                         reads=[Bpt, B_const], writes=[B_hT[kc]])
                else:
                    S.op("dve", lambda e, dst=dst, pt=pt, kc=kc: e.tensor_scalar(out=dst, in0=pt[:], scalar1=modv[:, mi, kc:kc + 1],
                                                                                 scalar2=modv[:, mi + 1, kc:kc + 1],
                                                                                 op0=ALU.mult, op1=ALU.add),
                         reads=[Bpt, B_const], writes=[B_hT[kc]])

        if upto < 1:
            raise StopBuild()
        with ExitStack() as ph:
            def sb(name, shape, dt):
                return ph.enter_context(nc.sbuf_tensor("s_" + name, shape, dt))

            def ps(name, shape, dt):
                return ph.enter_context(nc.psum_tensor("p_" + name, shape, dt))
            HN = NT // 2
            hTh = [sb(f"hT{i}", [128, KC, HN], BF16) for i in range(2)]
            B_hTh = [[Buf(f"hT{i}_{kc}") for kc in range(KC)] for i in range(2)]
            xtl = [sb(f"xt{i}", [128, D], F32) for i in range(2)]
            B_xtl = [Buf(f"xt{i}") for i in range(2)]
            c_xt = [S.dma_ctr() for _ in range(2)]
            xn = sb("xn", [128, 4, D], BF16)
            B_xn = Buf("xn")
            ssq = sb("ssq", [128, 4], F32)
            B_ssq = [Buf(f"ssq{i}") for i in range(4)]
            wb = [sb(f"wb{i}", [128, KC, 512], BF16) for i in range(2)]
            B_wb = [Buf("wb0"), Buf("wb1")]
            c_wb = [S.dma_ctr(), S.dma_ctr()]
            tabs = sb("tabs", [128, 4, NT], F32)
            B_tabs = Buf("tabs")
            pos_i = sb("pos_i", [128, 512], I32)
            posf = sb("posf", [128, 512], F32)
            ang = sb("ang", [128, 512], F32)
            ru = sb("ru", [128, 512], F32)
            B_posi, B_posf, B_ang, B_ru = Buf("posi"), Buf("posf"), Buf("ang"), Buf("ru")
            c_pos = S.dma_ctr()
            xq = Ring([(sb(f"xq{i}", [128, 512], BF16), Buf(f"xq{i}")) for i in range(2)])
            t1r = Ring([(sb(f"t1_{i}", [128, 512], F32), Buf(f"t1_{i}")) for i in range(2)])
            t2r = Ring([(sb(f"t2_{i}", [128, 512], F32), Buf(f"t2_{i}")) for i in range(2)])
            qst = Ring([(sb(f"qst{i}", [128, 512], BF16), Buf(f"qst{i}"), S.dma_ctr()) for i in range(3)])
            vst = Ring([(sb(f"vst{i}", [128, 4, 512], BF16), Buf(f"vst{i}"), S.dma_ctr()) for i in range(2)])
            psT = [ps(f"psT{i}", [128, 512], BF16) for i in range(2)]
            B_psT = [Buf("psT0"), Buf("psT1")]
            psQ = Ring([(ps(f"psQ{i}", [128, 512], F32), Buf(f"psQ{i}")) for i in range(2)])
            psW = Ring([(ps(f"psW{i}", [128, 512], F32), Buf(f"psW{i}")) for i in range(2)])
            psV = Ring([(ps(f"psV{i}", [128, 512], F32), Buf(f"psV{i}")) for i in range(2)])

            def blocks_for(slot):
                bl = []
                if slot == 0:
                    bl += [("q", 0, qaT, 0, 0), ("q", 512, qaT, 4, 0)]
                if slot <= 1:
                    bl += [("k", 1024, kaT, 0, 0), ("k", 1536, kaT, 4, 0)]
                if slot == 0:
                    bl += [("q", 3072, qbT, 0, 1), ("q", 3584, qbT, 4, 1)]
                bl += [("k", 4096, kbT, 0, 1), ("k", 4608, kbT, 4, 1)]
                if slot <= 1:
                    bl += [("v", 2048, va, 0, 0), ("v", 2560, va, 512, 0)]
                bl += [("v", 5120, vb, 0, 1), ("v", 5632, vb, 512, 1)]
                return bl

            DBG_SLOTS = int(os.environ.get('K_DBG_SLOTS', NSLOT))
            DBG_BLOCKS = int(os.environ.get('K_DBG_BLOCKS', 99))
            DBG_HT = int(os.environ.get('K_DBG_HT', 1))
            jobs = [(slot, bi) for slot in range(DBG_SLOTS) for half in range(2) for bi in blocks_for(slot)[:DBG_BLOCKS]]

            def load_w(n):
                if n >= len(jobs):
                    return
                i = n % 2
                col = jobs[n][1][1]
                src = w_in[:, col:col + 512].rearrange("(kc p) n -> p kc n", p=128)
                S.op("pool", lambda e, i=i, src=src: e.dma_start(out=wb[i][:], in_=src), writes=[B_wb[i]], dma_ctr=c_wb[i])
            load_w(0)
            nwb = 0
            hidx = 0

            pend = []

            def post_qk(pq, Bpq, xqt, Bxq, tb, psw, rs, dst, Bdst, h, c0_):
                pw, Bpw = psW.next()
                S.op("pe", lambda e: e.matmul(pw[:], lhsT=psw, rhs=xqt[:], start=True, stop=True),
                     reads=[Bxq, B_const], writes=[Bpw])
                t1, Bt1 = t1r.next()
                t2, Bt2 = t2r.next()
                S.op("dve", lambda e: e.tensor_tensor(out=t1[:], in0=pq[:], in1=tabs[:, 2 * rs, tb:tb + 512], op=ALU.mult),
                     reads=[Bpq, B_tabs], writes=[Bt1])
                S.op("dve", lambda e: e.tensor_tensor(out=t2[:], in0=pw[:], in1=tabs[:, 2 * rs + 1, tb:tb + 512], op=ALU.mult),
                     reads=[Bpw, B_tabs], writes=[Bt2])
                qs, Bqs, cqs = qst.next()
                S.op("pool", lambda e: e.tensor_tensor(out=qs[:], in0=t1[:], in1=t2[:], op=ALU.add),
                     reads=[Bt1, Bt2], writes=[Bqs])
                S.op("sp", lambda e: e.dma_start(out=dst[h, :, c0_:c0_ + 512], in_=qs[:]), reads=[Bqs], writes=[Bdst], dma_ctr=cqs)

            def emit_norm(hi, g):
                slot_, half_ = divmod(hi, 2)
                hT_ = hTh[hi % 2]

                def get_tile(t):
                    r0 = half_ * HN + g * 512 + t * 128
                    S.op("sp", lambda e, r0=r0, t=t: e.dma_start(out=xtl[t % 2][:], in_=xs[slot_, r0:r0 + 128, :]),
                         writes=[B_xtl[t % 2]], dma_ctr=c_xt[t % 2])
                    return xtl[t % 2][:], B_xtl[t % 2]
                norm_tiles_to_hT(get_tile, ssq, B_ssq, xn, B_xn, psT, B_psT,
                                 lambda kc: hT_[:, kc, g * 512:(g + 1) * 512], B_hTh[hi % 2], 0, hi * 2 + g)
            for slot in range(DBG_SLOTS):
                for g in range(4):
                    S.op("sp", lambda e, slot=slot, g=g: e.dma_start(
                        out=pos_i[:], in_=posi_d[slot, :, g * 512:(g + 1) * 512].broadcast_to([128, 512])),
                        writes=[B_posi], dma_ctr=c_pos)
                    S.op("dve", lambda e: e.tensor_copy(out=posf[:], in_=pos_i[:]), reads=[B_posi], writes=[B_posf])
                    for ts in range(2):
                        if ts == 0 and slot > 1:
                            continue
                        S.op("dve", lambda e, ts=ts: e.tensor_scalar(out=ang[:], in0=posf[:], scalar1=rc[:, ts:ts + 1], scalar2=None,
                                                                      op0=ALU.mult), reads=[B_posf, B_const], writes=[B_ang])
                        S.op("dve", lambda e: e.tensor_scalar(out=ru[:], in0=ang[:], scalar1=INV2PI, scalar2=MAGIC, op0=ALU.mult,
                                                               op1=ALU.add), reads=[B_ang], writes=[B_ru])
                        S.op("dve", lambda e: e.tensor_scalar(out=ru[:], in0=ru[:], scalar1=MAGIC, scalar2=None, op0=ALU.subtract),
                             reads=[B_ru], writes=[B_ru])
                        S.op("dve", lambda e: e.scalar_tensor_tensor(out=ang[:], in0=ru[:], scalar=-C1, in1=ang[:], op0=ALU.mult,
                                                                      op1=ALU.add), reads=[B_ru, B_ang], writes=[B_ang])
                        S.op("dve", lambda e: e.scalar_tensor_tensor(out=ang[:], in0=ru[:], scalar=-C2, in1=ang[:], op0=ALU.mult,
                                                                      op1=ALU.add), reads=[B_ru, B_ang], writes=[B_ang])
                        S.op("dve", lambda e: e.tensor_scalar(out=ang[:], in0=ang[:], scalar1=-PI_SAFE, scalar2=PI_SAFE, op0=ALU.max,
                                                               op1=ALU.min), reads=[B_ang], writes=[B_ang])
                        S.op("act", lambda e, ts=ts, g=g: e.activation(out=tabs[:, 2 * ts + 1, g * 512:(g + 1) * 512], in_=ang[:],
                                                                       func=AF.Sin, scale=rc[:, 2 + ts:3 + ts]),
                             reads=[B_ang, B_const], writes=[B_tabs])
                        S.op("act", lambda e: e.activation(out=ru[:], in_=ang[:], func=AF.Abs), reads=[B_ang], writes=[B_ru])
                        S.op("act", lambda e, ts=ts, g=g: e.activation(out=tabs[:, 2 * ts, g * 512:(g + 1) * 512], in_=ru[:],
                                                                       func=AF.Sin, bias=halfpi[:, 0:1], scale=-1.0),
                             reads=[B_ru, B_const], writes=[B_tabs])
                tokA = {0: NT, 1: 0}.get(slot, None)
                tokB = slot * NT
                for half in range(2):
                    hT = hTh[hidx % 2]
                    B_hT = B_hTh[hidx % 2]
                    hb0 = half * HN
                    if hidx == 0:
                        emit_norm(0, 0)
                        emit_norm(0, 1)
                    hidx += 1
                    for bidx, (kind, col, dst, h0, rs) in enumerate(blocks_for(slot)[:DBG_BLOCKS]):
                        if bidx in (1, 2) and hidx < 2 * DBG_SLOTS:
                            emit_norm(hidx, bidx - 1)
                        i = nwb % 2
                        assert jobs[nwb][0] == slot and jobs[nwb][1][1] == col
                        nwb += 1
                        load_w(nwb)
                        tok0 = (tokA if rs == 0 else tokB)
                        if kind == "q":
                            tok0 = 0
                        tok0 += hb0
                        Bdst = B_scr[{id(qaT): "qaT", id(kaT): "kaT", id(va): "va", id(qbT): "qbT", id(kbT): "kbT", id(vb): "vb"}[id(dst)]]
                        if kind in ("q", "k"):
                            psw = pswA if rs == 0 else pswB
                            for hh in range(4):
                                for g in range(2):
                                    tb = hb0 + g * 512
                                    pq, Bpq = psQ.next()
                                    for kc in range(KC):
                                        S.op("pe", lambda e, pq=pq, i=i, kc=kc, hh=hh, g=g, hT=hT: e.matmul(
                                            pq[:], lhsT=wb[i][:, kc, hh * 128:(hh + 1) * 128], rhs=hT[:, kc, g * 512:(g + 1) * 512],
                                            start=(kc == 0), stop=(kc == KC - 1)), reads=[B_wb[i], B_hT[kc]], writes=[Bpq])
                                    xqt, Bxq = xq.next()
                                    S.op("act", lambda e, xqt=xqt, pq=pq: e.activation(out=xqt[:], in_=pq[:], func=AF.Copy),
                                         reads=[Bpq], writes=[Bxq])
                                    if pend:
                                        pend.pop()()
                                    pend.append(lambda pq=pq, Bpq=Bpq, xqt=xqt, Bxq=Bxq, tb=tb, hh=hh, g=g: post_qk(
                                        pq, Bpq, xqt, Bxq, tb, psw, rs, dst, Bdst, h0 + hh, tok0 + g * 512))
                            if pend:
                                pend.pop()()
                            continue
                            if True:
                                if True:
                                    pw, Bpw = psW.next()
                                    S.op("pe", lambda e, pw=pw, psw=psw, xqt=xqt: e.matmul(pw[:], lhsT=psw, rhs=xqt[:], start=True,
                                                                                           stop=True),
                                         reads=[Bxq, B_const], writes=[Bpw])
                                    t1, Bt1 = t1r.next()
                                    t2, Bt2 = t2r.next()
                                    S.op("dve", lambda e, t1=t1, pq=pq, rs=rs, tb=tb: e.tensor_tensor(
                                        out=t1[:], in0=pq[:], in1=tabs[:, 2 * rs, tb:tb + 512], op=ALU.mult),
                                        reads=[Bpq, B_tabs], writes=[Bt1])
                                    S.op("dve", lambda e, t2=t2, pw=pw, rs=rs, tb=tb: e.tensor_tensor(
                                        out=t2[:], in0=pw[:], in1=tabs[:, 2 * rs + 1, tb:tb + 512], op=ALU.mult),
                                        reads=[Bpw, B_tabs], writes=[Bt2])
                                    qs, Bqs, cqs = qst.next()
                                    S.op("pool", lambda e, qs=qs, t1=t1, t2=t2: e.tensor_tensor(out=qs[:], in0=t1[:], in1=t2[:],
                                                                                               op=ALU.add),
                                         reads=[Bt1, Bt2], writes=[Bqs])
                                    c0_ = tok0 + g * 512
                                    S.op("sp", lambda e, dst=dst, qs=qs, h=h0 + hh, c0_=c0_: e.dma_start(
                                        out=dst[h, :, c0_:c0_ + 512], in_=qs[:]), reads=[Bqs], writes=[Bdst], dma_ctr=cqs)
                        else:
                            for tq in range(2):
                                vs, Bvs, cvs = vst.next()
                                for tt in range(4):
                                    tile = tq * 4 + tt
                                    pv, Bpv = psV.next()
                                    for kc in range(KC):
                                        S.op("pe", lambda e, pv=pv, i=i, kc=kc, tile=tile, hT=hT: e.matmul(
                                            pv[:], lhsT=hT[:, kc, tile * 128:(tile + 1) * 128], rhs=wb[i][:, kc, :],
                                            start=(kc == 0), stop=(kc == KC - 1)), reads=[B_wb[i], B_hT[kc]], writes=[Bpv])
                                    if tt % 2 == 0:
                                        S.op("act", lambda e, vs=vs, pv=pv, tt=tt: e.activation(out=vs[:, tt, :], in_=pv[:], func=AF.Copy),
                                             reads=[Bpv], writes=[Bvs])
                                    else:
                                        S.op("dve", lambda e, vs=vs, pv=pv, tt=tt: e.tensor_copy(out=vs[:, tt, :], in_=pv[:]),
                                             reads=[Bpv], writes=[Bvs])
                                r0 = tok0 + tq * 512
                                S.op("sp", lambda e, dst=dst, vs=vs, r0=r0, h0=h0: e.dma_start(
                                    out=dst[r0:r0 + 512, h0:h0 + 512].rearrange("(t p) n -> p t n", p=128), in_=vs[:]),
                                    reads=[Bvs], writes=[Bdst], dma_ctr=cvs)
            S.flush()

        if upto < 2:
            raise StopBuild()
        with ExitStack() as ph:
            def sb(name, shape, dt):
                return ph.enter_context(nc.sbuf_tensor("s_" + name, shape, dt))

            def ps(name, shape, dt):
                return ph.enter_context(nc.psum_tensor("p_" + name, shape, dt))
            hb = Ring([(sb(f"qh{i}", [128, NT], BF16), sb(f"kh{i}", [128, 4 * NT], BF16), sb(f"vh{i}", [128, 64, 128], BF16),
                        Buf(f"hb{i}"), S.dma_ctr()) for i in range(2)])
            pTr = Ring([(sb(f"pT_{i}", [128, 1024], BF16), Buf(f"pT_{i}")) for i in range(3)])
            accr = Ring([(sb(f"accA{i}", [128, 1024], F32), Buf(f"accA{i}"), sb(f"accB{i}", [128, 1024], F32), Buf(f"accB{i}"))
                         for i in range(2)])
            rden = [sb(f"rden{m}", [128, 512], F32) for m in range(2)]
            B_rden = [Buf("rden0"), Buf("rden1")]
            o1 = sb("o1", [128, 512], F32)
            o2 = sb("o2", [128, 512], F32)
            obr = Ring([(sb(f"ob{i}", [128, 512], F32), Buf(f"ob{i}")) for i in range(2)])
            sqr = Ring([(sb(f"sq{i}", [128, 512], F32), Buf(f"sq{i}")) for i in range(2)])
            rstd = sb("rstd", [128, 512], F32)
            B_o1, B_o2, B_rstd = Buf("o1"), Buf("o2"), Buf("rstd")
            mst = Ring([(sb(f"mst{i}", [128, NT], BF16), Buf(f"mst{i}"), S.dma_ctr()) for i in range(2)])
            psS = Ring([(ps(f"psS{i}", [128, 1024], F32), Buf(f"psS{i}")) for i in range(2)])
            psO = [ps(f"psO{m}", [128, 512], F32) for m in range(2)]
            B_psO = [Buf("psO0"), Buf("psO1")]
            psDen = ps("psDen", [128, 1024], F32)
            B_psDen = Buf("psDen")

            def load_head_B(h):
                qh, kh, vh, Bh, ch = hb.next()
                S.op("sp", lambda e: e.dma_start(out=qh[:], in_=qbT[h]), reads=[B_scr["qbT"]], writes=[Bh], dma_ctr=ch)
                S.op("sp", lambda e: e.dma_start(out=kh[:], in_=kbT[h]), reads=[B_scr["kbT"]], writes=[Bh], dma_ctr=ch)
                S.op("sp", lambda e: e.dma_start(out=vh[:], in_=vb[:, h * 128:(h + 1) * 128].rearrange("(t p) n -> p t n", p=128)),
                     reads=[B_scr["vb"]], writes=[Bh], dma_ctr=ch)
                return qh, kh, vh, Bh

            heads = {0: load_head_B(0)}
            mstage = {}
            items = []
            for h in range(8):
                for qb in range(4):
                    pairs = []
                    for slot in range(NSLOT):
                        nk = 4 * (qb + 1) if slot == 0 else 16
                        for kc in range(nk):
                            pairs.append((slot, kc))
                    for pi, (slot, kc) in enumerate(pairs):
                        items.append((h, qb, pi, len(pairs), slot, kc))
            qk_out = {}

            def issue_qk(idx):
                h, qb, pi, npairs, slot, kc = items[idx]
                if h not in heads:
                    heads[h] = load_head_B(h)
                qh, kh, vh, Bh = heads[h]
                ktok = slot * NT + kc * 128
                pS, BpS = psS.next()
                for m in range(2):
                    S.op("pe", lambda e, pS=pS, m=m, ktok=ktok, qb=qb, kh=kh, qh=qh: e.matmul(
                        pS[:, m * 512:(m + 1) * 512], lhsT=kh[m * 64:(m + 1) * 64, ktok:ktok + 128],
                        rhs=qh[m * 64:(m + 1) * 64, qb * 512:(qb + 1) * 512], start=True, stop=True), reads=[Bh], writes=[BpS])
                qk_out[idx] = (pS, BpS)

            deferred = []

            def finalize_a(h, qb, accs):
                ms, Bms, cms = mstage[h]
                accA, BaccA, accB, BaccB = accs
                for m in range(2):
                    S.op("pe", lambda e, m=m, accA=accA: e.matmul(psDen[:, m * 512:(m + 1) * 512], lhsT=ones_f[:],
                                                                  rhs=accA[:, m * 512:(m + 1) * 512], start=False, stop=False),
                         reads=[BaccA, B_const], writes=[B_psDen])
                    S.op("pe", lambda e, m=m, accB=accB: e.matmul(psDen[:, m * 512:(m + 1) * 512], lhsT=ones_f[:],
                                                                  rhs=accB[:, m * 512:(m + 1) * 512], start=False, stop=True),
                         reads=[BaccB, B_const], writes=[B_psDen])
                for m in range(2):
                    S.op("dve", lambda e, m=m: e.reciprocal(out=rden[m][:], in_=psDen[:, m * 512:(m + 1) * 512]),
                         reads=[B_psDen], writes=[B_rden[m]])
                S.op("dve", lambda e: e.tensor_tensor(out=o1[:], in0=psO[0][:], in1=rden[0][:], op=ALU.mult),
                     reads=[B_psO[0], B_rden[0]], writes=[B_o1])
                S.op("dve", lambda e: e.tensor_tensor(out=o2[:], in0=psO[1][:], in1=rden[1][:], op=ALU.mult),
                     reads=[B_psO[1], B_rden[1]], writes=[B_o2])
                ob, Bob = obr.next()
                sq, Bsq = sqr.next()
                S.op("dve", lambda e, ob=ob: e.scalar_tensor_tensor(out=ob[:], in0=o2[:], scalar=neglam[:, 0:1], in1=o1[:], op0=ALU.mult,
                                                                     op1=ALU.add), reads=[B_o1, B_o2, B_const], writes=[Bob])
                S.op("act", lambda e, ob=ob, sq=sq: e.activation(out=sq[:], in_=ob[:], func=AF.Square), reads=[Bob], writes=[Bsq])

                def part_b():
                    pD, BpD = psS.next()
                    psS.next()
                    S.op("pe", lambda e, pD=pD: e.matmul(pD[:, 0:512], lhsT=ones_f[:], rhs=sq[:], start=True, stop=True),
                         reads=[Bsq, B_const], writes=[BpD])
                    S.op("act", lambda e, pD=pD: e.activation(out=rstd[:], in_=pD[:, 0:512], func=AF.Sqrt, bias=EPS, scale=1.0 / HD),
                         reads=[BpD], writes=[B_rstd])
                    S.op("dve", lambda e: e.reciprocal(out=rstd[:], in_=rstd[:]), reads=[B_rstd], writes=[B_rstd])
                    S.op("dve", lambda e: e.scalar_tensor_tensor(out=ms[:, qb * 512:(qb + 1) * 512], in0=ob[:],
                                                                  scalar=gsub08[:, 0:1], in1=rstd[:], op0=ALU.mult, op1=ALU.mult),
                         reads=[Bob, B_rstd, B_const], writes=[Bms])
                    if qb == 3:
                        S.op("sp", lambda e: e.dma_start(out=mixs[8 + h], in_=ms[:]), reads=[Bms], writes=[B_scr["mixs"]], dma_ctr=cms)
                return part_b

            issue_qk(0)
            accs = None
            for idx, (h, qb, pi, npairs, slot, kc) in enumerate(items):
                if idx + 1 < len(items):
                    issue_qk(idx + 1)
                if h + 1 < 8 and h + 1 not in heads and pi == 0 and qb == 0:
                    heads[h + 1] = load_head_B(h + 1)
                if h not in mstage:
                    mstage[h] = mst.next()
                qh, kh, vh, Bh = heads[h]
                pS, BpS = qk_out.pop(idx)
                pT, BpT = pTr.next()
                kt = slot * 16 + kc
                S.op("act", lambda e, pT=pT, pS=pS, slot=slot: e.activation(out=pT[:], in_=pS[:], func=AF.Exp,
                                                                             bias=ebias[:, slot:slot + 1], scale=SCALE_B),
                     reads=[BpS, B_const], writes=[BpT])
                if slot == 0 and kc >= 4 * qb:
                    mk = cmask(kc - 4 * qb)
                    S.op("pool", lambda e, pT=pT, mk=mk: e.tensor_tensor(
                        out=pT[:].rearrange("p (m q) -> p m q", m=2), in0=pT[:].rearrange("p (m q) -> p m q", m=2),
                        in1=mk.unsqueeze(1).broadcast_to([128, 2, 512]), op=ALU.mult), reads=[BpT, B_const], writes=[BpT])
                if pi == 0:
                    accs = accr.next()
                acc, Bacc = (accs[0], accs[1]) if pi % 3 == 0 else (accs[2], accs[3])
                if pi % 3 == 2:
                    for m in range(2):
                        S.op("pe", lambda e, m=m, pT=pT, pi=pi: e.matmul(psDen[:, m * 512:(m + 1) * 512], lhsT=ones_bf[:],
                                                                         rhs=pT[:, m * 512:(m + 1) * 512], start=(pi == 2), stop=False),
                             reads=[BpT, B_const], writes=[B_psDen])
                elif pi <= 1:
                    S.op("dve", lambda e, acc=acc, pT=pT: e.tensor_copy(out=acc[:], in_=pT[:]), reads=[BpT], writes=[Bacc])
                else:
                    S.op("dve", lambda e, acc=acc, pT=pT: e.tensor_tensor(out=acc[:], in0=acc[:], in1=pT[:], op=ALU.add),
                         reads=[BpT, Bacc], writes=[Bacc])
                for m in range(2):
                    S.op("pe", lambda e, m=m, pT=pT, kt=kt, vh=vh, pi=pi, npairs=npairs: e.matmul(
                        psO[m][:], lhsT=vh[:, kt, :], rhs=pT[:, m * 512:(m + 1) * 512], start=(pi == 0), stop=(pi == npairs - 1)),
                        reads=[BpT, Bh], writes=[B_psO[m]])
                for dd in [x for x in deferred if x[0] <= idx]:
                    deferred.remove(dd)
                    dd[1]()
                if pi == npairs - 1:
                    deferred.append((idx + 4, finalize_a(h, qb, accs)))
            for dd in deferred:
                dd[1]()
            S.flush()

        if upto < 3:
            raise StopBuild()
        with ExitStack() as ph:
            def sb(name, shape, dt):
                return ph.enter_context(nc.sbuf_tensor("s_" + name, shape, dt))

            def ps(name, shape, dt):
                return ph.enter_context(nc.psum_tensor("p_" + name, shape, dt))
            DILS = (1, 4, 16)
            ha = Ring([(sb(f"qa{i}", [128, NT], BF16), sb(f"ka{i}", [128, 2 * NT], BF16),
                        [sb(f"va{i}_{d}", [128, 32, 128], BF16) for d in DILS], Buf(f"ha{i}"), S.dma_ctr()) for i in range(2)])
            numacc = sb("numacc", [128, NT], F32)
            denacc = sb("denacc", [128, NT], F32)
            B_num, B_den = Buf("numacc"), Buf("denacc")
            pAr = [Ring([(sb(f"pA{hf}_{i}", [128, 512], BF16), Buf(f"pA{hf}_{i}")) for i in range(2)]) for hf in range(2)]
            sqa = sb("sqa", [128, 512], F32)
            rsa = sb("rsa", [128, 512], F32)
            B_sqa, B_rsa = Buf("sqa"), Buf("rsa")
            msa = Ring([(sb(f"msa{i}", [128, NT], BF16), Buf(f"msa{i}"), S.dma_ctr()) for i in range(2)])
            psSa = [Ring([(ps(f"psSa{hf}_{i}", [128, 512], F32), Buf(f"psSa{hf}_{i}")) for i in range(2)]) for hf in range(2)]
            psOa = Ring([(ps(f"psOa{i}", [128, 512], F32), Buf(f"psOa{i}")) for i in range(2)])
            psDa = Ring([(ps(f"psDa{i}", [128, 512], F32), Buf(f"psDa{i}")) for i in range(2)])

            def load_head_A(h):
                qh, kh, vhs, Bh, ch = ha.next()
                S.op("sp", lambda e: e.dma_start(out=qh[:], in_=qaT[h]), reads=[B_scr["qaT"]], writes=[Bh], dma_ctr=ch)
                S.op("sp", lambda e: e.dma_start(out=kh[:], in_=kaT[h]), reads=[B_scr["kaT"]], writes=[Bh], dma_ctr=ch)
                for di, d in enumerate(DILS):
                    nb = 32 // d
                    for r in range(d):
                        src = va[:, h * 128:(h + 1) * 128].rearrange("(b p d) n -> d p b n", p=128, d=d)[r]
                        S.op("sp", lambda e, vt=vhs[di], r=r, nb=nb, src=src: e.dma_start(out=vt[:, r * nb:(r + 1) * nb, :], in_=src),
                             reads=[B_scr["va"]], writes=[Bh], dma_ctr=ch)
                return qh, kh, vhs, Bh

            cur = load_head_A(0)
            for h in range(8):
                qh, kh, vhs, Bh = cur
                if h + 1 < 8:
                    cur = load_head_A(h + 1)
                ms, Bms, cms = msa.next()
                for di, d in enumerate(DILS):
                    nb = 32 // d
                    ob0 = nb // 2
                    if d == 1:
                        batches = [[(b, 0) for b in range(b0, b0 + 4)] for b0 in range(ob0, nb, 4)]
                    else:
                        batches = []
                        for b in range(ob0, nb):
                            for r0 in range(0, d, 4):
                                batches.append([(b, r) for r in range(r0, r0 + 4)])
                    for items in batches:
                        pss = [psSa[0].next(), psSa[1].next()]
                        pas = [pAr[0].next(), pAr[1].next()]
                        for hf in range(2):
                            pS, BpS = pss[hf]
                            for it, (b, r) in enumerate(items):
                                bk = b - 1 + hf
                                kcol = bk * 128 * d + r
                                qcol = b * 128 * d + r - NT
                                S.op("pe", lambda e, pS=pS, it=it, kcol=kcol, qcol=qcol, d=d, kh=kh, qh=qh: e.matmul(
                                    pS[:, it * 128:(it + 1) * 128], lhsT=kh[:, kcol:kcol + 127 * d + 1:d],
                                    rhs=qh[:, qcol:qcol + 127 * d + 1:d], start=True, stop=True), reads=[Bh], writes=[BpS])
                            pA, BpA = pas[hf]
                            segs = []
                            for it, (b, r) in enumerate(items):
                                bk = b - 1 + hf
                                segs.append(1 if bk < ob0 else 0)
                            s0 = 0
                            while s0 < 4:
                                s1 = s0
                                while s1 < 4 and segs[s1] == segs[s0]:
                                    s1 += 1
                                S.op("act", lambda e, pA=pA, pS=pS, s0=s0, s1=s1, bs=segs[s0]: e.activation(
                                    out=pA[:, s0 * 128:s1 * 128], in_=pS[:, s0 * 128:s1 * 128], func=AF.Exp,
                                    bias=ebias[:, bs:bs + 1], scale=SCALE_A), reads=[BpS, B_const], writes=[BpA])
                                s0 = s1
                            S.op("pool", lambda e, pA=pA, hf=hf: e.tensor_tensor(
                                out=pA[:].rearrange("p (i q) -> p i q", i=4), in0=pA[:].rearrange("p (i q) -> p i q", i=4),
                                in1=bandm(hf).unsqueeze(1).broadcast_to([128, 4, 128]), op=ALU.mult),
                                reads=[BpA, B_const], writes=[BpA])
                        pO, BpO = psOa.next()
                        pD, BpD = psDa.next()
                        for it, (b, r) in enumerate(items):
                            for hf in range(2):
                                bk = b - 1 + hf
                                pA, BpA = pas[hf]
                                S.op("pe", lambda e, pO=pO, pA=pA, it=it, di=di, vi=r * nb + bk, hf=hf, vhs=vhs: e.matmul(
                                    pO[:, it * 128:(it + 1) * 128], lhsT=vhs[di][:, vi, :], rhs=pA[:, it * 128:(it + 1) * 128],
                                    start=(hf == 0), stop=(hf == 1)), reads=[BpA, Bh], writes=[BpO])
                        for it, (b, r) in enumerate(items):
                            for hf in range(2):
                                pA, BpA = pas[hf]
                                S.op("pe", lambda e, pD=pD, pA=pA, it=it, hf=hf: e.matmul(
                                    pD[:, it * 128:(it + 1) * 128], lhsT=ones_bf[:], rhs=pA[:, it * 128:(it + 1) * 128],
                                    start=(hf == 0), stop=(hf == 1)), reads=[BpA, B_const], writes=[BpD])
                        b0, r0 = items[0]
                        if d == 1:
                            c0_ = b0 * 128 - NT

                            def dstv(t):
                                return t[:, c0_:c0_ + 512].rearrange("e (i p) -> e i p", i=4)
                        else:
                            c0_ = b0 * 128 * d - NT

                            def dstv(t, c0_=c0_, d=d, r0=r0):
                                return t[:, c0_:c0_ + 128 * d].rearrange("e (p r) -> e r p", r=d)[:, r0:r0 + 4, :]
                        first = (di == 0)
                        for (accT, Bacc, pX, BpX, eng) in [(numacc, B_num, pO, BpO, "dve"), (denacc, B_den, pD, BpD, "dve")]:
                            dv = dstv(accT)
                            src = pX[:].rearrange("e (i p) -> e i p", i=4)
                            if first:
                                S.op(eng, lambda e, dv=dv, src=src: e.tensor_copy(out=dv, in_=src), reads=[BpX], writes=[Bacc])
                            else:
                                S.op(eng, lambda e, dv=dv, src=src: e.tensor_tensor(out=dv, in0=src, in1=dv, op=ALU.add),
                                     reads=[BpX, Bacc], writes=[Bacc])
                S.op("dve", lambda e: e.reciprocal(out=denacc[:], in_=denacc[:]), reads=[B_den], writes=[B_den])
                S.op("dve", lambda e: e.tensor_tensor(out=numacc[:], in0=numacc[:], in1=denacc[:], op=ALU.mult),
                     reads=[B_num, B_den], writes=[B_num])
                for qb in range(4):
                    sl = slice(qb * 512, (qb + 1) * 512)
                    S.op("act", lambda e, sl=sl: e.activation(out=sqa[:], in_=numacc[:, sl], func=AF.Square), reads=[B_num], writes=[B_sqa])
                    pD, BpD = psDa.next()
                    S.op("pe", lambda e, pD=pD: e.matmul(pD[:], lhsT=ones_f[:], rhs=sqa[:], start=True, stop=True),
                         reads=[B_sqa, B_const], writes=[BpD])
                    S.op("act", lambda e, pD=pD: e.activation(out=rsa[:], in_=pD[:], func=AF.Sqrt, bias=EPS, scale=1.0 / HD),
                         reads=[BpD], writes=[B_rsa])
                    S.op("dve", lambda e: e.reciprocal(out=rsa[:], in_=rsa[:]), reads=[B_rsa], writes=[B_rsa])
                    S.op("dve", lambda e, ms=ms, sl=sl: e.scalar_tensor_tensor(out=ms[:, sl], in0=numacc[:, sl], scalar=gcol[:, 0:1],
                                                                               in1=rsa[:], op0=ALU.mult, op1=ALU.mult),
                         reads=[B_num, B_rsa, B_const], writes=[Bms])
                S.op("sp", lambda e, ms=ms, h=h: e.dma_start(out=mixs[h], in_=ms[:]), reads=[Bms], writes=[B_scr["mixs"]], dma_ctr=cms)
            S.flush()

        if upto < 4:
            raise StopBuild()
        with ExitStack() as ph:
            def sb(name, shape, dt):
                return ph.enter_context(nc.sbuf_tensor("s_" + name, shape, dt))

            def ps(name, shape, dt):
                return ph.enter_context(nc.psum_tensor("p_" + name, shape, dt))
            wo = sb("wo", [128, KC, D], BF16)
            B_wo = Buf("wo")
            c_wo = S.dma_ctr()
            mg = Ring([(sb(f"mg{i}", [128, 16, 512], BF16), Buf(f"mg{i}"), S.dma_ctr()) for i in range(2)])
            xt4 = Ring([(sb(f"x4_{i}", [128, D], F32), Buf(f"x4_{i}"), S.dma_ctr()) for i in range(2)])
            x1t = [sb(f"x1t{i}", [128, D], F32) for i in range(4)]
            B_x1t = [Buf(f"x1t{i}") for i in range(4)]
            c_x1t = [S.dma_ctr() for _ in range(4)]
            xn = sb("xn4", [128, 4, D], BF16)
            B_xn = Buf("xn4")
            ssq = sb("ssq4", [128, 4], F32)
            B_ssq = [Buf(f"ssq4_{i}") for i in range(4)]
            ssy = sb("ssy", [128, 4], F32)
            rsy = sb("rsy", [128, 1], F32)
            B_ssy, B_rsy = Buf("ssy"), Buf("rsy")
            junk = sb("junk4", [128, 512], BF16)
            B_junk = Buf("junk4")
            GGa = sb("GGa", [128, D], F32)
            c_gg = S.dma_ctr()
            S.op("sp", lambda e: e.dma_start(out=GGa[:], in_=ggs[0]), reads=[B_scr["ggs"]], writes=[B_const], dma_ctr=c_gg)
            h2g = Ring([(sb(f"h2g{i}", [128, KC, 512], BF16), [Buf(f"h2g{i}_{kc}") for kc in range(KC)], S.dma_ctr()) for i in range(1)])
            psY = ps("psY", [128, D], F32)
            B_psY = Buf("psY")
            psT = [ps(f"psT4_{i}", [128, 512], BF16) for i in range(2)]
            B_psT = [Buf("psT4_0"), Buf("psT4_1")]
            for cbk in range(4):
                src = w_out[:, cbk * 512:(cbk + 1) * 512].rearrange("(kc p) n -> p kc n", p=128)
                S.op("pool", lambda e, cbk=cbk, src=src: e.dma_start(out=wo[:, :, cbk * 512:(cbk + 1) * 512], in_=src),
                     writes=[B_wo], dma_ctr=c_wo)
            for g in range(4):
                mgt, Bmg, cmg = mg.next()
                S.op("sp", lambda e, mgt=mgt, g=g: e.dma_start(out=mgt[:], in_=mixs[:, :, g * 512:(g + 1) * 512].rearrange("h p t -> p h t")),
                     reads=[B_scr["mixs"]], writes=[Bmg], dma_ctr=cmg)
                for t in range(4):
                    r0 = g * 512 + t * 128
                    xt, Bxt, cxt = xt4.next()
                    S.op("sp", lambda e, xt=xt, r0=r0: e.dma_start(out=xt[:], in_=xs[0, r0:r0 + 128, :]), writes=[Bxt], dma_ctr=cxt)
                    for cbk in range(4):
                        for hc in range(16):
                            S.op("pe", lambda e, mgt=mgt, t=t, cbk=cbk, hc=hc: e.matmul(
                                psY[:, cbk * 512:(cbk + 1) * 512], lhsT=mgt[:, hc, t * 128:(t + 1) * 128],
                                rhs=wo[:, hc, cbk * 512:(cbk + 1) * 512], start=(hc == 0), stop=(hc == 15)),
                                reads=[Bmg, B_wo], writes=[B_psY])
                    for cbk in range(4):
                        S.op("act", lambda e, cbk=cbk: e.activation(out=junk[:, 0:512], in_=psY[:, cbk * 512:(cbk + 1) * 512],
                                                                    func=AF.Square, accum_out=ssy[:, cbk:cbk + 1]),
                             reads=[B_psY], writes=[B_junk, B_ssy])
                    S.op("dve", lambda e: e.reduce_sum(out=rsy[:], in_=ssy[:], axis=AX.X), reads=[B_ssy], writes=[B_rsy])
                    S.op("act", lambda e: e.activation(out=rsy[:], in_=rsy[:], func=AF.Sqrt, bias=EPS, scale=1.0 / D),
                         reads=[B_rsy], writes=[B_rsy])
                    S.op("dve", lambda e: e.reciprocal(out=rsy[:], in_=rsy[:]), reads=[B_rsy], writes=[B_rsy])
                    x1, Bx1 = x1t[t], B_x1t[t]
                    for cbk in range(4):
                        sl = slice(cbk * 512, (cbk + 1) * 512)
                        S.op("dve", lambda e, x1=x1, sl=sl: e.scalar_tensor_tensor(out=x1[:, sl], in0=psY[:, sl], scalar=rsy[:, 0:1],
                                                                                   in1=GGa[:, sl], op0=ALU.mult, op1=ALU.mult),
                             reads=[B_psY, B_rsy, B_const], writes=[Bx1])
                    S.op("pool", lambda e, x1=x1, xt=xt: e.tensor_tensor(out=x1[:], in0=x1[:], in1=xt[:], op=ALU.add),
                         reads=[Bx1, Bxt], writes=[Bx1])
                    S.op("sp", lambda e, x1=x1, r0=r0: e.dma_start(out=x1s[r0:r0 + 128, :], in_=x1[:]), reads=[Bx1],
                         writes=[B_scr["x1s"]], dma_ctr=c_x1t[t])
                hg, Bhg, chg = h2g.next()
                norm_tiles_to_hT(lambda t: (x1t[t][:], B_x1t[t]), ssq, B_ssq, xn, B_xn, psT, B_psT,
                                 lambda kc, hg=hg: hg[:, kc, :], Bhg, 2, g)
                S.op("sp", lambda e, hg=hg, g=g: e.dma_start(out=h2s[g], in_=hg[:]), reads=Bhg, writes=[B_scr["h2s"]], dma_ctr=chg)
            S.flush()

        if upto < 5:
            raise StopBuild()
        with ExitStack() as ph:
            def sb(name, shape, dt):
                return ph.enter_context(nc.sbuf_tensor("s_" + name, shape, dt))

            def ps(name, shape, dt):
                return ph.enter_context(nc.psum_tensor("p_" + name, shape, dt))
            h2 = sb("h2", [128, KC, 512], BF16)
            B_h2 = Buf("h2")
            c_h2 = S.dma_ctr()
            actT = sb("actT", [128, NFC, 512], BF16)
            B_actT = Buf("actT")
            wgu = Ring([(sb(f"wg{i}", [128, KC, 256], BF16), sb(f"wu{i}", [128, KC, 256], BF16), Buf(f"wgu{i}"), S.dma_ctr())
                        for i in range(3)])
            wd = Ring([(sb(f"wd{i}", [128, 11, 512], BF16), Buf(f"wd{i}"), S.dma_ctr()) for i in range(3)])
            fbuf = sb("fbuf", [128, 4, D], F32)
            B_fbuf = [Buf(f"fbuf{i}") for i in range(4)]
            ssf = sb("ssf", [128, 4, 4], F32)
            B_ssf = Buf("ssf")
            rsf = sb("rsf", [128, 4], F32)
            B_rsf = Buf("rsf")
            sg = Ring([(sb(f"sg{i}", [128, 512], F32), Buf(f"sg{i}")) for i in range(2)])
            junk = sb("junk5", [128, 512], BF16)
            B_junk = Buf("junk5")
            x1r = Ring([(sb(f"x1r{i}", [128, D], F32), Buf(f"x1r{i}"), S.dma_ctr()) for i in range(1)])
            psG = Ring([(ps(f"psG{i}", [128, 512], F32), Buf(f"psG{i}")) for i in range(2)])
            psU = Ring([(ps(f"psU{i}", [128, 512], F32), Buf(f"psU{i}")) for i in range(2)])
            psF = [ps(f"psF{i}", [128, 512], F32) for i in range(4)]
            B_psF = [Buf(f"psF{i}") for i in range(4)]
            c_y = S.dma_ctr()
            GGf = sb("GGf", [128, D], F32)
            c_gg = S.dma_ctr()
            S.op("sp", lambda e: e.dma_start(out=GGf[:], in_=ggs[1]), reads=[B_scr["ggs"]], writes=[B_const], dma_ctr=c_gg)
            wjobs = []
            for g in range(4):
                for fb in range(NFC // 2):
                    wjobs.append(("gu", fb))
                for cbk in range(4):
                    for q4 in range(4):
                        wjobs.append(("d", cbk, q4))
            wloaded = {}

            def load_wj(n):
                if n >= len(wjobs):
                    return
                jb = wjobs[n]
                if jb[0] == "gu":
                    fb = jb[1]
                    wg, wu, Bw, cw_ = wgu.next()
                    srcg = w_gate[:, fb * 256:(fb + 1) * 256].rearrange("(kc p) n -> p kc n", p=128)
                    srcu = w_up[:, fb * 256:(fb + 1) * 256].rearrange("(kc p) n -> p kc n", p=128)
                    S.op("pool", lambda e, wg=wg, srcg=srcg: e.dma_start(out=wg[:], in_=srcg), writes=[Bw], dma_ctr=cw_)
                    S.op("pool", lambda e, wu=wu, srcu=srcu: e.dma_start(out=wu[:], in_=srcu), writes=[Bw], dma_ctr=cw_)
                    wloaded[n] = (wg, wu, Bw)
                else:
                    _, cbk, q4 = jb
                    wdt, Bwd, cwd = wd.next()
                    src = w_down[q4 * 11 * 128:(q4 + 1) * 11 * 128, cbk * 512:(cbk + 1) * 512].rearrange("(c p) n -> p c n", p=128)
                    S.op("pool", lambda e, wdt=wdt, src=src: e.dma_start(out=wdt[:], in_=src), writes=[Bwd], dma_ctr=cwd)
                    wloaded[n] = (wdt, Bwd)
            load_wj(0)
            load_wj(1)
            wn = 0
            for g in range(4):
                S.op("sp", lambda e, g=g: e.dma_start(out=h2[:], in_=h2s[g]), reads=[B_scr["h2s"]], writes=[B_h2], dma_ctr=c_h2)
                for fb in range(NFC // 2):
                    wg, wu, Bw = wloaded.pop(wn)
                    wn += 1
                    load_wj(wn + 1)
                    for j in range(2):
                        fc = fb * 2 + j
                        pG, BpG = psG.next()
                        pU, BpU = psU.next()
                        for kc in range(KC):
                            S.op("pe", lambda e, pG=pG, wg=wg, kc=kc, j=j: e.matmul(pG[:], lhsT=wg[:, kc, j * 128:(j + 1) * 128],
                                                                                    rhs=h2[:, kc, :], start=(kc == 0), stop=(kc == KC - 1)),
                                 reads=[Bw, B_h2], writes=[BpG])
                        for kc in range(KC):
                            S.op("pe", lambda e, pU=pU, wu=wu, kc=kc, j=j: e.matmul(pU[:], lhsT=wu[:, kc, j * 128:(j + 1) * 128],
                                                                                    rhs=h2[:, kc, :], start=(kc == 0), stop=(kc == KC - 1)),
                                 reads=[Bw, B_h2], writes=[BpU])
                        sgt, Bsg = sg.next()
                        S.op("act", lambda e, sgt=sgt, pG=pG: e.activation(out=sgt[:], in_=pG[:], func=AF.Silu), reads=[BpG], writes=[Bsg])
                        S.op("dve", lambda e, sgt=sgt, pU=pU, fc=fc: e.tensor_tensor(out=actT[:, fc, :], in0=sgt[:], in1=pU[:], op=ALU.mult),
                             reads=[Bsg, BpU], writes=[B_actT])
                for cbk in range(4):
                    for q4 in range(4):
                        wdt, Bwd = wloaded.pop(wn)
                        wn += 1
                        load_wj(wn + 1)
                        for c in range(11):
                            fc = q4 * 11 + c
                            for t in range(4):
                                S.op("pe", lambda e, t=t, fc=fc, c=c, wdt=wdt: e.matmul(
                                    psF[t][:], lhsT=actT[:, fc, t * 128:(t + 1) * 128], rhs=wdt[:, c, :],
                                    start=(fc == 0), stop=(fc == NFC - 1)), reads=[B_actT, Bwd], writes=[B_psF[t]])
                    for t in range(4):
                        S.op("act", lambda e, t=t, cbk=cbk: e.activation(out=junk[:], in_=psF[t][:], func=AF.Square,
                                                                         accum_out=ssf[:, t, cbk:cbk + 1]),
                             reads=[B_psF[t]], writes=[B_junk, B_ssf])
                        S.op("dve", lambda e, t=t, cbk=cbk: e.tensor_copy(out=fbuf[:, t, cbk * 512:(cbk + 1) * 512], in_=psF[t][:]),
                             reads=[B_psF[t]], writes=[B_fbuf[t]])
                S.op("dve", lambda e: e.reduce_sum(out=rsf[:], in_=ssf[:], axis=AX.X), reads=[B_ssf], writes=[B_rsf])
                S.op("act", lambda e: e.activation(out=rsf[:], in_=rsf[:], func=AF.Sqrt, bias=EPS, scale=1.0 / D),
                     reads=[B_rsf], writes=[B_rsf])
                S.op("dve", lambda e: e.reciprocal(out=rsf[:], in_=rsf[:]), reads=[B_rsf], writes=[B_rsf])
                for t in range(4):
                    r0 = g * 512 + t * 128
                    x1, Bx1, cx1 = x1r.next()
                    S.op("sp", lambda e, x1=x1, r0=r0: e.dma_start(out=x1[:], in_=x1s[r0:r0 + 128, :]), reads=[B_scr["x1s"]],
                         writes=[Bx1], dma_ctr=cx1)
                    S.op("dve", lambda e, t=t: e.scalar_tensor_tensor(out=fbuf[:, t, :], in0=fbuf[:, t, :], scalar=rsf[:, t:t + 1],
                                                                      in1=GGf[:], op0=ALU.mult, op1=ALU.mult),
                         reads=[B_fbuf[t], B_rsf, B_const], writes=[B_fbuf[t]])
                    S.op("pool", lambda e, t=t, x1=x1: e.tensor_tensor(out=fbuf[:, t, :], in0=fbuf[:, t, :], in1=x1[:], op=ALU.add),
                         reads=[B_fbuf[t], Bx1], writes=[B_fbuf[t]])
                    S.op("sp", lambda e, t=t, r0=r0: e.dma_start(out=y[r0:r0 + 128, :], in_=fbuf[:, t, :]), reads=[B_fbuf[t]],
                         writes=[B_scr["y"]], dma_ctr=c_y)
            S.op("sp", lambda e: None, reads=[B_scr["y"]], noinst=True)
            S.flush()
    except StopBuild:
        pass
    S = build.S
    sched_finish(S)
    build.n_instr = S.n_instr
    build.nsem = S.nvsem
    return nc


def _consts():
    bf = ml_dtypes.bfloat16
    ident = np.eye(128, dtype=np.float32)
    pA = np.zeros((128, 128), np.float32)
    for i in range(128):
        pA[(i + 64) % 128, i] = 1.0
    pB = np.zeros((128, 128), np.float32)
    for i in range(128):
        blk, w = divmod(i, 64)
        pB[blk * 64 + (w + 32) % 64, i] = 1.0
    k = np.arange(128)[:, None]
    q = np.arange(512)[None, :]
    cm = [(q >= (m * 128 + k)).astype(np.float32) for m in range(4)]
    q1 = np.arange(128)[None, :]
    band = [(k >= q1).astype(np.float32), (k <= q1).astype(np.float32)]
    cb16 = np.concatenate([ident, pA, pB] + cm + band, axis=1).astype(bf)
    rc = np.zeros((128, 4), np.float32)
    invA = (10000.0 ** (-(np.arange(64, dtype=np.float32)) / np.float32(64))).astype(np.float32)
    invB = (10000.0 ** (-(np.arange(32, dtype=np.float32)) / np.float32(32))).astype(np.float32)
    p = np.arange(128)
    rc[:, 0] = invA[p % 64]
    rc[:, 1] = invB[p % 32]
    rc[:, 2] = np.where(p < 64, -1.0, 1.0)
    rc[:, 3] = np.where((p % 64) < 32, -1.0, 1.0)
    return cb16, rc


def make_in_maps(x, c, positions, w_ada, b_ada, g_pre_attn, w_in, g_out_a, lambda_q1, lambda_k1, lambda_q2, lambda_k2,
                 g_subln_b, w_out, g_post_attn, g_pre_ffn, w_gate, w_up, w_down, g_post_ffn):
    f32 = np.float32
    x = np.asarray(x, f32)
    c = np.asarray(c, f32)
    positions = np.asarray(positions, np.int32)
    cb16, rc = _consts()
    w_in0 = np.asarray(w_in, f32)[0]
    perm = np.arange(1024).reshape(2, 8, 64).transpose(1, 0, 2).reshape(-1)
    w_in_p = np.concatenate([w_in0[:, 0:3072], w_in0[:, 3072:4096][:, perm], w_in0[:, 4096:5120][:, perm], w_in0[:, 5120:6144]],
                            axis=1)
    w_in_p = np.ascontiguousarray(w_in_p)
    shared = {
        "w_ada": np.ascontiguousarray(np.asarray(w_ada, f32)[0]),
        "b_ada": np.ascontiguousarray(np.asarray(b_ada, f32)[0][None, :]),
        "gpa": np.ascontiguousarray(np.asarray(g_pre_attn, f32)[0].reshape(KC, 128).T),
        "gpf": np.ascontiguousarray(np.asarray(g_pre_ffn, f32)[0].reshape(KC, 128).T),
        "gposta": np.ascontiguousarray(np.asarray(g_post_attn, f32)[0][None, :]),
        "gpostf": np.ascontiguousarray(np.asarray(g_post_ffn, f32)[0][None, :]),
        "w_in": w_in_p,
        "gcol": np.ascontiguousarray(np.stack([np.asarray(g_out_a, f32)[0], np.asarray(g_subln_b, f32)[0]], axis=1)),
        "lamv": np.ascontiguousarray(np.concatenate([np.asarray(a, f32)[0] for a in (lambda_q1, lambda_k1, lambda_q2, lambda_k2)])[None, :]),
        "w_out": np.ascontiguousarray(np.asarray(w_out, f32)[0]),
        "w_gate": np.ascontiguousarray(np.asarray(w_gate, f32)[0]),
        "w_up": np.ascontiguousarray(np.asarray(w_up, f32)[0]),
        "w_down": np.ascontiguousarray(np.asarray(w_down, f32)[0]),
        "rc": rc,
        "cb16": cb16,
    }
    in_maps = []
    for core in range(8):
        b, j = divmod(core, 4)
        chunks = [(j - s) % 4 for s in range(4)]
        xsl = np.stack([x[b, ch * NT:(ch + 1) * NT] for ch in chunks], axis=0)
        pos = np.stack([positions[b, ch * NT:(ch + 1) * NT] for ch in chunks], axis=0)[:, None, :]
        eb = np.zeros((128, 4), f32)
        for s in range(4):
            if s > j:
                eb[:, s] = NEG
        m = dict(shared)
        m["xs"] = np.ascontiguousarray(xsl)
        m["posi"] = np.ascontiguousarray(pos.astype(np.int32))
        m["ebias"] = eb
        m["cT"] = np.ascontiguousarray(c[b].reshape(KC, 128).T)
        in_maps.append(m)
    return in_maps


_NC_CACHE = {}


def kernel(**inputs):
    in_maps = make_in_maps(**inputs)
    if "nc" not in _NC_CACHE:
        _NC_CACHE["nc"] = build(debug=False)
    nc = _NC_CACHE["nc"]
    res = run_bass_kernel_spmd(nc, in_maps, core_ids=list(range(8)))
    out = np.empty((2, 4 * NT, D), np.float32)
    for core in range(8):
        b, j = divmod(core, 4)
        out[b, j * NT:(j + 1) * NT] = np.asarray(res.results[core]["y"], np.float32)
    return out
```
